# Optimizing a Trainium2 kernel written in Bass

```python
import math
import jax, jax.numpy as jnp
from jax import lax
import numpy as np

D_MODEL = 1024
BATCH = 16
SEQ = 4096
DEPTH = 1
DEC_BATCH = 4
DEC_SEQ = 4096
PAST_LEN = 128

MEM_LEN = 256
MLA_HEADS = 8
Q_LORA = 384
KV_LORA = 256
QK_NOPE = 64
QK_ROPE = 32
V_HEAD = 64
ROPE_THETA = 10000.0
Q_BLOCK = 128
RWKV_HEADS = 8
RWKV_HEAD = 64
RWKV_DIM = RWKV_HEADS * RWKV_HEAD
DECAY_LORA = 64
AAA_LORA = 64
GATE_LORA = 128
X_HEADS = 4
X_HEAD = 128
D_FF = 2816
N_BRANCH = 2
LN_EPS = 1e-5
RMS_EPS = 1e-6
GN_EPS = 64e-5
ALPHA = (2 * DEPTH) ** 0.25
BETA = (8 * DEPTH) ** -0.25

SEG_Q = Q_LORA
SEG_KV = KV_LORA + QK_ROPE
SEG_RWKV = 3 * RWKV_DIM + 2 * DECAY_LORA + AAA_LORA + GATE_LORA
SEG_GATE = N_BRANCH * D_MODEL
D_IN = SEG_Q + SEG_KV + SEG_RWKV + SEG_GATE
OFF_KV = SEG_Q
OFF_RWKV = OFF_KV + SEG_KV
OFF_GATE = OFF_RWKV + SEG_RWKV
R_WD = 3 * RWKV_DIM
R_AD = R_WD + 2 * DECAY_LORA
R_GD = R_AD + AAA_LORA

kernel_name = 'hybrid_mla_rwkv7_macaron_deepnorm_encoder'


def _layernorm(x, g, b):
    xf = x.astype(jnp.float32)
    mu = jnp.mean(xf, -1, keepdims=True)
    var = jnp.mean(jnp.square(xf - mu), -1, keepdims=True)
    return ((xf - mu) * lax.rsqrt(var + LN_EPS) * g + b).astype(x.dtype)


def _rmsnorm(x, g):
    xf = x.astype(jnp.float32)
    return (xf * lax.rsqrt(jnp.mean(jnp.square(xf), -1, keepdims=True) + RMS_EPS) * g).astype(x.dtype)


def _swiglu(x, w_gu, w_down):
    gate, up = jnp.split(x @ w_gu, 2, axis=-1)
    return (jax.nn.silu(gate) * up) @ w_down


def _rope_tables(seq):
    inv = 1.0 / (ROPE_THETA ** (jnp.arange(0, QK_ROPE, 2, dtype=jnp.float32) / QK_ROPE))
    ang = jnp.arange(seq, dtype=jnp.float32)[:, None] * inv[None, :]
    return jnp.cos(ang), jnp.sin(ang)


def _rope(x, cos, sin):
    x1, x2 = jnp.split(x.astype(jnp.float32), 2, axis=-1)
    return jnp.concatenate([x1 * cos - x2 * sin, x1 * sin + x2 * cos], -1).astype(x.dtype)


def _mla(h_q, h_kv, q_norm_g, w_uq, kv_norm_g, w_ukv):
    b, s, _ = h_q.shape
    cos, sin = _rope_tables(s)
    q = (_rmsnorm(h_q, q_norm_g) @ w_uq).reshape(b, s, MLA_HEADS, QK_NOPE + QK_ROPE)
    q_nope = q[..., :QK_NOPE]
    q_rope = _rope(q[..., QK_NOPE:], cos[:, None, :], sin[:, None, :])
    c_kv = _rmsnorm(h_kv[..., :KV_LORA], kv_norm_g)
    k_rope = _rope(h_kv[..., KV_LORA:], cos, sin)
    kv = (c_kv @ w_ukv).reshape(b, s, MLA_HEADS, QK_NOPE + V_HEAD)
    k_nope, v = kv[..., :QK_NOPE], kv[..., QK_NOPE:]
    scale = (QK_NOPE + QK_ROPE) ** -0.5
    nblk = s // Q_BLOCK
    qn_b = q_nope.reshape(b, nblk, Q_BLOCK, MLA_HEADS, QK_NOPE).transpose(1, 0, 2, 3, 4)
    qr_b = q_rope.reshape(b, nblk, Q_BLOCK, MLA_HEADS, QK_ROPE).transpose(1, 0, 2, 3, 4)

    def block(args):
        qn, qr = args
        sc = (jnp.einsum('bqhd,bkhd->bhqk', qn, k_nope, preferred_element_type=jnp.float32)
              + jnp.einsum('bqhd,bkd->bhqk', qr, k_rope, preferred_element_type=jnp.float32))
        pr = jax.nn.softmax(sc * scale, axis=-1)
        return jnp.einsum('bhqk,bkhd->bqhd', pr.astype(v.dtype), v)

    o = lax.map(block, (qn_b, qr_b))
    return o.transpose(1, 0, 2, 3, 4).reshape(b, s, MLA_HEADS * V_HEAD)


def _centred_shift(h, mu_prev, mu_next):
    prev = jnp.pad(h[:, :-1], ((0, 0), (1, 0), (0, 0)))
    nxt = jnp.pad(h[:, 1:], ((0, 0), (0, 1), (0, 0)))
    return h + mu_prev * (prev - h) + mu_next * (nxt - h)


def _rwkv_scan(r, w, k, v, a, bb, reverse):
    _, bsz, nh, n = r.shape

    def step(S, inp):
        rt, wt, kt, vt, at, bt = inp
        Sa = jnp.einsum('bhij,bhj->bhi', S, at)
        S = S * wt[:, :, None, :] + Sa[..., None] * bt[:, :, None, :] + vt[..., None] * kt[:, :, None, :]
        return S, jnp.einsum('bhij,bhj->bhi', S, rt)

    S0 = jnp.zeros((bsz, nh, n, n), jnp.float32)
    _, y = lax.scan(step, S0, (r, w, k, v, a, bb), reverse=reverse)
    return y


def _rwkv7(h, mu_prev, mu_next, w0, w_up, a0, a_up, g_up, k_k, k_a, r_k, lnx_g, lnx_b):
    b, s, _ = h.shape
    f32 = jnp.float32
    h = _centred_shift(h, mu_prev, mu_next).astype(f32)
    r = h[..., :RWKV_DIM]
    k = h[..., RWKV_DIM:2 * RWKV_DIM]
    v = h[..., 2 * RWKV_DIM:3 * RWKV_DIM]
    wd = h[..., R_WD:R_AD].reshape(b, s, 2, DECAY_LORA)
    ad = h[..., R_AD:R_GD]
    gd = h[..., R_GD:SEG_RWKV]
    w_log = -jax.nn.softplus(-(w0 + jnp.einsum('bsdl,dlc->bsdc', jnp.tanh(wd), w_up))) - 0.5
    decay = jnp.exp(-jnp.exp(w_log.astype(f32))).reshape(b, s, 2, RWKV_HEADS, RWKV_HEAD)
    a = jax.nn.sigmoid(a0 + ad @ a_up)
    g = jax.nn.sigmoid(gd) @ g_up
    kk = (k * k_k).reshape(b, s, RWKV_HEADS, RWKV_HEAD)
    kk = kk / jnp.maximum(jnp.sqrt(jnp.sum(kk * kk, -1, keepdims=True)), 1e-12)
    k = k * (1.0 + (a - 1.0) * k_a)
    hs = lambda t: t.reshape(b, s, RWKV_HEADS, RWKV_HEAD)
    r, k, v, a = hs(r), hs(k), hs(v), hs(a)
    tm = lambda t: t.transpose(1, 0, 2, 3)
    rt, kt, vt = tm(r), tm(k), tm(v)
    a_vec, b_vec = tm(-kk), tm(kk * a)
    y = (_rwkv_scan(rt, tm(decay[:, :, 0]), kt, vt, a_vec, b_vec, False)
         + _rwkv_scan(rt, tm(decay[:, :, 1]), kt, vt, a_vec, b_vec, True))
    y = tm(y)
    mu = jnp.mean(y, -1, keepdims=True)
    var = jnp.mean(jnp.square(y - mu), -1, keepdims=True)
    yn = ((y - mu) * lax.rsqrt(var + GN_EPS)).reshape(b, s, RWKV_DIM) * lnx_g + lnx_b
    bonus = (jnp.sum(r * k * r_k, -1, keepdims=True) * v).reshape(b, s, RWKV_DIM)
    return (yn + bonus) * g


def _cross(x, mem, mem_g, mem_b, w_cq, w_ckv, w_co):
    b, s, _ = x.shape
    m = _layernorm(mem, mem_g, mem_b)
    q = (x @ w_cq).reshape(b, s, X_HEADS, X_HEAD)
    kv = (m @ w_ckv).reshape(b, mem.shape[1], 2, X_HEADS, X_HEAD)
    sc = jnp.einsum('bqhd,bkhd->bhqk', q, kv[:, :, 0], preferred_element_type=jnp.float32) * (X_HEAD ** -0.5)
    pr = jax.nn.softmax(sc, axis=-1)
    o = jnp.einsum('bhqk,bkhd->bqhd', pr.astype(x.dtype), kv[:, :, 1]).reshape(b, s, X_HEADS * X_HEAD)
    return o @ w_co


def _layer(x, mem, p):
    b, s, _ = x.shape
    x = _layernorm(ALPHA * x + 0.5 * _swiglu(x, p['ffn1_wgu'], p['ffn1_wd']), p['ln1_g'], p['ln1_b'])
    h = x @ p['w_in']
    br_a = _mla(h[..., :OFF_KV], h[..., OFF_KV:OFF_RWKV], p['q_norm_g'], p['w_uq'],
                p['kv_norm_g'], p['w_ukv']) @ p['p_mla']
    br_b = _rwkv7(h[..., OFF_RWKV:OFF_GATE], p['mu_prev'], p['mu_next'], p['w0'], p['w_up'],
                  p['a0'], p['a_up'], p['g_up'], p['k_k'], p['k_a'], p['r_k'],
                  p['lnx_g'], p['lnx_b']).astype(x.dtype) @ p['p_rwkv']
    gates = jax.nn.sigmoid(h[..., OFF_GATE:] + p['b_gate']).reshape(b, s, N_BRANCH, D_MODEL)
    mix = (gates[:, :, 0] * br_a + gates[:, :, 1] * br_b) @ p['w_o']
    x = _layernorm(ALPHA * x + mix, p['ln2_g'], p['ln2_b'])
    x = _layernorm(ALPHA * x + _cross(x, mem, p['mem_g'], p['mem_b'], p['w_cq'], p['w_ckv'], p['w_co']),
                   p['ln3_g'], p['ln3_b'])
    x = _layernorm(ALPHA * x + 0.5 * _swiglu(x, p['ffn2_wgu'], p['ffn2_wd']), p['ln4_g'], p['ln4_b'])
    return x


def setup_inputs(seed: int = 0) -> dict:
    key = jax.random.key(seed)
    ks = iter(jax.random.split(key, 64))

    def nrm(shape, scale):
        return scale * jax.random.normal(next(ks), shape, jnp.float32)

    L, D = DEPTH, D_MODEL
    mla_out = MLA_HEADS * V_HEAD
    x_out = X_HEADS * X_HEAD
    w0_base = jnp.linspace(-6.0, -1.0, RWKV_DIM, dtype=jnp.float32) + 0.5
    return {
        'x_prompt': nrm((BATCH, SEQ, D), 1.0),
        'x_sample': nrm((DEC_BATCH, DEC_SEQ, D), 1.0),
        'mem_prompt': nrm((BATCH, MEM_LEN, D), 1.0),
        'mem_sample': nrm((DEC_BATCH, MEM_LEN, D), 1.0),
        'ln1_g': 1.0 + nrm((L, D), 0.02), 'ln1_b': nrm((L, D), 0.02),
        'ffn1_wgu': nrm((L, D, 2 * D_FF), D ** -0.5),
        'ffn1_wd': nrm((L, D_FF, D), BETA * D_FF ** -0.5),
        'w_in': nrm((L, D, D_IN), D ** -0.5),
        'b_gate': nrm((L, SEG_GATE), 0.02),
        'q_norm_g': 1.0 + nrm((L, Q_LORA), 0.02),
        'w_uq': nrm((L, Q_LORA, MLA_HEADS * (QK_NOPE + QK_ROPE)), Q_LORA ** -0.5),
        'kv_norm_g': 1.0 + nrm((L, KV_LORA), 0.02),
        'w_ukv': nrm((L, KV_LORA, MLA_HEADS * (QK_NOPE + V_HEAD)), KV_LORA ** -0.5),
        'p_mla': nrm((L, mla_out, D), mla_out ** -0.5),
        'mu_prev': jax.random.uniform(next(ks), (L, SEG_RWKV), jnp.float32, 0.1, 0.6),
        'mu_next': jax.random.uniform(next(ks), (L, SEG_RWKV), jnp.float32, 0.1, 0.6),
        'w0': w0_base + nrm((L, 2, RWKV_DIM), 0.3),
        'w_up': nrm((L, 2, DECAY_LORA, RWKV_DIM), 0.1 * DECAY_LORA ** -0.5),
        'a0': nrm((L, RWKV_DIM), 0.1),
        'a_up': nrm((L, AAA_LORA, RWKV_DIM), 0.5 * AAA_LORA ** -0.5),
        'g_up': nrm((L, GATE_LORA, RWKV_DIM), GATE_LORA ** -0.5),
        'k_k': 0.85 + nrm((L, RWKV_DIM), 0.05),
        'k_a': 1.0 + nrm((L, RWKV_DIM), 0.05),
        'r_k': nrm((L, RWKV_HEADS, RWKV_HEAD), 0.1),
        'lnx_g': 1.0 + nrm((L, RWKV_DIM), 0.02), 'lnx_b': nrm((L, RWKV_DIM), 0.02),
        'p_rwkv': nrm((L, RWKV_DIM, D), RWKV_DIM ** -0.5),
        'w_o': nrm((L, D, D), BETA * D ** -0.5),
        'ln2_g': 1.0 + nrm((L, D), 0.02), 'ln2_b': nrm((L, D), 0.02),
        'mem_g': 1.0 + nrm((L, D), 0.02), 'mem_b': nrm((L, D), 0.02),
        'w_cq': nrm((L, D, x_out), D ** -0.5),
        'w_ckv': nrm((L, D, 2 * x_out), D ** -0.5),
        'w_co': nrm((L, x_out, D), BETA * x_out ** -0.5),
        'ln3_g': 1.0 + nrm((L, D), 0.02), 'ln3_b': nrm((L, D), 0.02),
        'ffn2_wgu': nrm((L, D, 2 * D_FF), D ** -0.5),
        'ffn2_wd': nrm((L, D_FF, D), BETA * D_FF ** -0.5),
        'ln4_g': 1.0 + nrm((L, D), 0.02), 'ln4_b': nrm((L, D), 0.02),
    }


def reference(x_prompt, x_sample, mem_prompt, mem_sample, ln1_g, ln1_b, ffn1_wgu, ffn1_wd, w_in, b_gate,
              q_norm_g, w_uq, kv_norm_g, w_ukv, p_mla, mu_prev, mu_next, w0, w_up, a0, a_up, g_up,
              k_k, k_a, r_k, lnx_g, lnx_b, p_rwkv, w_o, ln2_g, ln2_b, mem_g, mem_b, w_cq, w_ckv, w_co,
              ln3_g, ln3_b, ffn2_wgu, ffn2_wd, ln4_g, ln4_b):
    y_prompt, y_sample = x_prompt, x_sample
    for l in range(DEPTH):
        p = dict(ln1_g=ln1_g[l], ln1_b=ln1_b[l], ffn1_wgu=ffn1_wgu[l], ffn1_wd=ffn1_wd[l],
                 w_in=w_in[l], b_gate=b_gate[l], q_norm_g=q_norm_g[l], w_uq=w_uq[l],
                 kv_norm_g=kv_norm_g[l], w_ukv=w_ukv[l], p_mla=p_mla[l], mu_prev=mu_prev[l],
                 mu_next=mu_next[l], w0=w0[l], w_up=w_up[l], a0=a0[l], a_up=a_up[l], g_up=g_up[l],
                 k_k=k_k[l], k_a=k_a[l], r_k=r_k[l], lnx_g=lnx_g[l], lnx_b=lnx_b[l], p_rwkv=p_rwkv[l],
                 w_o=w_o[l], ln2_g=ln2_g[l], ln2_b=ln2_b[l], mem_g=mem_g[l], mem_b=mem_b[l],
                 w_cq=w_cq[l], w_ckv=w_ckv[l], w_co=w_co[l], ln3_g=ln3_g[l], ln3_b=ln3_b[l],
                 ffn2_wgu=ffn2_wgu[l], ffn2_wd=ffn2_wd[l], ln4_g=ln4_g[l], ln4_b=ln4_b[l])
        y_prompt = _layer(y_prompt, mem_prompt, p)
        y_sample = _layer(y_sample, mem_sample, p)
    return (y_prompt, y_sample)
```

```python
import os
import numpy as np
import concourse.bass as bass
import concourse.mybir as mybir
from concourse.bass_utils import run_bass_kernel_spmd
from contextlib import ExitStack

F32 = mybir.dt.float32
BF16 = mybir.dt.bfloat16
AF = mybir.ActivationFunctionType
ALU = mybir.AluOpType
AX = mybir.AxisListType

D = 1024
DFF = 2816
NFC = 22
TT = 512
H = 8
ALPHA = float(2 ** 0.25)
LN_EPS = 1e-5
RMS_EPS = 1e-6
GN_EPS = 64e-5
WCONST = float(np.exp(-0.5))
OFF_KV = 384
OFF_RWKV = 672
OFF_GATE = 2528
NCORE = 8
NSLOT = 3


class Ev:
    __slots__ = ("sem", "val", "key")

    def __init__(self, sem, key, val=None):
        self.sem = sem
        self.key = key
        self.val = val


class Buf:
    __slots__ = ("name", "w", "r", "dsem", "dkey", "dcount", "psum")

    def __init__(self, name):
        self.name = name
        self.psum = name.startswith("ps")
        self.w = {}
        self.r = {}
        self.dsem = None
        self.dkey = None
        self.dcount = 0


class Iss:
    def __init__(self, K, name, eng):
        self.name = name
        self.eng = eng
        self.sem = K.new_sem("e_" + name)
        self.key = "e_" + name
        self.count = 0
        self.seen = {}
        self.cur = Ev(self.sem, self.key)
        self.ninstr = 0


class Kern:
    def __init__(self, nc, es):
        self.nc = nc
        self.es = es
        self.nsem = 0
        self.pe = Iss(self, "pe", nc.tensor)
        self.act = Iss(self, "act", nc.scalar)
        self.dve = Iss(self, "dve", nc.vector)
        self.pool = Iss(self, "pool", nc.gpsimd)
        self.sp = Iss(self, "sp", nc.sync)
        self.all = [self.pe, self.act, self.dve, self.pool, self.sp]
        self.dbufs = []
        self.nbuf = 0

    def new_sem(self, name):
        self.nsem += 1
        return self.es.enter_context(self.nc.semaphore(name))

    def begin_scope(self, scope):
        self.scope = scope
        self.scnt = {}

    def buf(self, name=None):
        name = name or "b"
        scope = getattr(self, "scope", None)
        if scope is None:
            self.nbuf += 1
            return Buf("%s_%d" % (name, self.nbuf))
        k = self.scnt.get(name, 0)
        self.scnt[name] = k + 1
        key = (scope, name, k)
        cache = self.__dict__.setdefault("bcache", {})
        if key not in cache:
            self.nbuf += 1
            cache[key] = Buf("%s_%d" % (name, self.nbuf))
        return cache[key]

    def bufs(self, n, name="b"):
        return [self.buf("%s%d" % (name, i)) for i in range(n)]

    def _wait(self, iss, ev):
        assert ev.val is not None, "waiting on unresolved event (%s)" % ev.key
        if iss.seen.get(ev.key, 0) >= ev.val:
            return
        iss.eng.wait_ge(ev.sem, ev.val)
        iss.seen[ev.key] = ev.val
        iss.ninstr += 1

    def _need(self, iss, ev, out):
        assert ev.val is not None, "waiting on unresolved event (%s)" % ev.key
        if iss.seen.get(ev.key, 0) >= ev.val:
            return
        iss.seen[ev.key] = ev.val
        for i, o in enumerate(out):
            if o.key == ev.key:
                if o.val < ev.val:
                    out[i] = ev
                return
        out.append(ev)

    def _deps(self, iss, reads, writes, acc, inline=False):
        out = []
        for b in reads:
            for ev in b.w.values():
                self._need(iss, ev, out)
            if b.psum:
                for ev in b.r.values():
                    if ev.key != iss.key:
                        self._need(iss, ev, out)
        for b in writes:
            for ev in b.r.values():
                self._need(iss, ev, out)
            if not acc:
                for ev in b.w.values():
                    self._need(iss, ev, out)
        last = out.pop() if (inline and out) else None
        for ev in out:
            iss.eng.wait_ge(ev.sem, ev.val)
            iss.ninstr += 1
        return last

    def op(self, iss, fn, reads=(), writes=(), inc=True, acc=False, lhs=None):
        last = self._deps(iss, reads, writes, acc, inline=True)
        ins = fn()
        if last is not None:
            ins.wait_op(last.sem, last.val, "sem-ge")
        iss.ninstr += 1
        ev = iss.cur
        if inc:
            iss.count += 1
            ins.then_inc(iss.sem, 1)
            ev.val = iss.count
            iss.cur = Ev(iss.sem, iss.key)
        for b in reads:
            b.r[ev.key] = ev
        for b in writes:
            if not acc:
                b.w = {}
                b.r = {}
            b.w[ev.key] = ev
        return ins

    def dma(self, iss, out, in_, reads=(), writes=(), acc=False, owner=None, **kw):
        self._deps(iss, reads, writes, acc)
        b0 = owner if owner is not None else writes[0]
        if b0.dsem is None:
            b0.dkey = "d_%s" % b0.name
            b0.dsem = self.new_sem(b0.dkey)
            self.dbufs.append(b0)
        ins = iss.eng.dma_start(out=out, in_=in_, **kw)
        iss.ninstr += 1
        b0.dcount += 16
        ins.then_inc(b0.dsem, 16)
        ev = Ev(b0.dsem, b0.dkey, b0.dcount)
        for b in reads:
            b.r[ev.key] = ev
        for b in writes:
            if not acc:
                b.w = {}
                b.r = {}
            b.w[ev.key] = ev
        return ins

    def barrier(self):
        for iss in self.all:
            for o in self.all:
                if o is not iss and o.count > 0:
                    self._wait(iss, Ev(o.sem, o.key, o.count))
            for b in self.dbufs:
                if b.dcount > 0:
                    self._wait(iss, Ev(b.dsem, b.dkey, b.dcount))


def _col_layout():
    off = {}
    n = 0
    for name, c in [("ln1_g", 8), ("ln1_b", 8), ("ln2_g", 8), ("ln2_b", 8), ("ln3_g", 8), ("ln3_b", 8),
                    ("ln4_g", 8), ("ln4_b", 8), ("b_gate", 16), ("q_norm_g", 3), ("kv_norm_g", 2),
                    ("mu_prev", 15), ("mu_next", 15), ("w0", 8), ("a0", 4), ("k_k", 4), ("k_a", 4), ("r_k", 4),
                    ("lnx_g", 4), ("lnx_b", 4)]:
        off[name] = (n, c)
        n += c
    return off, n


COLS, NCOL = _col_layout()
RW_CH = [(i * 128, 128) for i in range(12)] + [(1536, 128), (1664, 64), (1728, 128)]


def _pack_cols(p):
    a = np.zeros((128, NCOL), np.float32)

    def put(name, vec):
        o, c = COLS[name]
        v = np.asarray(vec, np.float32).reshape(-1)
        a[:, o:o + c] = v.reshape(c, 128).T

    for nm in ["ln1_g", "ln1_b", "ln2_g", "ln2_b", "ln3_g", "ln3_b", "ln4_g", "ln4_b", "b_gate", "q_norm_g",
               "kv_norm_g", "a0", "k_k", "k_a", "r_k", "lnx_g", "lnx_b", "w0"]:
        put(nm, p[nm][0])
    for nm in ["mu_prev", "mu_next"]:
        o, c = COLS[nm]
        v = np.asarray(p[nm][0], np.float32)
        for j, (ro, wd) in enumerate(RW_CH):
            a[:wd, o + j] = v[ro:ro + wd]
    return a


WL = {"wgu0": [11, 128, 8, 512], "wgu1": [11, 128, 8, 512], "wd0": [8, 128, NFC, 128], "wd1": [8, 128, NFC, 128],
      "win": [11, 128, 8, 512], "wuq": [128, 3, 1536], "wukv": [128, 2, 1024], "pmla": [128, 4, 1024],
      "prwkv": [128, 4, 1024], "wo": [128, 8, 1024], "wcq": [128, 8, 512], "wckv": [128, 8, 1024],
      "wco": [128, 4, 1024], "wup": [128, 1, 512], "aup": [64, 1, 512], "gup": [128, 1, 512]}


def _layout_weights(p):
    g = lambda n: np.asarray(p[n][0], np.float32)
    kp = lambda a: a.reshape(a.shape[0] // 128, 128, a.shape[1]).transpose(1, 0, 2)
    o = {}
    for i, pre in enumerate(["ffn1", "ffn2"]):
        gu = kp(g(pre + "_wgu"))
        blk = np.zeros((11, 128, 8, 512), np.float32)
        for j in range(11):
            blk[j, :, :, 0:256] = gu[:, :, j * 256:(j + 1) * 256]
            blk[j, :, :, 256:512] = gu[:, :, DFF + j * 256:DFF + (j + 1) * 256]
        o["wgu%d" % i] = blk
        wd = kp(g(pre + "_wd"))
        o["wd%d" % i] = np.stack([wd[:, :, c * 128:(c + 1) * 128] for c in range(8)])
    win = kp(g("w_in"))
    blk = np.zeros((11, 128, 8, 512), np.float32)
    blk[0, :, :, 0:384] = win[:, :, 0:384]
    blk[1, :, :, 0:256] = win[:, :, 384:640]
    blk[2, :, :, 64:96] = win[:, :, 640:672]
    blk[2, :, :, 160:176] = win[:, :, 656:672]
    blk[2, :, :, 176:192] = win[:, :, 640:656]
    for j in range(3):
        blk[3 + j] = win[:, :, OFF_RWKV + j * 512:OFF_RWKV + (j + 1) * 512]
    blk[6, :, :, 0:320] = win[:, :, OFF_RWKV + 1536:OFF_RWKV + 1856]
    for j in range(4):
        blk[7 + j] = win[:, :, OFF_GATE + j * 512:OFF_GATE + (j + 1) * 512]
    o["win"] = blk
    uq = kp(g("w_uq")).reshape(128, 3, 8, 96)
    uq2 = np.zeros((128, 3, 2, 8, 96), np.float32)
    uq2[:, :, 0] = uq
    uq2[:, :, 1, :, 64:80] = uq[:, :, :, 80:96]
    uq2[:, :, 1, :, 80:96] = uq[:, :, :, 64:80]
    o["wuq"] = uq2.reshape(128, 3, 1536)
    ukv = kp(g("w_ukv")).reshape(128, 2, 8, 128)
    o["wukv"] = np.concatenate([ukv[..., 0:64].reshape(128, 2, 512), ukv[..., 64:128].reshape(128, 2, 512)], -1)
    o["pmla"] = kp(g("p_mla"))
    o["prwkv"] = kp(g("p_rwkv"))
    o["wo"] = kp(g("w_o"))
    o["wcq"] = kp(g("w_cq"))
    o["wckv"] = kp(g("w_ckv"))
    o["wco"] = kp(g("w_co"))
    o["wup"] = g("w_up").reshape(128, 1, 512)
    o["aup"] = g("a_up").reshape(64, 1, 512)
    o["gup"] = g("g_up").reshape(128, 1, 512)
    return {k + "_f": np.ascontiguousarray(v) for k, v in o.items()}


class Builder:
    def __init__(self, S, nslot, debug=False):
        self.S = S
        self.nslot = nslot
        self.NT = S // TT
        self.debug = debug

    def declare(self, nc):
        S, ns = self.S, self.nslot
        I = lambda n, sh, dt=F32: nc.dram_tensor(n, sh, dt, kind="ExternalInput").ap()
        self.x = I("x", [ns, S, D])
        self.mem = I("mem", [ns, 256, D])
        self.wl = {n: I(n + "_f", sh) for n, sh in WL.items()}
        self.pcols = I("pcols", [128, NCOL])
        self.ident = I("ident", [128, 128])
        self.ropec = I("ropec", [32, S])
        self.ropes = I("ropes", [32, S])
        self.memgb = I("memgb", [2, D])
        self.cst2 = I("cst2", [128, 128 + 512 + 384 + 64])
        self.y = nc.dram_tensor("y", [ns, S, D], F32, kind="ExternalOutput").ap()
        kind = "ExternalOutput" if self.debug else "Internal"
        Sc = lambda n, sh, dt=F32: nc.dram_tensor(n, sh, dt, kind=kind).ap()
        self.Sc = Sc
        self.ws = {n: Sc(n + "_s", sh, BF16) for n, sh in WL.items()}
        self.wgu_s = [self.ws["wgu0"], self.ws["wgu1"]]
        self.wd_s = [self.ws["wd0"], self.ws["wd1"]]
        self.win_s = self.ws["win"]
        self.x1T = Sc("x1T", [D, S])
        self.qT = Sc("qT", [8, 96, S], BF16)
        self.kT = Sc("kT", [8, 96, S], BF16)
        self.vtok = Sc("vtok", [S, 8, 65], BF16)
        self.gates = Sc("gatesT", [2048, S])
        self.hrT = Sc("hrT", [1856, S])
        self.OT = Sc("OT", [512, S], BF16)
        self.rw4 = Sc("rw4", [4, 512, S])
        self.rwlw = Sc("rwlw", [2, 512, S])
        self.rwg = Sc("rwg", [512, S])
        self.rwbon = Sc("rwbon", [512, S])
        self.rwv = Sc("rwv", [S, 512], BF16)
        self.rwy = Sc("rwy", [2, S, 512])
        self.kmT = Sc("kmT_s", [128, 4, 256], BF16)
        self.vms = Sc("vm_s", [128, 2, 516], BF16)
        self.rwo = Sc("rwoT", [512, S], BF16)

    def build(self):
        nc = bass.Bass("TRN2", target_bir_lowering=False)
        self.nc = nc
        self.declare(nc)
        with ExitStack() as es:
            K = Kern(nc, es)
            self.K = K
            self.es = es
            self.db = {n: K.buf("dram_" + n) for n in ["x1T", "qT", "kT", "vtok", "gates", "hrT", "y", "OT", "rwo", "rw4", "rwlw", "rwg", "rwbon", "rwv", "rwy0", "rwy1", "kmT", "vms"]}
            self.consts()
            stage = getattr(self, "stage", "all")
            if stage not in ("c", "p1a_nop0"):
                self.p0_weights()
            for slot in range(self.nslot):
                if stage not in ("p0", "c"):
                    self.p1(slot)
                K.barrier()
                if stage in ("p2", "all"):
                    self.p2(slot)
                    K.barrier()
                if stage in ("p3a", "p3b", "p3c", "all"):
                    self.p3a(slot)
                    K.barrier()
                if stage in ("p3b", "p3c", "all"):
                    self.p3b(slot)
                    K.barrier()
                if stage in ("p3c", "all"):
                    self.p3c(slot)
                    K.barrier()
                if stage in ("all",):
                    self.p4m(slot)
                    K.barrier()
                    self.p4(slot)
                    K.barrier()
            K.barrier()
        return nc

    def sbt(self, es, name, shape, dt):
        return es.enter_context(self.nc.sbuf_tensor(name, shape, dt))

    def pst(self, es, name, shape, dt):
        return es.enter_context(self.nc.psum_tensor(name, shape, dt))

    def consts(self):
        nc, K, es = self.nc, self.K, self.es
        self.pc = self.sbt(es, "pc", [128, NCOL], F32)
        self.b_pc = K.buf("pc")
        K.dma(K.sp, self.pc[:], self.pcols, writes=[self.b_pc])
        self.idf = self.sbt(es, "idf", [128, 128], F32)
        self.idb = self.sbt(es, "idb", [128, 128], BF16)
        self.b_id = K.buf("id")
        K.dma(K.sp, self.idf[:], self.ident, writes=[self.b_id])
        K.op(K.dve, lambda: nc.vector.tensor_copy(out=self.idb[:], in_=self.idf[:]), reads=[self.b_id], writes=[self.b_id], acc=True)
        self.ones = self.sbt(es, "ones", [128, 128], F32)
        self.b_ones = K.buf("ones")
        K.op(K.pool, lambda: nc.gpsimd.memset(self.ones[:], 1.0), writes=[self.b_ones])
        self.c2 = self.sbt(es, "c2", [128, 128 + 512 + 384 + 64], F32)
        self.b_c2 = K.buf("c2")
        K.dma(K.sp, self.c2[:], self.cst2, writes=[self.b_c2])
        self.ones2 = self.c2[:, 0:128]
        self.cmask = self.c2[:, 128:640]
        self.rmask = lambda d: self.c2[:, 640 + d * 192:640 + (d + 1) * 192]
        self.idl = self.c2[:, 1024:1088]
        self.epsc = self.sbt(es, "epsc", [128, 4], F32)
        for i, v in enumerate([LN_EPS, RMS_EPS, GN_EPS, 1e-18]):
            K.op(K.pool, lambda: nc.gpsimd.memset(self.epsc[:, i:i + 1], v), writes=[self.b_ones], acc=True)

    def col(self, name, j=0):
        o, c = COLS[name]
        return self.pc[:, o + j:o + j + 1]

    def p0_weights(self):
        K = self.K
        P = K.pool
        hist = []
        for n, sh in WL.items():
            self.db[n] = K.buf("dram_" + n)
            blocks = [(self.ws[n][j], self.wl[n][j]) for j in range(sh[0])] if len(sh) == 4 else [(self.ws[n], self.wl[n])]
            for o, i in blocks:
                if len(hist) >= 2:
                    b, c = hist[-2]
                    K._wait(P, Ev(b.dsem, b.dkey, c))
                K.dma(P, o, i, writes=[self.db[n]], acc=True)
                hist.append((self.db[n], self.db[n].dcount))

    def ln_fm(self, z, bz, gname, bname, o32, bo32, o16, bo16, ps, bps, sqf, bsq, st, bst):
        nc, K = self.nc, self.K
        for c in range(8):
            K.op(K.pe, lambda: nc.tensor.matmul(ps[:], lhsT=self.ones[:], rhs=z[:, c, :], start=(c == 0), stop=(c == 7)),
                 reads=[bz[c], self.b_ones], writes=[bps], inc=(c == 7), acc=(c > 0))
        K.op(K.act, lambda: nc.scalar.mul(out=st[:], in_=ps[:], mul=1.0 / D), reads=[bps], writes=[bst])
        for c in range(8):
            K.op(K.dve, lambda: nc.vector.tensor_tensor(out=z[:, c, :], in0=z[:, c, :], in1=st[:], op=ALU.subtract),
                 reads=[bz[c], bst], writes=[bz[c]])
            K.op(K.act, lambda: nc.scalar.activation(out=sqf(c), in_=z[:, c, :], func=AF.Square), reads=[bz[c]], writes=[bsq[c]])
        for c in range(8):
            K.op(K.pe, lambda: nc.tensor.matmul(ps[:], lhsT=self.ones[:], rhs=sqf(c), start=(c == 0), stop=(c == 7)),
                 reads=[bsq[c], self.b_ones], writes=[bps], inc=(c == 7), acc=(c > 0))
        K.op(K.act, lambda: nc.scalar.activation(out=st[:], in_=ps[:], func=AF.Ln, scale=1.0 / D, bias=self.epsc[:, 0:1]),
             reads=[bps, self.b_ones], writes=[bst])
        K.op(K.act, lambda: nc.scalar.activation(out=st[:], in_=st[:], func=AF.Exp, scale=-0.5), reads=[bst], writes=[bst])
        for c in range(8):
            K.op(K.dve, lambda: nc.vector.tensor_tensor(out=z[:, c, :], in0=z[:, c, :], in1=st[:], op=ALU.mult),
                 reads=[bz[c], bst], writes=[bz[c]])
            K.op(K.act, lambda: nc.scalar.activation(out=o32[:, c, :], in_=z[:, c, :], func=AF.Identity,
                                                      scale=self.col(gname, c), bias=self.col(bname, c)),
                 reads=[bz[c], self.b_pc], writes=[bo32[c]])
            if o16 is not None:
                K.op(K.pool, lambda: nc.gpsimd.tensor_copy(out=o16[:, c, :], in_=o32[:, c, :]), reads=[bo32[c]], writes=[bo16[c]])

    def rms_fm(self, src, bsrc, nch, nfeat, gname, sqf, bsq, ps, bps, st, bst, outb, boutb):
        nc, K = self.nc, self.K
        for c in range(nch):
            K.op(K.act, lambda: nc.scalar.activation(out=sqf(c), in_=src(c), func=AF.Square), reads=[bsrc[c]], writes=[bsq[c]])
        for c in range(nch):
            K.op(K.pe, lambda: nc.tensor.matmul(ps[:], lhsT=self.ones[:], rhs=sqf(c), start=(c == 0), stop=(c == nch - 1)),
                 reads=[bsq[c], self.b_ones], writes=[bps], inc=(c == nch - 1), acc=(c > 0))
        K.op(K.act, lambda: nc.scalar.activation(out=st[:], in_=ps[:], func=AF.Ln, scale=1.0 / nfeat, bias=self.epsc[:, 1:2]),
             reads=[bps, self.b_ones], writes=[bst])
        K.op(K.act, lambda: nc.scalar.activation(out=st[:], in_=st[:], func=AF.Exp, scale=-0.5), reads=[bst], writes=[bst])
        for c in range(nch):
            K.op(K.dve, lambda: nc.vector.scalar_tensor_tensor(out=outb(c), in0=src(c), scalar=self.col(gname, c), in1=st[:], op0=ALU.mult, op1=ALU.mult),
                 reads=[bsrc[c], bst, self.b_pc], writes=[boutb[c]])

    def ffn(self, idx, xb, bxb, xa, bxa, hT, bhT, wgu, bwgu, wdn, bwdn, psA, bpsA, z, bz, extra={}):
        nc, K = self.nc, self.K
        dbg, dbd = self.db["wgu%d" % idx], self.db["wd%d" % idx]
        K.dma(K.sp, wgu[0][:], self.wgu_s[idx][0], reads=[dbg], writes=[bwgu[0]])
        pi = 0
        for j in range(11):
            if j + 1 < 11:
                K.dma(K.sp, wgu[(j + 1) % 2][:], self.wgu_s[idx][j + 1], reads=[dbg], writes=[bwgu[(j + 1) % 2]])
            else:
                K.dma(K.sp, wdn[0][:], self.wd_s[idx][0], reads=[dbd], writes=[bwdn[0]])
            wt, bwt = wgu[j % 2], bwgu[j % 2]
            for f in range(2):
                fc = j * 2 + f
                pg, bpg = psA[pi % 4], bpsA[pi % 4]
                pu, bpu = psA[(pi + 1) % 4], bpsA[(pi + 1) % 4]
                pi += 2
                for kc in range(8):
                    K.op(K.pe, lambda: nc.tensor.matmul(pg[:], lhsT=wt[:, kc, f * 128:(f + 1) * 128], rhs=xb[:, kc, :], start=(kc == 0), stop=(kc == 7)),
                         reads=[bwt, bxb[kc]], writes=[bpg], inc=(kc == 7), acc=(kc > 0), lhs=[bwt])
                for kc in range(8):
                    K.op(K.pe, lambda: nc.tensor.matmul(pu[:], lhsT=wt[:, kc, 256 + f * 128:256 + (f + 1) * 128], rhs=xb[:, kc, :], start=(kc == 0), stop=(kc == 7)),
                         reads=[bwt, bxb[kc]], writes=[bpu], inc=(kc == 7), acc=(kc > 0), lhs=[bwt])
                K.op(K.act, lambda: nc.scalar.activation(out=hT[:, fc, :], in_=pg[:], func=AF.Silu), reads=[bpg], writes=[bhT[fc]] + extra.get(fc, []))
                K.op(K.dve, lambda: nc.vector.tensor_tensor(out=hT[:, fc, :], in0=hT[:, fc, :], in1=pu[:], op=ALU.mult),
                     reads=[bhT[fc], bpu], writes=[bhT[fc]])
        for c in range(8):
            if c + 1 < 8:
                K.dma(K.sp, wdn[(c + 1) % 2][:], self.wd_s[idx][c + 1], reads=[dbd], writes=[bwdn[(c + 1) % 2]])
            wt, bwt = wdn[c % 2], bwdn[c % 2]
            pz, bpz = psA[c % 4], bpsA[c % 4]
            for kc in range(NFC):
                K.op(K.pe, lambda: nc.tensor.matmul(pz[:], lhsT=wt[:, kc, :], rhs=hT[:, kc, :], start=(kc == 0), stop=(kc == NFC - 1)),
                     reads=[bwt, bhT[kc]], writes=[bpz], inc=(kc == NFC - 1), acc=(kc > 0), lhs=[bwt])
            K.op(K.dve, lambda: nc.vector.scalar_tensor_tensor(out=z[:, c, :], in0=pz[:], scalar=0.5, in1=xa[:, c, :], op0=ALU.mult, op1=ALU.add),
                 reads=[bpz, bxa[c]], writes=[bz[c]])

    def p1(self, slot):
        nc, K = self.nc, self.K
        K.begin_scope("p1")
        with ExitStack() as es:
            sb = lambda n, sh, dt: self.sbt(es, "p1_%d_" % slot + n, sh, dt)
            xtok = sb("xtok", [128, 4, D], F32); bsq = K.bufs(8, "xtok_sq")
            sqf = lambda c: xtok[:, c // 2, (c % 2) * TT:(c % 2 + 1) * TT]
            xa = sb("xa", [128, 8, TT], F32); bxa = K.bufs(8, "xa")
            xb = sb("xb", [128, 8, TT], BF16); bxb = K.bufs(8, "xb")
            hT = sb("hT", [128, NFC, TT], BF16); bhT = K.bufs(NFC, "hT")
            bhTr = K.bufs(16, "hTr")
            z = sb("z", [128, 8, TT], F32); bz = K.bufs(8, "z")
            st = sb("st", [128, TT], F32); bst = K.buf("st")
            st2 = sb("st2", [128, TT], F32); bst2 = K.buf("st2")
            x1b = sb("x1b", [128, 8, TT], BF16); bx1b = K.bufs(8, "x1b")
            wgu = [sb("wgu%d" % i, [128, 8, 512], BF16) for i in range(2)]; bwgu = K.bufs(2, "wgu")
            wdn = [sb("wdn%d" % i, [128, NFC, 128], BF16) for i in range(2)]; bwdn = K.bufs(2, "wdn")
            ev = sb("ev", [128, 4, TT], F32); bev = K.bufs(4, "ev")
            wuq = sb("wuq", [128, 3, 1536], BF16); bwuq = K.buf("wuq")
            wukv = sb("wukv", [128, 2, 1024], BF16); bwukv = K.buf("wukv")
            vst = sb("vst", [128, 4, 520], BF16); bvst = K.buf("vst")
            ropet = sb("ropet", [96, 2, TT], F32); bropet = K.buf("ropet")
            K.dma(K.sp, wuq[:], self.ws["wuq"], reads=[self.db["wuq"]], writes=[bwuq])
            K.dma(K.sp, wukv[:], self.ws["wukv"], reads=[self.db["wukv"]], writes=[bwukv])
            K.op(K.pool, lambda: nc.gpsimd.memset(vst[:], 1.0), writes=[bvst])
            psA = [self.pst(es, "p1_%d_ps%%d" % slot % i, [128, TT], F32) for i in range(4)]; bpsA = K.bufs(4, "psA")
            psS = self.pst(es, "p1_%d_pss" % slot, [128, TT], F32); bpsS = K.buf("psS")
            psT = [self.pst(es, "p1_%d_pst%%d" % slot % i, [128, TT], F32) for i in range(2)]; bpsT = K.bufs(2, "psT")
            L = dict(locals())
            xload = lambda tt: K.dma(K.sp, xtok[:], self.x[slot, tt * TT:(tt + 1) * TT, :].rearrange("(n p) d -> p n d", p=128), writes=bsq)
            L["xload"] = xload
            xload(0)
            for t in range(self.NT):
                t0 = t * TT
                for kc in range(8):
                    p, bp = psT[kc % 2], bpsT[kc % 2]
                    for n in range(4):
                        K.op(K.pe, lambda: nc.tensor.transpose(out=p[:, n * 128:(n + 1) * 128], in_=xtok[:, n, kc * 128:(kc + 1) * 128], identity=self.idf[:]),
                             reads=bsq + [self.b_id], writes=[bp], inc=(n == 3), acc=(n > 0))
                    K.op(K.act, lambda: nc.scalar.mul(out=xa[:, kc, :], in_=p[:], mul=ALPHA), reads=[bp], writes=[bxa[kc]])
                    K.op(K.dve, lambda: nc.vector.tensor_copy(out=xb[:, kc, :], in_=p[:]), reads=[bp], writes=[bxb[kc]])
                self.ffn(0, xb, bxb, xa, bxa, hT, bhT, wgu, bwgu, wdn, bwdn, psA, bpsA, z, bz, extra={i: [bhTr[i]] for i in range(16)})
                self.ln_fm(z, bz, "ln1_g", "ln1_b", xa, bxa, x1b, bx1b, psS, bpsS, sqf, bsq, st, bst)
                K.dma(K.pool, self.x1T.rearrange("(c p) s -> p c s", p=128)[:, :, t0:t0 + TT], xa[:], reads=bxa, writes=[self.db["x1T"]], acc=True, owner=bxa[0])
                self.p1_win(slot, t, L)

    def p1_win(self, slot, t, L):
        nc, K = self.nc, self.K
        x1b, bx1b, wbuf, bwbuf, psA, bpsA, psS, bpsS, ev, bev = [L[k] for k in ["x1b", "bx1b", "wgu", "bwgu", "psA", "bpsA", "psS", "bpsS", "ev", "bev"]]
        z, bz, xb, bxb, hT, bhT, st, bst, sqf, bsq, bhTr = [L[k] for k in ["z", "bz", "xb", "bxb", "hT", "bhT", "st", "bst", "sqf", "bsq", "bhTr"]]
        wuq, bwuq, wukv, bwukv, vst, bvst, ropet, bropet = [L[k] for k in ["wuq", "bwuq", "wukv", "bwukv", "vst", "bvst", "ropet", "bropet"]]
        t0 = t * TT
        dbw = self.db["win"]
        K.dma(K.sp, wbuf[0][:], self.win_s[0], reads=[dbw], writes=[bwbuf[0]])
        K.dma(K.sp, ropet[64:96, 0, :], self.ropec[:, t0:t0 + TT], writes=[bropet])
        K.dma(K.sp, ropet[64:96, 1, :], self.ropes[:, t0:t0 + TT], writes=[bropet], acc=True)
        stt = {"pi": 0, "ei": 0}

        def nextps():
            i = stt["pi"] % 4
            stt["pi"] += 1
            return psA[i], bpsA[i]

        def proj(wt, bwt, c0, width):
            p, bp = nextps()
            for kc in range(8):
                K.op(K.pe, lambda: nc.tensor.matmul(p[0:width, :], lhsT=wt[:, kc, c0:c0 + width], rhs=x1b[:, kc, :], start=(kc == 0), stop=(kc == 7)),
                     reads=[bwt, bx1b[kc]], writes=[bp], inc=(kc == 7), acc=(kc > 0), lhs=[bwt])
            return p, bp

        def rope(pp, bpp, psw, bpsw, out_ap, bout):
            t1, t2 = z[64:96, 3, :], z[64:96, 4, :]
            K.op(K.dve, lambda: nc.vector.tensor_tensor(out=t1, in0=pp[64:96, :], in1=ropet[64:96, 0, :], op=ALU.mult), reads=[bpp, bropet], writes=[bz[3]])
            K.op(K.dve, lambda: nc.vector.tensor_tensor(out=t2, in0=psw[64:96, :], in1=ropet[64:96, 1, :], op=ALU.mult), reads=[bpsw, bropet], writes=[bz[4]])
            K.op(K.pool, lambda: nc.gpsimd.tensor_tensor(out=out_ap, in0=t1, in1=t2, op=ALU.add), reads=[bz[3], bz[4]], writes=bout)

        deferred = []
        for blk in range(11):
            if blk + 1 < 11:
                K.dma(K.sp, wbuf[(blk + 1) % 2][:], self.win_s[blk + 1], reads=[dbw], writes=[bwbuf[(blk + 1) % 2]])
            wt, bwt = wbuf[blk % 2], bwbuf[blk % 2]
            if blk == 0:
                for c in range(3):
                    p, bp = proj(wt, bwt, c * 128, 128)
                    K.op(K.act, lambda: nc.scalar.copy(out=z[:, c, :], in_=p[:]), reads=[bp], writes=[bz[c]])
                self.rms_fm(lambda c: z[:, c, :], bz, 3, 384.0, "q_norm_g", sqf, bsq, psS, bpsS, st, bst, lambda c: xb[:, c, :], bxb)

                def q_part_b():
                  for h in range(H):
                      pp, bpp = nextps()
                      psw, bpsw = nextps()
                      for v_, (pt_, bpt_) in enumerate([(pp, bpp), (psw, bpsw)]):
                          for kc in range(3):
                              K.op(K.pe, lambda: nc.tensor.matmul(pt_[0:96, :], lhsT=wuq[:, kc, (v_ * 8 + h) * 96:(v_ * 8 + h + 1) * 96], rhs=xb[:, kc, :], start=(kc == 0), stop=(kc == 2)),
                                   reads=[bwuq, bxb[kc]], writes=[bpt_], inc=(kc == 2), acc=(kc > 0))
                      K.op(K.act, lambda: nc.scalar.copy(out=hT[0:64, h, :], in_=pp[0:64, :]), reads=[bpp], writes=[bhT[h]])
                      rope(pp, bpp, psw, bpsw, hT[64:96, h, :], [bhTr[h]])
                  K.dma(K.pool, self.qT.rearrange("h d s -> d h s")[:, :, t0:t0 + TT], hT[0:96, 0:8, :], reads=bhT[0:8] + bhTr[0:8], writes=[self.db["qT"]], acc=True, owner=bhT[0])
                deferred.append(q_part_b)
            elif blk == 1:
                for c in range(2):
                    p, bp = proj(wt, bwt, c * 128, 128)
                    K.op(K.act, lambda: nc.scalar.copy(out=z[:, 5 + c, :], in_=p[:]), reads=[bp], writes=[bz[5 + c]])
                self.rms_fm(lambda c: z[:, 5 + c, :], bz[5:7], 2, 256.0, "kv_norm_g", sqf, bsq, psS, bpsS, st, bst, lambda c: xb[:, 3 + c, :], bxb[3:5])

                def kv_part_b():
                  for h in range(H):
                      p, bp = nextps()
                      for kc in range(2):
                          K.op(K.pe, lambda: nc.tensor.matmul(p[0:64, :], lhsT=wukv[:, kc, h * 64:(h + 1) * 64], rhs=xb[:, 3 + kc, :], start=(kc == 0), stop=(kc == 1)),
                               reads=[bwukv, bxb[3 + kc]], writes=[bp], inc=(kc == 1), acc=(kc > 0))
                      K.op(K.act, lambda: nc.scalar.copy(out=hT[0:64, 8 + h, :], in_=p[0:64, :]), reads=[bp], writes=[bhT[8 + h]])
                  for n in range(4):
                      p, bp = nextps()
                      for kc in range(2):
                          K.op(K.pe, lambda: nc.tensor.matmul(p[:], lhsT=xb[:, 3 + kc, n * 128:(n + 1) * 128], rhs=wukv[:, kc, 512:1024], start=(kc == 0), stop=(kc == 1)),
                               reads=[bwukv, bxb[3 + kc]], writes=[bp], inc=(kc == 1), acc=(kc > 0))
                      K.op(K.dve, lambda: nc.vector.tensor_copy(out=vst[:, n, :].rearrange("p (h d) -> p h d", d=65)[:, :, 0:64], in_=p[:].rearrange("p (h d) -> p h d", d=64)), reads=[bp], writes=[bvst], acc=(n > 0))
                  K.dma(K.pool, self.vtok[t0:t0 + TT].rearrange("(n p) h d -> p n (h d)", p=128), vst[:], reads=[bvst], writes=[self.db["vtok"]], acc=True, owner=bvst)
                deferred.append(kv_part_b)
                if t + 1 < self.NT:
                    L["xload"](t + 1)
            elif blk == 2:
                pp, bpp = proj(wt, bwt, 0, 96)
                psw, bpsw = proj(wt, bwt, 96, 96)
                rope(pp, bpp, psw, bpsw, z[64:96, 7, :], [bz[7]])
                for h in range(H):
                    eng = K.act if h % 2 == 0 else K.pool
                    if h % 2 == 0:
                        K.op(K.act, lambda: nc.scalar.copy(out=hT[64:96, 8 + h, :], in_=z[64:96, 7, :]), reads=[bz[7]], writes=[bhTr[8 + h]])
                    else:
                        K.op(K.pool, lambda: nc.gpsimd.tensor_copy(out=hT[64:96, 8 + h, :], in_=z[64:96, 7, :]), reads=[bz[7]], writes=[bhTr[8 + h]])
                deferred.append(lambda: K.dma(K.pool, self.kT.rearrange("h d s -> d h s")[:, :, t0:t0 + TT], hT[0:96, 8:16, :], reads=bhT[8:16] + bhTr[8:16], writes=[self.db["kT"]], acc=True, owner=bhT[8]))
            elif blk in (3, 4, 5, 6):
                widths = [128] * 4 if blk < 6 else [128, 64, 128]
                c0 = 0
                for i, wd_ in enumerate(widths):
                    p, bp = proj(wt, bwt, c0, wd_)
                    e, be = ev[:, stt["ei"] % 4, :], bev[stt["ei"] % 4]
                    stt["ei"] += 1
                    K.op(K.act, lambda: nc.scalar.copy(out=e[0:wd_, :], in_=p[0:wd_, :]), reads=[bp], writes=[be])
                    r0 = (blk - 3) * 512 + c0
                    K.dma(K.pool, self.hrT[r0:r0 + wd_, t0:t0 + TT], e[0:wd_, :], reads=[be], writes=[self.db["hrT"]], acc=True, owner=be)
                    c0 += wd_
            else:
                for i in range(4):
                    gc = (blk - 7) * 4 + i
                    p, bp = proj(wt, bwt, i * 128, 128)
                    e, be = ev[:, stt["ei"] % 4, :], bev[stt["ei"] % 4]
                    stt["ei"] += 1
                    K.op(K.act, lambda: nc.scalar.activation(out=e, in_=p[:], func=AF.Sigmoid, bias=self.col("b_gate", gc)),
                         reads=[bp, self.b_pc], writes=[be])
                    K.dma(K.pool, self.gates[gc * 128:(gc + 1) * 128, t0:t0 + TT], e, reads=[be], writes=[self.db["gates"]], acc=True, owner=be)


        for f in deferred:
            f()

    def p2(self, slot):
        nc, K = self.nc, self.K
        K.begin_scope("p2")
        S = self.S
        NK = S // 128
        scale = float(96 ** -0.5)
        with ExitStack() as es:
            sb = lambda n, sh, dt: self.sbt(es, "p2_%d_" % slot + n, sh, dt)
            KT = sb("KT", [96, 8, S], BF16); bKT = K.buf("KT")
            Vt = sb("Vt", [128, NK, 520], BF16); bVt = K.buf("Vt")
            Qt = [sb("Qt%d" % i, [96, 8, TT], BF16) for i in range(2)]; bQt = K.bufs(2, "Qt")
            PT = [sb("PT%d" % i, [128, TT], BF16) for i in range(4)]; bPT = K.bufs(4, "PT")
            Otok = sb("Otok", [128, 4, 512], BF16); bOtok = K.bufs(8, "Otok")
            OTs = sb("OTs", [128, 4, TT], BF16); bOTs = K.bufs(4, "OTs")
            rs = sb("rs", [128, 2, 4], F32); brs = K.bufs(2, "rs")
            psS = [self.pst(es, "p2_%d_pss%%d" % slot % i, [128, TT], F32) for i in range(3)]; bpsS = K.bufs(3, "psS")
            psO = [self.pst(es, "p2_%d_pso%%d" % slot % i, [128, TT], F32) for i in range(2)]; bpsO = K.bufs(2, "psO")
            psT = self.pst(es, "p2_%d_pst" % slot, [128, 2 * TT], BF16); bpsT = K.buf("psT")
            for h in range(H):
                K.dma(K.sp, KT[:, h, :], self.kT[h], reads=[self.db["kT"]], writes=[bKT], acc=(h > 0))
            K.dma(K.sp, Vt[:], self.vtok.rearrange("(n p) h d -> p n (h d)", p=128), reads=[self.db["vtok"]], writes=[bVt])
            qTv = self.qT.rearrange("h d s -> d h s")
            K.dma(K.sp, Qt[0][:], qTv[:, :, 0:TT], reads=[self.db["qT"]], writes=[bQt[0]])
            items = [(t, h, kt) for t in range(self.NT) for h in range(H) for kt in range(NK)]

            def emit_S(i):
                t, h, kt = items[i]
                if h == 0 and kt == 0 and t + 1 < self.NT:
                    K.dma(K.sp, Qt[(t + 1) % 2][:], qTv[:, :, (t + 1) * TT:(t + 2) * TT], reads=[self.db["qT"]], writes=[bQt[(t + 1) % 2]])
                pS, bpS = psS[i % 3], bpsS[i % 3]
                K.op(K.pe, lambda: nc.tensor.matmul(pS[:], lhsT=KT[:, h, kt * 128:(kt + 1) * 128], rhs=Qt[t % 2][:, h, :], start=True, stop=True),
                     reads=[bKT, bQt[t % 2]], writes=[bpS], lhs=[bKT])

            emit_S(0)
            emit_S(1)
            for i, (t, h, kt) in enumerate(items):
                t0 = t * TT
                if i + 2 < len(items):
                    emit_S(i + 2)
                pS, bpS = psS[i % 3], bpsS[i % 3]
                P_, bP = PT[i % 4], bPT[i % 4]
                pO, bpO = psO[h % 2], bpsO[h % 2]
                K.op(K.act, lambda: nc.scalar.activation(out=P_[:], in_=pS[:], func=AF.Exp, scale=scale), reads=[bpS], writes=[bP])
                for qs in range(4):
                    K.op(K.pe, lambda: nc.tensor.matmul(pO[:, qs * 65:(qs + 1) * 65], lhsT=P_[:, qs * 128:(qs + 1) * 128], rhs=Vt[:, kt, h * 65:(h + 1) * 65],
                                                         start=(kt == 0 and qs == 0), stop=(kt == NK - 1 and qs == 3)),
                         reads=[bP, bVt], writes=[bpO], inc=(qs == 3), acc=not (kt == 0 and qs == 0))
                if kt == NK - 1:
                    r_, br = rs[:, h % 2, :], brs[h % 2]
                    K.op(K.dve, lambda: nc.vector.reciprocal(out=r_, in_=pO[:, 0:260].rearrange("p (q d) -> p q d", d=65)[:, :, 64]), reads=[bpO], writes=[br])
                    for qs in range(4):
                        K.op(K.dve, lambda: nc.vector.tensor_scalar(out=Otok[:, qs, h * 64:(h + 1) * 64], in0=pO[:, qs * 65:qs * 65 + 64], scalar1=rs[:, h % 2, qs:qs + 1], scalar2=None, op0=ALU.mult),
                             reads=[bpO, br], writes=[bOtok[h]], acc=(qs > 0))
                    if h == H - 1:
                        for c in range(4):
                            for qs in range(4):
                                K.op(K.pe, lambda: nc.tensor.transpose(out=psT[:, qs * 128:(qs + 1) * 128], in_=Otok[:, qs, c * 128:(c + 1) * 128], identity=self.idb[:]),
                                     reads=[bOtok[2 * c], bOtok[2 * c + 1], self.b_id], writes=[bpsT], inc=(qs == 3), acc=(qs > 0))
                            K.op(K.dve, lambda: nc.vector.tensor_copy(out=OTs[:, c, :], in_=psT[:, 0:TT]), reads=[bpsT], writes=[bOTs[c]])
                        K.dma(K.pool, self.OT.rearrange("(c p) s -> p c s", p=128)[:, :, t0:t0 + TT], OTs[:], reads=bOTs, writes=[self.db["OT"]], acc=True, owner=bOTs[0])

    def p3a(self, slot):
        nc, K = self.nc, self.K
        K.begin_scope("p3a")
        S, NT = self.S, self.NT
        with ExitStack() as es:
            sb = lambda n, sh, dt: self.sbt(es, "p3a_%d_" % slot + n, sh, dt)
            hr = sb("hr", [128, 15, TT + 2], F32); bhr = K.bufs(15, "hr")
            sh = sb("sh", [128, 15, TT], F32); bsh = K.bufs(15, "sh")
            lw = sb("lw", [128, 8, TT], F32); blw = K.bufs(8, "lw")
            eta = sb("eta", [128, 4, TT], F32); beta = K.bufs(4, "eta")
            o4 = sb("o4", [128, 3, 4, TT], F32); bo4 = [K.bufs(4, "o4_%d" % i) for i in range(3)]
            gT = sb("gT", [128, 4, TT], F32); bgT = K.bufs(4, "gT")
            bon = sb("bon", [128, 4, TT], F32); bbon = K.bufs(4, "bon")
            tmp = sb("tmp", [128, 4, TT], F32); btmp = K.bufs(4, "tmp")
            tb16 = sb("tb16", [128, 3, TT], BF16); btb = K.bufs(3, "tb16")
            vrt = sb("vrt", [128, 4, 512], BF16); bvrt = K.bufs(4, "vrt")
            wup = sb("wup", [128, 512], BF16); aup = sb("aup", [64, 512], BF16); gup = sb("gup", [128, 512], BF16); bw = K.buf("rwkvw")
            cc = sb("cc", [128, 15 + 4], F32); bcc = K.buf("cc")
            K.dma(K.sp, wup[:], self.ws["wup"][:, 0, :], reads=[self.db["wup"]], writes=[bw])
            K.dma(K.sp, aup[:], self.ws["aup"][:, 0, :], reads=[self.db["aup"]], writes=[bw], acc=True)
            K.dma(K.sp, gup[:], self.ws["gup"][:, 0, :], reads=[self.db["gup"]], writes=[bw], acc=True)
            o, _ = COLS["mu_prev"]; o2, _ = COLS["mu_next"]; oka, _ = COLS["k_a"]
            K.op(K.dve, lambda: nc.vector.tensor_tensor(out=cc[:, 0:15], in0=self.pc[:, o:o + 15], in1=self.pc[:, o2:o2 + 15], op=ALU.add), reads=[self.b_pc], writes=[bcc])
            K.op(K.dve, lambda: nc.vector.tensor_scalar(out=cc[:, 0:15], in0=cc[:, 0:15], scalar1=-1.0, scalar2=1.0, op0=ALU.mult, op1=ALU.add), reads=[bcc], writes=[bcc])
            K.op(K.dve, lambda: nc.vector.tensor_scalar(out=cc[:, 15:19], in0=self.pc[:, oka:oka + 4], scalar1=-1.0, scalar2=1.0, op0=ALU.mult, op1=ALU.add), reads=[self.b_pc, bcc], writes=[bcc])
            ps = [self.pst(es, "p3a_%d_ps%%d" % slot % i, [128, TT], F32) for i in range(5)]; bps = K.bufs(5, "ps3a")
            pi = [0]

            def nps():
                i = pi[0] % 5
                pi[0] += 1
                return ps[i], bps[i]

            dbh = self.db["hrT"]
            for t in range(NT):
                t0 = t * TT
                lo, hi = max(t0 - 1, 0), min(t0 + TT + 1, S)
                a_, b_ = lo - (t0 - 1), hi - (t0 - 1)
                K.dma(K.sp, hr[:, 0:12, a_:b_], self.hrT[0:1536].rearrange("(c p) s -> p c s", p=128)[:, :, lo:hi], reads=[dbh], writes=bhr[0:12])
                K.dma(K.sp, hr[:, 12, a_:b_], self.hrT[1536:1664, lo:hi], reads=[dbh], writes=[bhr[12]])
                K.dma(K.sp, hr[0:64, 13, a_:b_], self.hrT[1664:1728, lo:hi], reads=[dbh], writes=[bhr[13]])
                K.dma(K.sp, hr[:, 14, a_:b_], self.hrT[1728:1856, lo:hi], reads=[dbh], writes=[bhr[14]])
                if t == 0:
                    K.op(K.pool, lambda: nc.gpsimd.memset(hr[:, :, 0:1], 0.0), writes=bhr, acc=True)
                if t == NT - 1:
                    K.op(K.pool, lambda: nc.gpsimd.memset(hr[:, :, TT + 1:TT + 2], 0.0), writes=bhr, acc=True)
                for j, (ro, wd_) in enumerate(RW_CH):
                    K.op(K.act, lambda: nc.scalar.activation(out=sh[0:wd_, j, :], in_=hr[0:wd_, j, 1:TT + 1], func=AF.Identity, scale=cc[0:wd_, j:j + 1]),
                         reads=[bhr[j], bcc], writes=[bsh[j]])
                    K.op(K.dve, lambda: nc.vector.scalar_tensor_tensor(out=sh[0:wd_, j, :], in0=hr[0:wd_, j, 0:TT], scalar=self.pc[0:wd_, o + j:o + j + 1], in1=sh[0:wd_, j, :], op0=ALU.mult, op1=ALU.add),
                         reads=[bhr[j], bsh[j], self.b_pc], writes=[bsh[j]])
                    K.op(K.dve, lambda: nc.vector.scalar_tensor_tensor(out=sh[0:wd_, j, :], in0=hr[0:wd_, j, 2:TT + 2], scalar=self.pc[0:wd_, o2 + j:o2 + j + 1], in1=sh[0:wd_, j, :], op0=ALU.mult, op1=ALU.add),
                         reads=[bhr[j], bsh[j], self.b_pc], writes=[bsh[j]])
                r_ = lambda c: sh[:, c, :]
                k_ = lambda c: sh[:, 4 + c, :]
                v_ = lambda c: sh[:, 8 + c, :]
                K.op(K.act, lambda: nc.scalar.activation(out=tb16[:, 0, :], in_=sh[:, 12, :], func=AF.Tanh), reads=[bsh[12]], writes=[btb[0]])
                K.op(K.pool, lambda: nc.gpsimd.tensor_copy(out=tb16[0:64, 1, :], in_=sh[0:64, 13, :]), reads=[bsh[13]], writes=[btb[1]])
                K.op(K.act, lambda: nc.scalar.activation(out=tb16[:, 2, :], in_=sh[:, 14, :], func=AF.Sigmoid), reads=[bsh[14]], writes=[btb[2]])
                for d in range(2):
                    for c in range(4):
                        p, bp = nps()
                        K.op(K.pe, lambda: nc.tensor.matmul(p[:], lhsT=wup[64 * d:64 * d + 64, c * 128:(c + 1) * 128], rhs=tb16[64 * d:64 * d + 64, 0, :], start=True, stop=True),
                             reads=[bw, btb[0]], writes=[bp])
                        K.op(K.act, lambda: nc.scalar.activation(out=lw[:, d * 4 + c, :], in_=p[:], func=AF.Sigmoid, bias=self.col("w0", d * 4 + c)), reads=[bp, self.b_pc], writes=[blw[d * 4 + c]])
                        K.op(K.pool, lambda: nc.gpsimd.tensor_scalar(out=lw[:, d * 4 + c, :], in0=lw[:, d * 4 + c, :], scalar1=-WCONST, scalar2=None, op0=ALU.mult), reads=[blw[d * 4 + c]], writes=[blw[d * 4 + c]])
                    K.dma(K.pool, self.rwlw[d].rearrange("(c p) s -> p c s", p=128)[:, :, t0:t0 + TT], lw[:, d * 4:d * 4 + 4, :], reads=blw[d * 4:d * 4 + 4], writes=[self.db["rwlw"]], acc=True, owner=blw[d * 4])
                for c in range(4):
                    p, bp = nps()
                    K.op(K.pe, lambda: nc.tensor.matmul(p[:], lhsT=aup[0:64, c * 128:(c + 1) * 128], rhs=tb16[0:64, 1, :], start=True, stop=True), reads=[bw, btb[1]], writes=[bp])
                    K.op(K.act, lambda: nc.scalar.activation(out=eta[:, c, :], in_=p[:], func=AF.Sigmoid, bias=self.col("a0", c)), reads=[bp, self.b_pc], writes=[beta[c]])
                    p, bp = nps()
                    K.op(K.pe, lambda: nc.tensor.matmul(p[:], lhsT=gup[:, c * 128:(c + 1) * 128], rhs=tb16[:, 2, :], start=True, stop=True), reads=[bw, btb[2]], writes=[bp])
                    K.op(K.act, lambda: nc.scalar.copy(out=gT[:, c, :], in_=p[:]), reads=[bp], writes=[bgT[c]])
                    kk, bkk = tmp[:, c % 2, :], btmp[c % 2]
                    sq, bsq_ = tmp[:, 2 + c % 2, :], btmp[2 + c % 2]
                    K.op(K.act, lambda: nc.scalar.activation(out=kk, in_=k_(c), func=AF.Identity, scale=self.col("k_k", c)), reads=[bsh[4 + c], self.b_pc], writes=[bkk])
                    K.op(K.act, lambda: nc.scalar.activation(out=sq, in_=kk, func=AF.Square), reads=[bkk], writes=[bsq_])
                    p, bp = nps()
                    K.op(K.pe, lambda: nc.tensor.matmul(p[:], lhsT=self.ones2, rhs=sq, start=True, stop=True), reads=[self.b_c2, bsq_], writes=[bp])
                    K.op(K.act, lambda: nc.scalar.activation(out=sq, in_=p[:], func=AF.Ln, bias=self.epsc[:, 3:4]), reads=[bp, self.b_ones], writes=[bsq_])
                    K.op(K.act, lambda: nc.scalar.activation(out=sq, in_=sq, func=AF.Exp, scale=-0.5), reads=[bsq_], writes=[bsq_])
                    av, bav = o4[:, 1, c, :], bo4[1][c]
                    bv, bbv = o4[:, 2, c, :], bo4[2][c]
                    km, bkm = o4[:, 0, c, :], bo4[0][c]
                    K.op(K.dve, lambda: nc.vector.scalar_tensor_tensor(out=av, in0=kk, scalar=-1.0, in1=sq, op0=ALU.mult, op1=ALU.mult), reads=[bkk, bsq_], writes=[bav])
                    K.op(K.dve, lambda: nc.vector.scalar_tensor_tensor(out=bv, in0=av, scalar=-1.0, in1=eta[:, c, :], op0=ALU.mult, op1=ALU.mult), reads=[bav, beta[c]], writes=[bbv])
                    K.op(K.dve, lambda: nc.vector.tensor_scalar(out=km, in0=eta[:, c, :], scalar1=self.col("k_a", c), scalar2=cc[:, 15 + c:16 + c], op0=ALU.mult, op1=ALU.add), reads=[beta[c], self.b_pc, bcc], writes=[bkm])
                    K.op(K.pool, lambda: nc.gpsimd.tensor_tensor(out=km, in0=km, in1=k_(c), op=ALU.mult), reads=[bkm, bsh[4 + c]], writes=[bkm])
                    K.op(K.dve, lambda: nc.vector.scalar_tensor_tensor(out=kk, in0=r_(c), scalar=self.col("r_k", c), in1=km, op0=ALU.mult, op1=ALU.mult), reads=[bsh[c], bkm, self.b_pc], writes=[bkk])
                    p, bp = nps()
                    K.op(K.pe, lambda: nc.tensor.matmul(p[:], lhsT=self.ones2, rhs=kk, start=True, stop=True), reads=[self.b_c2, bkk], writes=[bp])
                    K.op(K.dve, lambda: nc.vector.tensor_tensor(out=bon[:, c, :], in0=p[:], in1=v_(c), op=ALU.mult), reads=[bp, bsh[8 + c]], writes=[bbon[c]])
                for n in range(4):
                    p, bp = nps()
                    for c in range(4):
                        K.op(K.pe, lambda: nc.tensor.transpose(out=p[:, c * 128:(c + 1) * 128], in_=sh[:, 8 + c, n * 128:(n + 1) * 128], identity=self.idf[:]),
                             reads=[bsh[8 + c], self.b_id], writes=[bp], inc=(c == 3), acc=(c > 0))
                    K.op(K.act, lambda: nc.scalar.copy(out=vrt[:, n, :], in_=p[:]), reads=[bp], writes=[bvrt[n]])
                K.dma(K.pool, self.rwv[t0:t0 + TT].rearrange("(n p) c -> p n c", p=128), vrt[:], reads=bvrt, writes=[self.db["rwv"]], acc=True, owner=bvrt[0])
                rw4v = lambda i: self.rw4[i].rearrange("(c p) s -> p c s", p=128)[:, :, t0:t0 + TT]
                K.dma(K.pool, rw4v(0), sh[:, 0:4, :], reads=bsh[0:4], writes=[self.db["rw4"]], acc=True, owner=bsh[0])
                for i in range(3):
                    K.dma(K.pool, rw4v(1 + i), o4[:, i, :, :], reads=bo4[i], writes=[self.db["rw4"]], acc=True, owner=bo4[i][0])
                K.dma(K.pool, self.rwg.rearrange("(c p) s -> p c s", p=128)[:, :, t0:t0 + TT], gT[:], reads=bgT, writes=[self.db["rwg"]], acc=True, owner=bgT[0])
                K.dma(K.pool, self.rwbon.rearrange("(c p) s -> p c s", p=128)[:, :, t0:t0 + TT], bon[:], reads=bbon, writes=[self.db["rwbon"]], acc=True, owner=bbon[0])

    def p3b(self, slot):
        nc, K = self.nc, self.K
        K.begin_scope("p3b")
        S, NT = self.S, self.NT
        with ExitStack() as es:
            sb = lambda n, sh, dt: self.sbt(es, "p3b_%d_" % slot + n, sh, dt)
            R2 = range(2)
            arT = [sb("arT%d" % d, [128, 4, 8, 128], BF16) for d in R2]; barT = [K.bufs(4, "arT%d_" % d) for d in R2]
            bkT = [sb("bkT%d" % d, [128, 4, 8, 128], BF16) for d in R2]; bbkT = [K.bufs(4, "bkT%d_" % d) for d in R2]
            btk = [sb("btk%d" % d, [128, 8, 2, 512], BF16) for d in R2]; bbtk = [K.bufs(16, "btk%d_" % d) for d in R2]
            vt = [sb("vt%d" % d, [128, 8, 512], BF16) for d in R2]; bvt = K.bufs(2, "vt")
            et = [sb("et%d" % d, [128, 4, 8], F32) for d in R2]; bet = K.bufs(2, "et")
            S32 = [sb("S32_%d" % d, [128, 4, 64], F32) for d in R2]; bS32 = K.bufs(2, "S32")
            Sb = [sb("Sb%d" % d, [128, 4, 64], BF16) for d in R2]; bSb = K.bufs(2, "Sb")
            inb = [sb("inb%d" % i, [128, 5, TT], F32) for i in range(2)]; binb = K.bufs(2, "inb")
            gt = sb("gt", [128, 5, TT], F32); bgt = K.bufs(5, "gt")
            Xs = [[sb("Xs%d_%d" % (d, q), [128, 2, 4, 128], BF16) for q in R2] for d in R2]; bXs = [[K.bufs(2, "Xs%d_%d_" % (d, q)) for q in R2] for d in R2]
            As = [sb("As%d" % d, [128, 2, 4, 64], BF16) for d in R2]; bAs = [K.bufs(2, "As%d_" % d) for d in R2]
            Ns = [sb("Ns%d" % d, [128, 2, 4, 64], BF16) for d in R2]; bNs = [K.bufs(2, "Ns%d_" % d) for d in R2]
            Ms = [[sb("Ms%d_%d" % (d, q), [128, 4, 64], BF16) for q in R2] for d in R2]; bMs = [K.bufs(2, "Ms%d_" % d) for d in R2]
            Ws = [sb("Ws%d" % d, [128, 4, 64], BF16) for d in R2]; bWs = K.bufs(2, "Ws")
            Us = [sb("Us%d" % d, [128, 4, 64], BF16) for d in R2]; bUs = K.bufs(2, "Us")
            yt = [sb("yt%d" % d, [128, 2, 256], F32) for d in R2]; byt = [K.bufs(2, "yt%d_" % d) for d in R2]
            stmp = sb("stmp", [128, 2, 256], F32); bstmp = K.bufs(2, "stmp")
            ps = [self.pst(es, "p3b_%d_ps%%d" % slot % i, [128, TT], F32) for i in range(6)]; bps = K.bufs(6, "ps3b")
            pst = self.pst(es, "p3b_%d_pst" % slot, [128, 2 * TT], BF16); bpst = K.buf("ps3bt")
            pi = [0]

            def nps2():
                i = pi[0] % 3
                pi[0] += 1
                return [(ps[2 * i], bps[2 * i]), (ps[2 * i + 1], bps[2 * i + 1])]

            for d in R2:
                K.op(K.pool, lambda: nc.gpsimd.memset(S32[d][:], 0.0), writes=[bS32[d]])
                K.op(K.pool, lambda: nc.gpsimd.memset(Sb[d][:], 0.0), writes=[bSb[d]])
            ii = [0]
            LIM = int(os.environ.get("P3B_LIM", "9"))
            PL = lambda l: slice(64 * l, 64 * l + 64)

            def prep(d, tile):
                t0 = tile * TT
                for l in range(2):
                    K.dma(K.sp, vt[d][PL(l), :, :], self.rwv[t0:t0 + TT].rearrange("(n p) c -> p n c", p=64), reads=[self.db["rwv"]], writes=[bvt[d]], acc=(l > 0))
                for c in range(4):
                    ib, bib = inb[ii[0] % 2], binb[ii[0] % 2]
                    ii[0] += 1
                    K.dma(K.sp, ib[:, 0:4, :], self.rw4[:, c * 128:(c + 1) * 128, t0:t0 + TT].rearrange("i p s -> p i s"), reads=[self.db["rw4"]], writes=[bib])
                    K.dma(K.sp, ib[:, 4, :], self.rwlw[d, c * 128:(c + 1) * 128, t0:t0 + TT], reads=[self.db["rwlw"]], writes=[bib], acc=True)
                    r_, k_, a_, b_, lw_ = [ib[:, i, :] for i in range(5)]
                    L, G, Er, Ei, Ea = [gt[:, i, :] for i in range(5)]
                    v3 = lambda ap: ap.rearrange("p (n t) -> p n t", t=64)
                    K.op(K.dve, lambda: nc.vector.tensor_tensor_scan(out=L, data0=self.cmask, data1=lw_, initial=0.0, op0=ALU.mult, op1=ALU.add),
                         reads=[bib, self.b_c2], writes=[bgt[0]])
                    Ltot = v3(L)[:, :, 63]
                    K.op(K.act, lambda: nc.scalar.activation(out=et[d][:, c, :], in_=Ltot, func=AF.Exp), reads=[bgt[0]], writes=[bet[d]], acc=(c > 0))
                    if d == 0:
                        Gs, bG = L, bgt[0]
                    else:
                        K.op(K.dve, lambda: nc.vector.tensor_tensor(out=G, in0=lw_, in1=L, op=ALU.subtract), reads=[bib, bgt[0]], writes=[bgt[1]])
                        K.op(K.dve, lambda: nc.vector.tensor_tensor(out=v3(G), in0=v3(G), in1=Ltot.unsqueeze(2).broadcast_to([128, 8, 64]), op=ALU.add), reads=[bgt[1], bgt[0]], writes=[bgt[1]])
                        Gs, bG = G, bgt[1]
                    K.op(K.act, lambda: nc.scalar.activation(out=Er, in_=Gs, func=AF.Exp), reads=[bG], writes=[bgt[2]])
                    K.op(K.act, lambda: nc.scalar.activation(out=Ei, in_=Gs, func=AF.Exp, scale=-1.0), reads=[bG], writes=[bgt[3]])
                    K.op(K.dve, lambda: nc.vector.tensor_tensor(out=Ea, in0=Gs, in1=lw_, op=ALU.subtract), reads=[bG, bib], writes=[bgt[4]])
                    K.op(K.act, lambda: nc.scalar.activation(out=Ea, in_=Ea, func=AF.Exp), reads=[bgt[4]], writes=[bgt[4]])
                    K.op(K.dve, lambda: nc.vector.tensor_tensor(out=arT[d][:, c, :, 0:64], in0=v3(a_), in1=v3(Ea), op=ALU.mult), reads=[bib, bgt[4]], writes=[barT[d][c]])
                    K.op(K.pool, lambda: nc.gpsimd.tensor_tensor(out=arT[d][:, c, :, 64:128], in0=v3(r_), in1=v3(Er), op=ALU.mult), reads=[bib, bgt[2]], writes=[barT[d][c]], acc=True)
                    K.op(K.dve, lambda: nc.vector.tensor_tensor(out=bkT[d][:, c, :, 0:64], in0=v3(b_), in1=v3(Ei), op=ALU.mult), reads=[bib, bgt[3]], writes=[bbkT[d][c]])
                    K.op(K.pool, lambda: nc.gpsimd.tensor_tensor(out=bkT[d][:, c, :, 64:128], in0=v3(k_), in1=v3(Ei), op=ALU.mult), reads=[bib, bgt[3]], writes=[bbkT[d][c]], acc=True)
                for n in range(8):
                    for wh in range(2):
                        for hp in range(4):
                            for l in range(2):
                                K.op(K.pe, lambda: nc.tensor.transpose(out=pst[PL(l), hp * 128:(hp + 1) * 128], in_=bkT[d][:, hp, n, wh * 64:(wh + 1) * 64], identity=self.idb[:]),
                                     reads=[bbkT[d][hp], self.b_id], writes=[bpst], inc=(hp == 3 and l == 1), acc=not (hp == 0 and l == 0))
                        if wh == 0:
                            K.op(K.act, lambda: nc.scalar.copy(out=btk[d][:, n, wh, :], in_=pst[:, 0:512]), reads=[bpst], writes=[bbtk[d][n * 2 + wh]])
                        else:
                            K.op(K.dve, lambda: nc.vector.tensor_copy(out=btk[d][:, n, wh, :], in_=pst[:, 0:512]), reads=[bpst], writes=[bbtk[d][n * 2 + wh]])

            def grp_(n, mm, evac, width=64):
                pr = nps2()
                for l in range(2):
                    p, bp = pr[l]
                    for hh in range(4):
                        terms = mm(l, hh)
                        for ti, (l_, r_, lb, rb) in enumerate(terms):
                            K.op(K.pe, lambda: nc.tensor.matmul(p[PL(l), hh * width:(hh + 1) * width], lhsT=l_, rhs=r_, start=(ti == 0), stop=(ti == len(terms) - 1)),
                                 reads=lb + rb, writes=[bp], inc=(hh == 3 and ti == len(terms) - 1), acc=not (hh == 0 and ti == 0), lhs=lb)
                for l in range(2):
                    p, bp = pr[l]
                    evac(l, p[PL(l), 0:4 * width].rearrange("p (h t) -> p h t", t=width), bp)

            v64 = lambda ap: ap.rearrange("p (h t) -> p h t", t=64)

            def pre(d, n, q):
                XS, bX, MS, bM = Xs[d][q], bXs[d][q], Ms[d][q], bMs[d][q]
                mX = self.rmask(d)[:, 0:128].unsqueeze(1).broadcast_to([128, 4, 128])
                mA = self.rmask(d)[:, 128:192].unsqueeze(1).broadcast_to([128, 4, 64])
                idl4 = self.idl.unsqueeze(1).broadcast_to([128, 4, 64])
                AR, BK = arT[d], bkT[d]
                grp = lambda mm, evac, width=64: grp_(n, mm, evac, width)
                ar = lambda l, hh, c0: AR[PL(l), hh, n, c0:c0 + 64]
                bk = lambda l, hh, c0: BK[PL(l), hh, n, c0:c0 + 64]
                for wh in range(2):
                    def ev_X(l, pv, bp, wh=wh):
                        K.op(K.dve, lambda: nc.vector.tensor_tensor(out=XS[PL(l), wh, :, :], in0=pv, in1=mX[PL(l)], op=ALU.mult), reads=[bp, self.b_c2], writes=[bX[wh]], acc=(l > 0))
                    grp(lambda l, hh: [(bk(l, hh, wh * 64), AR[PL(l), hh, n, :], [bbkT[d][hh]], [barT[d][hh]])], ev_X, width=128)
                    yield

                def ev_A0(l, pv, bp):
                    K.op(K.act, lambda: nc.scalar.copy(out=As[d][PL(l), 0, :, :], in_=pv), reads=[bp], writes=[bAs[d][0]], acc=(l > 0))
                grp(lambda l, hh: [(ar(l, hh, 0), bk(l, hh, 0), [barT[d][hh]], [bbkT[d][hh]])], ev_A0)
                K.op(K.pool, lambda: nc.gpsimd.tensor_tensor(out=As[d][:, 0, :, :], in0=As[d][:, 0, :, :], in1=mA, op=ALU.mult), reads=[bAs[d][0], self.b_c2], writes=[bAs[d][0]])
                yield
                Ncur = lambda l, hh: XS[PL(l), 0, hh, 0:64]
                bNcur = [bX[0]]
                Acur = lambda l, hh: As[d][PL(l), 0, hh, :]
                bAcur = [bAs[d][0]]
                K.op(K.dve, lambda: nc.vector.tensor_tensor(out=MS[:], in0=XS[:, 0, :, 0:64], in1=idl4, op=ALU.add), reads=bNcur + [self.b_c2], writes=[bM])
                for lv in range(5):
                    o_ = (lv + 1) % 2
                    Nc, Ac, bNc, bAc = Ncur, Acur, bNcur, bAcur

                    def ev_A(l, pv, bp, o_=o_):
                        K.op(K.act, lambda: nc.scalar.copy(out=As[d][PL(l), o_, :, :], in_=pv), reads=[bp], writes=[bAs[d][o_]], acc=(l > 0))

                    def ev_N(l, pv, bp, o_=o_):
                        K.op(K.act, lambda: nc.scalar.copy(out=Ns[d][PL(l), o_, :, :], in_=pv), reads=[bp], writes=[bNs[d][o_]], acc=(l > 0))
                    grp(lambda l, hh: [(Nc(l, hh), Ac(l, hh), bNc, bAc)], ev_A)
                    yield
                    if lv < 4:
                        grp(lambda l, hh: [(Ac(l, hh), Nc(l, hh), bAc, bNc)], ev_N)
                        yield
                    Acur = (lambda oo: (lambda l, hh: As[d][PL(l), oo, hh, :]))(o_)
                    bAcur = [bAs[d][o_]]
                    Ncur = (lambda oo: (lambda l, hh: Ns[d][PL(l), oo, hh, :]))(o_)
                    bNcur = [bNs[d][o_]]
                    An, bAn = Acur, bAcur

                    def ev_M(l, pv, bp):
                        K.op(K.dve, lambda: nc.vector.tensor_tensor(out=MS[PL(l)], in0=MS[PL(l)], in1=pv, op=ALU.add), reads=[bM, bp], writes=[bM], acc=True)
                    grp(lambda l, hh: [(An(l, hh), MS[PL(l), hh, :], bAn, [bM])], ev_M)
                    yield

            def seq(d, tile, n, q, yi):
                XS, bX, MS, bM = Xs[d][q], bXs[d][q], Ms[d][q], bMs[d][q]
                AR = arT[d]
                grp = lambda mm, evac, width=64: grp_(n, mm, evac, width)
                ar = lambda l, hh, c0: AR[PL(l), hh, n, c0:c0 + 64]
                V = lambda l, hh: vt[d][PL(l), n, (2 * hh + l) * 64:(2 * hh + l + 1) * 64]
                St = lambda l, hh: Sb[d][PL(l), hh, :]

                def ev_W(l, pv, bp):
                    K.op(K.act, lambda: nc.scalar.copy(out=Ws[d][PL(l)], in_=pv), reads=[bp], writes=[bWs[d]], acc=(l > 0))
                grp(lambda l, hh: [(XS[PL(l), 1, hh, 0:64], V(l, hh), [bX[1]], [bvt[d]]),
                                   (ar(l, hh, 0), St(l, hh), [barT[d][hh]], [bSb[d]])], ev_W)
                yield

                def ev_U(l, pv, bp):
                    K.op(K.act, lambda: nc.scalar.copy(out=Us[d][PL(l)], in_=pv), reads=[bp], writes=[bUs[d]], acc=(l > 0))
                grp(lambda l, hh: [(MS[PL(l), hh, :], Ws[d][PL(l), hh, :], [bM], [bWs[d]])], ev_U)
                yield
                r0 = tile * TT + n * 64

                def ev_Y(l, pv, bp):
                    K.op(K.act, lambda: nc.scalar.copy(out=v64(yt[d][PL(l), yi % 2, :]), in_=pv), reads=[bp], writes=[byt[d][yi % 2]], acc=(l > 0))
                grp(lambda l, hh: [(ar(l, hh, 64), St(l, hh), [barT[d][hh]], [bSb[d]]),
                                   (XS[PL(l), 0, hh, 64:128], Us[d][PL(l), hh, :], [bX[0]], [bUs[d]]),
                                   (XS[PL(l), 1, hh, 64:128], V(l, hh), [bX[1]], [bvt[d]])], ev_Y)
                for l in range(2):
                    K.dma(K.pool, self.rwy[d, r0:r0 + 64, :].rearrange("t (hh two i) -> t hh two i", two=2, i=64)[:, :, l, :], v64(yt[d][PL(l), yi % 2, :]),
                          reads=[byt[d][yi % 2]], writes=[self.db["rwy%d" % d]], acc=True, owner=byt[d][yi % 2])
                yield

                def ev_S(l, pv, bp):
                    K.op(K.dve, lambda: nc.vector.tensor_tensor(out=v64(stmp[PL(l), d, :]), in0=pv, in1=S32[d][PL(l)], op=ALU.add), reads=[bp, bS32[d]], writes=[bstmp[d]], acc=(l > 0))
                grp(lambda l, hh: [(btk[d][PL(l), n, 0, (2 * hh + l) * 64:(2 * hh + l + 1) * 64], Us[d][PL(l), hh, :], [bbtk[d][n * 2]], [bUs[d]]),
                                   (btk[d][PL(l), n, 1, (2 * hh + l) * 64:(2 * hh + l + 1) * 64], V(l, hh), [bbtk[d][n * 2 + 1]], [bvt[d]])], ev_S)
                K.op(K.dve, lambda: nc.vector.tensor_tensor(out=S32[d][:], in0=v64(stmp[:, d, :]), in1=et[d][:, :, n:n + 1].broadcast_to([128, 4, 64]), op=ALU.mult),
                     reads=[bstmp[d], bet[d]], writes=[bS32[d]])
                K.op(K.act, lambda: nc.scalar.copy(out=Sb[d][:], in_=S32[d][:]), reads=[bS32[d]], writes=[bSb[d]])
                yield

            def run_rr(gens):
                gens = list(gens)
                while gens:
                    for g in list(gens):
                        try:
                            next(g)
                        except StopIteration:
                            gens.remove(g)

            yi = [0, 0]
            cn = lambda d, i: i if d == 0 else 7 - i
            for step in range(NT):
                tiles = [step, NT - 1 - step]
                for d in R2:
                    prep(d, tiles[d])
                run_rr([pre(d, cn(d, 0), 0) for d in R2])
                for i in range(8):
                    gens = [seq(d, tiles[d], cn(d, i), i % 2, yi[d]) for d in R2]
                    if i + 1 < 8:
                        gens = [g for pair in zip(gens, [pre(d, cn(d, i + 1), (i + 1) % 2) for d in R2]) for g in pair]
                    run_rr(gens)
                    for d in R2:
                        yi[d] += 1

    def p3c(self, slot):
        nc, K = self.nc, self.K
        K.begin_scope("p3c")
        with ExitStack() as es:
            sb = lambda n, sh, dt: self.sbt(es, "p3c_%d_" % slot + n, sh, dt)
            yf = sb("yf", [128, 4, 512], F32); byf = K.buf("yf")
            yb = sb("yb", [128, 4, 512], F32); byb = K.buf("yb")
            sq = sb("sq", [128, 4, 512], F32); bsq = K.buf("sq")
            stt = sb("stt", [128, 2, 32], F32); bstt = K.buf("stt")
            bon = sb("bon", [128, 4, TT], F32); bbon = K.buf("bon")
            gg = sb("gg", [128, 4, TT], F32); bgg = K.buf("gg")
            t1 = sb("t1", [128, 2, TT], F32); bt1 = K.bufs(2, "t1")
            ro = sb("ro", [128, 4, TT], BF16); bro = K.bufs(4, "ro")
            ps = [self.pst(es, "p3c_%d_ps%%d" % slot % i, [128, TT], F32) for i in range(2)]; bps = K.bufs(2, "ps3c")
            g3 = lambda ap: ap.rearrange("p n (h i) -> p (n h) i", i=64)
            for t in range(self.NT):
                t0 = t * TT
                K.dma(K.sp, yf[:], self.rwy[0, t0:t0 + TT].rearrange("(n p) c -> p n c", p=128), reads=[self.db["rwy0"]], writes=[byf])
                K.dma(K.sp, yb[:], self.rwy[1, t0:t0 + TT].rearrange("(n p) c -> p n c", p=128), reads=[self.db["rwy1"]], writes=[byb])
                K.dma(K.sp, bon[:], self.rwbon.rearrange("(c p) s -> p c s", p=128)[:, :, t0:t0 + TT], reads=[self.db["rwbon"]], writes=[bbon])
                K.dma(K.sp, gg[:], self.rwg.rearrange("(c p) s -> p c s", p=128)[:, :, t0:t0 + TT], reads=[self.db["rwg"]], writes=[bgg])
                K.op(K.dve, lambda: nc.vector.tensor_tensor(out=yf[:], in0=yf[:], in1=yb[:], op=ALU.add), reads=[byf, byb], writes=[byf])
                mean, var = stt[:, 0, :], stt[:, 1, :]
                K.op(K.dve, lambda: nc.vector.tensor_reduce(out=mean, in_=g3(yf[:]), axis=AX.X, op=ALU.add), reads=[byf], writes=[bstt])
                K.op(K.dve, lambda: nc.vector.tensor_scalar(out=mean, in0=mean, scalar1=1.0 / 64, scalar2=None, op0=ALU.mult), reads=[bstt], writes=[bstt])
                K.op(K.dve, lambda: nc.vector.tensor_tensor(out=g3(yf[:]), in0=g3(yf[:]), in1=mean.unsqueeze(2).broadcast_to([128, 32, 64]), op=ALU.subtract), reads=[byf, bstt], writes=[byf])
                K.op(K.act, lambda: nc.scalar.activation(out=sq[:], in_=yf[:], func=AF.Square), reads=[byf], writes=[bsq])
                K.op(K.dve, lambda: nc.vector.tensor_reduce(out=var, in_=g3(sq[:]), axis=AX.X, op=ALU.add), reads=[bsq], writes=[bstt], acc=True)
                K.op(K.act, lambda: nc.scalar.activation(out=var, in_=var, func=AF.Ln, scale=1.0 / 64, bias=self.epsc[:, 2:3]), reads=[bstt, self.b_ones], writes=[bstt], acc=True)
                K.op(K.act, lambda: nc.scalar.activation(out=var, in_=var, func=AF.Exp, scale=-0.5), reads=[bstt], writes=[bstt], acc=True)
                K.op(K.dve, lambda: nc.vector.tensor_tensor(out=g3(yf[:]), in0=g3(yf[:]), in1=var.unsqueeze(2).broadcast_to([128, 32, 64]), op=ALU.mult), reads=[byf, bstt], writes=[byf])
                for c in range(4):
                    p, bp = ps[c % 2], bps[c % 2]
                    for n in range(4):
                        K.op(K.pe, lambda: nc.tensor.transpose(out=p[:, n * 128:(n + 1) * 128], in_=yf[:, n, c * 128:(c + 1) * 128], identity=self.idf[:]),
                             reads=[byf, self.b_id], writes=[bp], inc=(n == 3), acc=(n > 0))
                    tt_, btt = t1[:, c % 2, :], bt1[c % 2]
                    K.op(K.act, lambda: nc.scalar.activation(out=tt_, in_=p[:], func=AF.Identity, scale=self.col("lnx_g", c), bias=self.col("lnx_b", c)), reads=[bp, self.b_pc], writes=[btt])
                    K.op(K.dve, lambda: nc.vector.tensor_tensor(out=tt_, in0=tt_, in1=bon[:, c, :], op=ALU.add), reads=[btt, bbon], writes=[btt])
                    K.op(K.pool, lambda: nc.gpsimd.tensor_tensor(out=ro[:, c, :], in0=tt_, in1=gg[:, c, :], op=ALU.mult), reads=[btt, bgg], writes=[bro[c]])
                K.dma(K.pool, self.rwo.rearrange("(c p) s -> p c s", p=128)[:, :, t0:t0 + TT], ro[:], reads=bro, writes=[self.db["rwo"]], acc=True, owner=bro[0])

    def p4m(self, slot):
        nc, K = self.nc, self.K
        K.begin_scope("p4m")
        with ExitStack() as es:
            sb = lambda n, sh, dt: self.sbt(es, "p4m_%d_" % slot + n, sh, dt)
            m = sb("m", [128, 2, D], F32); bm = K.buf("m")
            sq = sb("sq", [128, 2, D], F32); bsq = K.buf("sq")
            gb = sb("gb", [128, 2, D], F32); bgb = K.buf("gb")
            stt = sb("stt", [128, 2, 2], F32); bstt = K.buf("stt")
            mT = sb("mT", [128, 8, 256], BF16); bmT = K.bufs(8, "mT")
            wck = sb("wck", [128, 8, 1024], BF16); bwck = K.buf("wck")
            km = sb("km", [128, 4, 256], BF16); bkm = K.bufs(4, "km")
            vm = sb("vm", [128, 2, 516], BF16); bvm = K.buf("vm")
            ps = [self.pst(es, "p4m_%d_ps%%d" % slot % i, [128, TT], F32) for i in range(2)]; bps = K.bufs(2, "ps4m")
            K.dma(K.sp, m[:], self.mem[slot].rearrange("(n p) d -> p n d", p=128), writes=[bm])
            K.dma(K.sp, gb[:, 0, :], self.memgb[0:1, :].partition_broadcast(128), writes=[bgb])
            K.dma(K.sp, gb[:, 1, :], self.memgb[1:2, :].partition_broadcast(128), writes=[bgb], acc=True)
            K.dma(K.sp, wck[:], self.ws["wckv"], reads=[self.db["wckv"]], writes=[bwck])
            K.op(K.pool, lambda: nc.gpsimd.memset(vm[:], 1.0), writes=[bvm])
            mean, var = stt[:, 0, :], stt[:, 1, :]
            K.op(K.dve, lambda: nc.vector.tensor_reduce(out=mean, in_=m[:], axis=AX.X, op=ALU.add), reads=[bm], writes=[bstt])
            K.op(K.dve, lambda: nc.vector.tensor_scalar(out=mean, in0=mean, scalar1=1.0 / D, scalar2=None, op0=ALU.mult), reads=[bstt], writes=[bstt])
            K.op(K.dve, lambda: nc.vector.tensor_tensor(out=m[:], in0=m[:], in1=mean.unsqueeze(2).broadcast_to([128, 2, D]), op=ALU.subtract), reads=[bm, bstt], writes=[bm])
            K.op(K.act, lambda: nc.scalar.activation(out=sq[:], in_=m[:], func=AF.Square), reads=[bm], writes=[bsq])
            K.op(K.dve, lambda: nc.vector.tensor_reduce(out=var, in_=sq[:], axis=AX.X, op=ALU.add), reads=[bsq], writes=[bstt], acc=True)
            K.op(K.act, lambda: nc.scalar.activation(out=var, in_=var, func=AF.Ln, scale=1.0 / D, bias=self.epsc[:, 0:1]), reads=[bstt, self.b_ones], writes=[bstt], acc=True)
            K.op(K.act, lambda: nc.scalar.activation(out=var, in_=var, func=AF.Exp, scale=-0.5), reads=[bstt], writes=[bstt], acc=True)
            K.op(K.dve, lambda: nc.vector.tensor_tensor(out=m[:], in0=m[:], in1=var.unsqueeze(2).broadcast_to([128, 2, D]), op=ALU.mult), reads=[bm, bstt], writes=[bm])
            K.op(K.dve, lambda: nc.vector.tensor_tensor(out=m[:], in0=m[:], in1=gb[:, 0:1, :].broadcast_to([128, 2, D]), op=ALU.mult), reads=[bm, bgb], writes=[bm])
            K.op(K.dve, lambda: nc.vector.tensor_tensor(out=m[:], in0=m[:], in1=gb[:, 1:2, :].broadcast_to([128, 2, D]), op=ALU.add), reads=[bm, bgb], writes=[bm])
            for kc in range(8):
                p, bp = ps[kc % 2], bps[kc % 2]
                for n in range(2):
                    K.op(K.pe, lambda: nc.tensor.transpose(out=p[:, n * 128:(n + 1) * 128], in_=m[:, n, kc * 128:(kc + 1) * 128], identity=self.idf[:]),
                         reads=[bm, self.b_id], writes=[bp], inc=(n == 1), acc=(n > 0))
                K.op(K.act, lambda: nc.scalar.copy(out=mT[:, kc, :], in_=p[:, 0:256]), reads=[bp], writes=[bmT[kc]])
            for hc in range(4):
                p, bp = ps[hc % 2], bps[hc % 2]
                for kc in range(8):
                    K.op(K.pe, lambda: nc.tensor.matmul(p[:, 0:256], lhsT=wck[:, kc, hc * 128:(hc + 1) * 128], rhs=mT[:, kc, :], start=(kc == 0), stop=(kc == 7)),
                         reads=[bwck, bmT[kc]], writes=[bp], inc=(kc == 7), acc=(kc > 0))
                K.op(K.act, lambda: nc.scalar.copy(out=km[:, hc, :], in_=p[:, 0:256]), reads=[bp], writes=[bkm[hc]])
            for mt in range(2):
                p, bp = ps[mt % 2], bps[mt % 2]
                for kc in range(8):
                    K.op(K.pe, lambda: nc.tensor.matmul(p[:], lhsT=mT[:, kc, mt * 128:(mt + 1) * 128], rhs=wck[:, kc, 512:1024], start=(kc == 0), stop=(kc == 7)),
                         reads=[bwck, bmT[kc]], writes=[bp], inc=(kc == 7), acc=(kc > 0))
                K.op(K.dve, lambda: nc.vector.tensor_copy(out=vm[:, mt, :].rearrange("p (h d) -> p h d", d=129)[:, :, 0:128], in_=p[:].rearrange("p (h d) -> p h d", d=128)),
                     reads=[bp], writes=[bvm], acc=True)
            K.dma(K.pool, self.kmT, km[:], reads=bkm, writes=[self.db["kmT"]], owner=bkm[0])
            K.dma(K.pool, self.vms, vm[:], reads=[bvm], writes=[self.db["vms"]], owner=bvm)

    def p4(self, slot):
        nc, K = self.nc, self.K
        K.begin_scope("p4")
        scale_c = float(128 ** -0.5)
        with ExitStack() as es:
            sb = lambda n, sh, dt: self.sbt(es, "p4_%d_" % slot + n, sh, dt)
            xr = sb("xr", [128, 8, TT], F32); bxr = K.bufs(8, "xr")
            z = sb("z", [128, 8, TT], F32); bz = K.bufs(8, "z")
            xb = sb("xb", [128, 8, TT], BF16); bxb = K.bufs(8, "xb")
            st = sb("st", [128, TT], F32); bst = K.buf("st")
            st2 = sb("st2", [128, TT], F32); bst2 = K.buf("st2")
            ytok = sb("ytok", [128, 4, D], F32); bsq = K.bufs(8, "ytok_sq")
            sqf = lambda c: ytok[:, c // 2, (c % 2) * TT:(c % 2 + 1) * TT]
            hT = sb("hT", [128, NFC, TT], BF16); bhT = K.bufs(NFC, "hT")
            wgu = [sb("wgu%d" % i, [128, 8, 512], BF16) for i in range(2)]; bwgu = K.bufs(2, "wgu")
            wdn = [sb("wdn%d" % i, [128, NFC, 128], BF16) for i in range(2)]; bwdn = K.bufs(2, "wdn")
            OTt = sb("OTt", [128, 4, TT], BF16); bOT = K.buf("OTt")
            RWt = sb("RWt", [128, 4, TT], BF16); bRW = K.buf("RWt")
            gts = sb("gts", [128, 2, 2, TT], F32); bgts = K.bufs(2, "gts")
            tmpf = sb("tmpf", [128, 2, TT], F32); btmp = K.bufs(2, "tmpf")
            qc = sb("qc", [128, 4, TT], BF16); bqc = K.bufs(4, "qc")
            km = sb("km", [128, 4, 256], BF16); bkm = K.buf("km")
            vm = sb("vm", [128, 2, 516], BF16); bvm = K.buf("vm")
            PTc = [sb("PTc%d" % i, [128, TT], BF16) for i in range(2)]; bPTc = K.bufs(2, "PTc")
            oct_ = sb("oct", [128, 4, 512], BF16); boct = K.bufs(4, "oct")
            ocT = sb("ocT", [128, 4, TT], BF16); bocT = K.bufs(4, "ocT")
            rs = sb("rs", [128, 4], F32); brs = K.buf("rs")
            psA = [self.pst(es, "p4_%d_ps%%d" % slot % i, [128, TT], F32) for i in range(4)]; bpsA = K.bufs(4, "psA4")
            psS = self.pst(es, "p4_%d_pss" % slot, [128, TT], F32); bpsS = K.buf("psS4")
            psO = [self.pst(es, "p4_%d_pso%%d" % slot % i, [128, TT], F32) for i in range(2)]; bpsO = K.bufs(2, "psO4")
            psT = self.pst(es, "p4_%d_pst" % slot, [128, 2 * TT], BF16); bpsT = K.buf("psT4")
            K.dma(K.sp, km[:], self.kmT, reads=[self.db["kmT"]], writes=[bkm])
            K.dma(K.sp, vm[:], self.vms, reads=[self.db["vms"]], writes=[bvm])
            w4 = lambda wt: wt[:].rearrange("p a b -> p (a b)").rearrange("p (k c) -> p k c", k=4)
            pi = [0]

            def nps():
                i = pi[0] % 4
                pi[0] += 1
                return psA[i], bpsA[i]

            def ld_br(tt):
                K.dma(K.sp, OTt[:], self.OT.rearrange("(c p) s -> p c s", p=128)[:, :, tt * TT:(tt + 1) * TT], reads=[self.db["OT"]], writes=[bOT])
                K.dma(K.sp, RWt[:], self.rwo.rearrange("(c p) s -> p c s", p=128)[:, :, tt * TT:(tt + 1) * TT], reads=[self.db["rwo"]], writes=[bRW])

            ld_br(0)
            for t in range(self.NT):
                t0 = t * TT
                K.dma(K.sp, xr[:], self.x1T.rearrange("(c p) s -> p c s", p=128)[:, :, t0:t0 + TT], reads=[self.db["x1T"]], writes=bxr)
                K.dma(K.sp, wgu[0][:], self.ws["pmla"].rearrange("p k (a b) -> p (k a) b", a=2), reads=[self.db["pmla"]], writes=[bwgu[0]])
                K.dma(K.sp, wgu[1][:], self.ws["prwkv"].rearrange("p k (a b) -> p (k a) b", a=2), reads=[self.db["prwkv"]], writes=[bwgu[1]])
                WA, WB = w4(wgu[0]), w4(wgu[1])
                for c in range(8):
                    g_, bg = gts[:, c % 2], bgts[c % 2]
                    K.dma(K.sp, g_[:, 0, :], self.gates[c * 128:(c + 1) * 128, t0:t0 + TT], reads=[self.db["gates"]], writes=[bg])
                    K.dma(K.sp, g_[:, 1, :], self.gates[1024 + c * 128:1024 + (c + 1) * 128, t0:t0 + TT], reads=[self.db["gates"]], writes=[bg], acc=True)
                    pa, bpa = nps()
                    pb, bpb = nps()
                    for k in range(4):
                        K.op(K.pe, lambda: nc.tensor.matmul(pa[:], lhsT=WA[:, k, c * 128:(c + 1) * 128], rhs=OTt[:, k, :], start=(k == 0), stop=(k == 3)),
                             reads=[bwgu[0], bOT], writes=[bpa], inc=(k == 3), acc=(k > 0))
                    for k in range(4):
                        K.op(K.pe, lambda: nc.tensor.matmul(pb[:], lhsT=WB[:, k, c * 128:(c + 1) * 128], rhs=RWt[:, k, :], start=(k == 0), stop=(k == 3)),
                             reads=[bwgu[1], bRW], writes=[bpb], inc=(k == 3), acc=(k > 0))
                    K.op(K.dve, lambda: nc.vector.tensor_tensor(out=tmpf[:, 0, :], in0=pa[:], in1=g_[:, 0, :], op=ALU.mult), reads=[bpa, bg], writes=[btmp[0]])
                    K.op(K.dve, lambda: nc.vector.tensor_tensor(out=tmpf[:, 1, :], in0=pb[:], in1=g_[:, 1, :], op=ALU.mult), reads=[bpb, bg], writes=[btmp[1]])
                    K.op(K.pool, lambda: nc.gpsimd.tensor_tensor(out=xb[:, c, :], in0=tmpf[:, 0, :], in1=tmpf[:, 1, :], op=ALU.add), reads=btmp, writes=[bxb[c]])
                if t + 1 < self.NT:
                    ld_br(t + 1)
                for half in range(2):
                    K.dma(K.sp, wgu[half][:], self.ws["wo"][:, :, half * 512:(half + 1) * 512], reads=[self.db["wo"]], writes=[bwgu[half]])
                for c in range(8):
                    wt, bwt = wgu[c // 4], bwgu[c // 4]
                    pz, bpz = nps()
                    for k in range(8):
                        K.op(K.pe, lambda: nc.tensor.matmul(pz[:], lhsT=wt[:, k, (c % 4) * 128:(c % 4 + 1) * 128], rhs=xb[:, k, :], start=(k == 0), stop=(k == 7)),
                             reads=[bwt, bxb[k]], writes=[bpz], inc=(k == 7), acc=(k > 0))
                    K.op(K.dve, lambda: nc.vector.scalar_tensor_tensor(out=z[:, c, :], in0=xr[:, c, :], scalar=ALPHA, in1=pz[:], op0=ALU.mult, op1=ALU.add),
                         reads=[bpz, bxr[c]], writes=[bz[c]])
                self.ln_fm(z, bz, "ln2_g", "ln2_b", xr, bxr, xb, bxb, psS, bpsS, sqf, bsq, st, bst)
                K.dma(K.sp, wgu[0][:], self.ws["wcq"], reads=[self.db["wcq"]], writes=[bwgu[0]])
                K.dma(K.sp, wgu[1][:], self.ws["wco"].rearrange("p k (a b) -> p (k a) b", a=2), reads=[self.db["wco"]], writes=[bwgu[1]])
                for hc in range(4):
                    p, bp = nps()
                    for k in range(8):
                        K.op(K.pe, lambda: nc.tensor.matmul(p[:], lhsT=wgu[0][:, k, hc * 128:(hc + 1) * 128], rhs=xb[:, k, :], start=(k == 0), stop=(k == 7)),
                             reads=[bwgu[0], bxb[k]], writes=[bp], inc=(k == 7), acc=(k > 0))
                    K.op(K.act, lambda: nc.scalar.copy(out=qc[:, hc, :], in_=p[:]), reads=[bp], writes=[bqc[hc]])
                si = 0
                for hc in range(4):
                    for mt in range(2):
                        pS_, bpS_ = nps()
                        P_, bP = PTc[si % 2], bPTc[si % 2]
                        si += 1
                        K.op(K.pe, lambda: nc.tensor.matmul(pS_[:], lhsT=km[:, hc, mt * 128:(mt + 1) * 128], rhs=qc[:, hc, :], start=True, stop=True), reads=[bkm, bqc[hc]], writes=[bpS_])
                        K.op(K.act, lambda: nc.scalar.activation(out=P_[:], in_=pS_[:], func=AF.Exp, scale=scale_c), reads=[bpS_], writes=[bP])
                        for qs in range(4):
                            po, bpo = psO[qs // 2], bpsO[qs // 2]
                            K.op(K.pe, lambda: nc.tensor.matmul(po[:, (qs % 2) * 129:(qs % 2 + 1) * 129], lhsT=P_[:, qs * 128:(qs + 1) * 128], rhs=vm[:, mt, hc * 129:(hc + 1) * 129],
                                                                 start=(mt == 0 and qs % 2 == 0), stop=(mt == 1 and qs % 2 == 1)),
                                 reads=[bP, bvm], writes=[bpo], inc=(qs % 2 == 1), acc=not (mt == 0 and qs % 2 == 0))
                    for qs in range(4):
                        po, bpo = psO[qs // 2], bpsO[qs // 2]
                        o0 = (qs % 2) * 129
                        K.op(K.dve, lambda: nc.vector.reciprocal(out=rs[:, qs:qs + 1], in_=po[:, o0 + 128:o0 + 129]), reads=[bpo], writes=[brs], acc=(qs > 0))
                        K.op(K.dve, lambda: nc.vector.tensor_scalar(out=oct_[:, qs, hc * 128:(hc + 1) * 128], in0=po[:, o0:o0 + 128], scalar1=rs[:, qs:qs + 1], scalar2=None, op0=ALU.mult),
                             reads=[bpo, brs], writes=[boct[hc]], acc=(qs > 0))
                for hc in range(4):
                    for qs in range(4):
                        K.op(K.pe, lambda: nc.tensor.transpose(out=psT[:, qs * 128:(qs + 1) * 128], in_=oct_[:, qs, hc * 128:(hc + 1) * 128], identity=self.idb[:]),
                             reads=[boct[hc], self.b_id], writes=[bpsT], inc=(qs == 3), acc=(qs > 0))
                    K.op(K.dve, lambda: nc.vector.tensor_copy(out=ocT[:, hc, :], in_=psT[:, 0:TT]), reads=[bpsT], writes=[bocT[hc]])
                WC = w4(wgu[1])
                for c in range(8):
                    pz, bpz = nps()
                    for k in range(4):
                        K.op(K.pe, lambda: nc.tensor.matmul(pz[:], lhsT=WC[:, k, c * 128:(c + 1) * 128], rhs=ocT[:, k, :], start=(k == 0), stop=(k == 3)),
                             reads=[bwgu[1], bocT[k]], writes=[bpz], inc=(k == 3), acc=(k > 0))
                    K.op(K.dve, lambda: nc.vector.scalar_tensor_tensor(out=z[:, c, :], in0=xr[:, c, :], scalar=ALPHA, in1=pz[:], op0=ALU.mult, op1=ALU.add),
                         reads=[bpz, bxr[c]], writes=[bz[c]])
                self.ln_fm(z, bz, "ln3_g", "ln3_b", xr, bxr, xb, bxb, psS, bpsS, sqf, bsq, st, bst)
                for c in range(8):
                    K.op(K.pool, lambda: nc.gpsimd.tensor_scalar(out=xr[:, c, :], in0=xr[:, c, :], scalar1=ALPHA, scalar2=None, op0=ALU.mult), reads=[bxr[c]], writes=[bxr[c]])
                self.ffn(1, xb, bxb, xr, bxr, hT, bhT, wgu, bwgu, wdn, bwdn, psA, bpsA, z, bz)
                self.ln_fm(z, bz, "ln4_g", "ln4_b", xr, bxr, None, None, psS, bpsS, sqf, bsq, st, bst)
                for n in range(4):
                    for hf in range(2):
                        p, bp = nps()
                        for cc in range(4):
                            c = hf * 4 + cc
                            K.op(K.pe, lambda: nc.tensor.transpose(out=p[:, cc * 128:(cc + 1) * 128], in_=xr[:, c, n * 128:(n + 1) * 128], identity=self.idf[:]),
                                 reads=[bxr[c], self.b_id], writes=[bp], inc=(cc == 3), acc=(cc > 0))
                        if hf == 0:
                            K.op(K.act, lambda: nc.scalar.copy(out=ytok[:, n, 0:512], in_=p[:]), reads=[bp], writes=[bsq[2 * n]])
                        else:
                            K.op(K.dve, lambda: nc.vector.tensor_copy(out=ytok[:, n, 512:1024], in_=p[:]), reads=[bp], writes=[bsq[2 * n + 1]])
                K.dma(K.pool, self.y[slot, t0:t0 + TT, :].rearrange("(n p) d -> p n d", p=128), ytok[:], reads=bsq, writes=[self.db["y"]], acc=True, owner=bsq[0])

def _prep_consts(S):
    inv = 1.0 / (10000.0 ** (np.arange(0, 32, 2, dtype=np.float32) / 32.0))
    ang = np.arange(S, dtype=np.float32)[:, None] * inv[None, :].astype(np.float32)
    c, s = np.cos(ang).astype(np.float32), np.sin(ang).astype(np.float32)
    ropec = np.concatenate([c, c], 1).T.copy()
    ropes = np.concatenate([-s, s], 1).T.copy()
    return ropec, ropes


def _cst2():
    c = np.zeros((128, 128 + 512 + 384 + 64), np.float32)
    c[0:64, 0:64] = 1.0
    c[64:128, 64:128] = 1.0
    cm = np.ones(512, np.float32)
    cm[::64] = 0.0
    c[:, 128:640] = cm[None, :]
    j = np.arange(64)[:, None]
    t = np.arange(64)[None, :]
    for d in range(2):
        strict = (j < t) if d == 0 else (j > t)
        incl = (j <= t) if d == 0 else (j >= t)
        base = 640 + d * 192
        for l0 in (0, 64):
            c[l0:l0 + 64, base:base + 64] = strict
            c[l0:l0 + 64, base + 64:base + 128] = incl
            c[l0:l0 + 64, base + 128:base + 192] = strict.T
    c[0:64, 1024:1088] = np.eye(64)
    c[64:128, 1024:1088] = np.eye(64)
    return c


def make_in_map(p, x, mem, S):
    ropec, ropes = _prep_consts(S)
    im = {"x": np.ascontiguousarray(x, np.float32), "mem": np.ascontiguousarray(mem, np.float32),
          "pcols": _pack_cols(p), "ident": np.eye(128, dtype=np.float32), "cst2": _cst2(), "ropec": ropec, "ropes": ropes,
          "memgb": np.stack([p["mem_g"][0], p["mem_b"][0]]).astype(np.float32)}
    im.update(_layout_weights(p))
    return im


_SEQ_MAP = None


def _slot_map():
    m = []
    seqs = [("p", i) for i in range(16)] + [("s", i) for i in range(4)]
    k = 0
    for c in range(NCORE):
        n = 3 if c < 4 else 2
        sl = seqs[k:k + n]
        k += n
        while len(sl) < NSLOT:
            sl = sl + [sl[-1]]
        m.append(sl)
    return m


def kernel(**inputs):
    S = 4096
    p = {k: np.asarray(v) for k, v in inputs.items() if k not in ("x_prompt", "x_sample", "mem_prompt", "mem_sample")}
    xs = {"p": np.asarray(inputs["x_prompt"], np.float32), "s": np.asarray(inputs["x_sample"], np.float32)}
    ms = {"p": np.asarray(inputs["mem_prompt"], np.float32), "s": np.asarray(inputs["mem_sample"], np.float32)}
    smap = _slot_map()
    B = Builder(S, NSLOT, debug=False)
    nc = B.build()
    shared = make_in_map(p, np.zeros((0,), np.float32), np.zeros((0,), np.float32), S)
    in_maps = []
    for c in range(NCORE):
        im = dict(shared)
        im["x"] = np.ascontiguousarray(np.stack([xs[g][i] for g, i in smap[c]]))
        im["mem"] = np.ascontiguousarray(np.stack([ms[g][i] for g, i in smap[c]]))
        in_maps.append(im)
    res = run_bass_kernel_spmd(nc, in_maps, core_ids=list(range(NCORE)))
    yp = np.zeros_like(xs["p"])
    ysm = np.zeros_like(xs["s"])
    done = set()
    for c in range(NCORE):
        y = res.results[c]["y"]
        for sl, (g, i) in enumerate(smap[c]):
            if (g, i) in done:
                continue
            done.add((g, i))
            (yp if g == "p" else ysm)[i] = y[sl]
    return (yp, ysm)
```

```python
import os
import numpy as np
import concourse.bass as bass
import concourse.mybir as mybir
from concourse.bass_utils import run_bass_kernel_spmd
from contextlib import ExitStack

F32 = mybir.dt.float32
BF16 = mybir.dt.bfloat16
AF = mybir.ActivationFunctionType
ALU = mybir.AluOpType
AX = mybir.AxisListType

D = 1024
DFF = 2816
NFC = 22
TT = 512
H = 8
ALPHA = float(2 ** 0.25)
LN_EPS = 1e-5
RMS_EPS = 1e-6
GN_EPS = 64e-5
WCONST = float(np.exp(-0.5))
OFF_KV = 384
OFF_RWKV = 672
OFF_GATE = 2528
NCORE = 8
NSLOT = 3


class Ev:
    __slots__ = ("sem", "val", "key")

    def __init__(self, sem, key, val=None):
        self.sem = sem
        self.key = key
        self.val = val


class Buf:
    __slots__ = ("name", "w", "r", "dsem", "dkey", "dcount", "psum")

    def __init__(self, name):
        self.name = name
        self.psum = name.startswith("ps")
        self.w = {}
        self.r = {}
        self.dsem = None
        self.dkey = None
        self.dcount = 0


class Iss:
    def __init__(self, K, name, eng):
        self.name = name
        self.eng = eng
        self.sem = K.new_sem("e_" + name)
        self.key = "e_" + name
        self.count = 0
        self.seen = {}
        self.cur = Ev(self.sem, self.key)
        self.ninstr = 0


class Kern:
    def __init__(self, nc, es):
        self.nc = nc
        self.es = es
        self.nsem = 0
        self.pe = Iss(self, "pe", nc.tensor)
        self.act = Iss(self, "act", nc.scalar)
        self.dve = Iss(self, "dve", nc.vector)
        self.pool = Iss(self, "pool", nc.gpsimd)
        self.sp = Iss(self, "sp", nc.sync)
        self.all = [self.pe, self.act, self.dve, self.pool, self.sp]
        self.dbufs = []
        self.nbuf = 0

    def new_sem(self, name):
        self.nsem += 1
        return self.es.enter_context(self.nc.semaphore(name))

    def begin_scope(self, scope):
        self.scope = scope
        self.scnt = {}

    def buf(self, name=None):
        name = name or "b"
        scope = getattr(self, "scope", None)
        if scope is None:
            self.nbuf += 1
            return Buf("%s_%d" % (name, self.nbuf))
        k = self.scnt.get(name, 0)
        self.scnt[name] = k + 1
        key = (scope, name, k)
        cache = self.__dict__.setdefault("bcache", {})
        if key not in cache:
            self.nbuf += 1
            cache[key] = Buf("%s_%d" % (name, self.nbuf))
        return cache[key]

    def bufs(self, n, name="b"):
        return [self.buf("%s%d" % (name, i)) for i in range(n)]

    def _wait(self, iss, ev):
        assert ev.val is not None, "waiting on unresolved event (%s)" % ev.key
        if iss.seen.get(ev.key, 0) >= ev.val:
            return
        iss.eng.wait_ge(ev.sem, ev.val)
        iss.seen[ev.key] = ev.val
        iss.ninstr += 1

    def _need(self, iss, ev, out):
        assert ev.val is not None, "waiting on unresolved event (%s)" % ev.key
        if iss.seen.get(ev.key, 0) >= ev.val:
            return
        iss.seen[ev.key] = ev.val
        for i, o in enumerate(out):
            if o.key == ev.key:
                if o.val < ev.val:
                    out[i] = ev
                return
        out.append(ev)

    def _deps(self, iss, reads, writes, acc, inline=False):
        out = []
        for b in reads:
            for ev in b.w.values():
                self._need(iss, ev, out)
            if b.psum:
                for ev in b.r.values():
                    if ev.key != iss.key:
                        self._need(iss, ev, out)
        for b in writes:
            for ev in b.r.values():
                self._need(iss, ev, out)
            if not acc:
                for ev in b.w.values():
                    self._need(iss, ev, out)
        last = out.pop() if (inline and out) else None
        for ev in out:
            iss.eng.wait_ge(ev.sem, ev.val)
            iss.ninstr += 1
        return last

    def op(self, iss, fn, reads=(), writes=(), inc=True, acc=False, lhs=None):
        last = self._deps(iss, reads, writes, acc, inline=True)
        ins = fn()
        if last is not None:
            ins.wait_op(last.sem, last.val, "sem-ge")
        iss.ninstr += 1
        ev = iss.cur
        if inc:
            iss.count += 1
            ins.then_inc(iss.sem, 1)
            ev.val = iss.count
            iss.cur = Ev(iss.sem, iss.key)
        for b in reads:
            b.r[ev.key] = ev
        for b in writes:
            if not acc:
                b.w = {}
                b.r = {}
            b.w[ev.key] = ev
        return ins

    def dma(self, iss, out, in_, reads=(), writes=(), acc=False, owner=None, **kw):
        self._deps(iss, reads, writes, acc)
        b0 = owner if owner is not None else writes[0]
        if b0.dsem is None:
            b0.dkey = "d_%s" % b0.name
            b0.dsem = self.new_sem(b0.dkey)
            self.dbufs.append(b0)
        ins = iss.eng.dma_start(out=out, in_=in_, **kw)
        iss.ninstr += 1
        b0.dcount += 16
        ins.then_inc(b0.dsem, 16)
        ev = Ev(b0.dsem, b0.dkey, b0.dcount)
        for b in reads:
            b.r[ev.key] = ev
        for b in writes:
            if not acc:
                b.w = {}
                b.r = {}
            b.w[ev.key] = ev
        return ins

    def barrier(self):
        for iss in self.all:
            for o in self.all:
                if o is not iss and o.count > 0:
                    self._wait(iss, Ev(o.sem, o.key, o.count))
            for b in self.dbufs:
                if b.dcount > 0:
                    self._wait(iss, Ev(b.dsem, b.dkey, b.dcount))


def _col_layout():
    off = {}
    n = 0
    for name, c in [("ln1_g", 8), ("ln1_b", 8), ("ln2_g", 8), ("ln2_b", 8), ("ln3_g", 8), ("ln3_b", 8),
                    ("ln4_g", 8), ("ln4_b", 8), ("b_gate", 16), ("q_norm_g", 3), ("kv_norm_g", 2),
                    ("mu_prev", 15), ("mu_next", 15), ("w0", 8), ("a0", 4), ("k_k", 4), ("k_a", 4), ("r_k", 4),
                    ("lnx_g", 4), ("lnx_b", 4)]:
        off[name] = (n, c)
        n += c
    return off, n


COLS, NCOL = _col_layout()
RW_CH = [(i * 128, 128) for i in range(12)] + [(1536, 128), (1664, 64), (1728, 128)]


def _pack_cols(p):
    a = np.zeros((128, NCOL), np.float32)

    def put(name, vec):
        o, c = COLS[name]
        v = np.asarray(vec, np.float32).reshape(-1)
        a[:, o:o + c] = v.reshape(c, 128).T

    for nm in ["ln1_g", "ln1_b", "ln2_g", "ln2_b", "ln3_g", "ln3_b", "ln4_g", "ln4_b", "b_gate", "q_norm_g",
               "kv_norm_g", "a0", "k_k", "k_a", "r_k", "lnx_g", "lnx_b", "w0"]:
        put(nm, p[nm][0])
    for nm in ["mu_prev", "mu_next"]:
        o, c = COLS[nm]
        v = np.asarray(p[nm][0], np.float32)
        for j, (ro, wd) in enumerate(RW_CH):
            a[:wd, o + j] = v[ro:ro + wd]
    return a


WL = {"wgu0": [11, 128, 8, 512], "wgu1": [11, 128, 8, 512], "wd0": [8, 128, NFC, 128], "wd1": [8, 128, NFC, 128],
      "win": [11, 128, 8, 512], "wuq": [128, 3, 1536], "wukv": [128, 2, 1024], "pmla": [128, 4, 1024],
      "prwkv": [128, 4, 1024], "wo": [128, 8, 1024], "wcq": [128, 8, 512], "wckv": [128, 8, 1024],
      "wco": [128, 4, 1024], "wup": [128, 1, 512], "aup": [64, 1, 512], "gup": [128, 1, 512]}


def _layout_weights(p):
    g = lambda n: np.asarray(p[n][0], np.float32)
    kp = lambda a: a.reshape(a.shape[0] // 128, 128, a.shape[1]).transpose(1, 0, 2)
    o = {}
    for i, pre in enumerate(["ffn1", "ffn2"]):
        gu = kp(g(pre + "_wgu"))
        blk = np.zeros((11, 128, 8, 512), np.float32)
        for j in range(11):
            blk[j, :, :, 0:256] = gu[:, :, j * 256:(j + 1) * 256]
            blk[j, :, :, 256:512] = gu[:, :, DFF + j * 256:DFF + (j + 1) * 256]
        o["wgu%d" % i] = blk
        wd = kp(g(pre + "_wd"))
        o["wd%d" % i] = np.stack([wd[:, :, c * 128:(c + 1) * 128] for c in range(8)])
    win = kp(g("w_in"))
    blk = np.zeros((11, 128, 8, 512), np.float32)
    blk[0, :, :, 0:384] = win[:, :, 0:384]
    blk[1, :, :, 0:256] = win[:, :, 384:640]
    blk[2, :, :, 64:96] = win[:, :, 640:672]
    blk[2, :, :, 160:176] = win[:, :, 656:672]
    blk[2, :, :, 176:192] = win[:, :, 640:656]
    for j in range(3):
        blk[3 + j] = win[:, :, OFF_RWKV + j * 512:OFF_RWKV + (j + 1) * 512]
    blk[6, :, :, 0:320] = win[:, :, OFF_RWKV + 1536:OFF_RWKV + 1856]
    for j in range(4):
        blk[7 + j] = win[:, :, OFF_GATE + j * 512:OFF_GATE + (j + 1) * 512]
    o["win"] = blk
    uq = kp(g("w_uq")).reshape(128, 3, 8, 96)
    uq2 = np.zeros((128, 3, 2, 8, 96), np.float32)
    uq2[:, :, 0] = uq
    uq2[:, :, 1, :, 64:80] = uq[:, :, :, 80:96]
    uq2[:, :, 1, :, 80:96] = uq[:, :, :, 64:80]
    o["wuq"] = uq2.reshape(128, 3, 1536)
    ukv = kp(g("w_ukv")).reshape(128, 2, 8, 128)
    o["wukv"] = np.concatenate([ukv[..., 0:64].reshape(128, 2, 512), ukv[..., 64:128].reshape(128, 2, 512)], -1)
    o["pmla"] = kp(g("p_mla"))
    o["prwkv"] = kp(g("p_rwkv"))
    o["wo"] = kp(g("w_o"))
    o["wcq"] = kp(g("w_cq"))
    o["wckv"] = kp(g("w_ckv"))
    o["wco"] = kp(g("w_co"))
    o["wup"] = g("w_up").reshape(128, 1, 512)
    o["aup"] = g("a_up").reshape(64, 1, 512)
    o["gup"] = g("g_up").reshape(128, 1, 512)
    return {k + "_f": np.ascontiguousarray(v) for k, v in o.items()}


class Builder:
    def __init__(self, S, nslot, debug=False):
        self.S = S
        self.nslot = nslot
        self.NT = S // TT
        self.debug = debug

    def declare(self, nc):
        S, ns = self.S, self.nslot
        I = lambda n, sh, dt=F32: nc.dram_tensor(n, sh, dt, kind="ExternalInput").ap()
        self.x = I("x", [ns, S, D])
        self.mem = I("mem", [ns, 256, D])
        self.wl = {n: I(n + "_f", sh) for n, sh in WL.items()}
        self.pcols = I("pcols", [128, NCOL])
        self.ident = I("ident", [128, 128])
        self.ropec = I("ropec", [32, S])
        self.ropes = I("ropes", [32, S])
        self.memgb = I("memgb", [2, D])
        self.cst2 = I("cst2", [128, 128 + 512 + 384 + 64])
        self.y = nc.dram_tensor("y", [ns, S, D], F32, kind="ExternalOutput").ap()
        kind = "ExternalOutput" if self.debug else "Internal"
        Sc = lambda n, sh, dt=F32: nc.dram_tensor(n, sh, dt, kind=kind).ap()
        self.Sc = Sc
        self.ws = {n: Sc(n + "_s", sh, BF16) for n, sh in WL.items()}
        self.wgu_s = [self.ws["wgu0"], self.ws["wgu1"]]
        self.wd_s = [self.ws["wd0"], self.ws["wd1"]]
        self.win_s = self.ws["win"]
        self.x1T = Sc("x1T", [D, S])
        self.qT = Sc("qT", [8, 96, S], BF16)
        self.kT = Sc("kT", [8, 96, S], BF16)
        self.vtok = Sc("vtok", [S, 8, 65], BF16)
        self.gates = Sc("gatesT", [2048, S])
        self.hrT = Sc("hrT", [1856, S])
        self.OT = Sc("OT", [512, S], BF16)
        self.rw4 = Sc("rw4", [4, 512, S])
        self.rwlw = Sc("rwlw", [2, 512, S])
        self.rwg = Sc("rwg", [512, S])
        self.rwbon = Sc("rwbon", [512, S])
        self.rwv = Sc("rwv", [S, 512], BF16)
        self.rwy = Sc("rwy", [2, S, 512])
        self.kmT = Sc("kmT_s", [128, 4, 256], BF16)
        self.vms = Sc("vm_s", [128, 2, 516], BF16)
        self.rwo = Sc("rwoT", [512, S], BF16)

    def build(self):
        nc = bass.Bass("TRN2", target_bir_lowering=False)
        self.nc = nc
        self.declare(nc)
        with ExitStack() as es:
            K = Kern(nc, es)
            self.K = K
            self.es = es
            self.db = {n: K.buf("dram_" + n) for n in ["x1T", "qT", "kT", "vtok", "gates", "hrT", "y", "OT", "rwo", "rw4", "rwlw", "rwg", "rwbon", "rwv", "rwy0", "rwy1", "kmT", "vms"]}
            self.consts()
            stage = getattr(self, "stage", "all")
            if stage not in ("c", "p1a_nop0"):
                self.p0_weights()
            for slot in range(self.nslot):
                if stage not in ("p0", "c"):
                    self.p1(slot)
                K.barrier()
                if stage in ("p2", "all"):
                    self.p2(slot)
                    K.barrier()
                if stage in ("p3a", "p3b", "p3c", "all"):
                    self.p3a(slot)
                    K.barrier()
                if stage in ("p3b", "p3c", "all"):
                    self.p3b(slot)
                    K.barrier()
                if stage in ("p3c", "all"):
                    self.p3c(slot)
                    K.barrier()
                if stage in ("all",):
                    self.p4m(slot)
                    K.barrier()
                    self.p4(slot)
                    K.barrier()
            K.barrier()
        return nc

    def sbt(self, es, name, shape, dt):
        return es.enter_context(self.nc.sbuf_tensor(name, shape, dt))

    def pst(self, es, name, shape, dt):
        return es.enter_context(self.nc.psum_tensor(name, shape, dt))

    def consts(self):
        nc, K, es = self.nc, self.K, self.es
        self.pc = self.sbt(es, "pc", [128, NCOL], F32)
        self.b_pc = K.buf("pc")
        K.dma(K.sp, self.pc[:], self.pcols, writes=[self.b_pc])
        self.idf = self.sbt(es, "idf", [128, 128], F32)
        self.idb = self.sbt(es, "idb", [128, 128], BF16)
        self.b_id = K.buf("id")
        K.dma(K.sp, self.idf[:], self.ident, writes=[self.b_id])
        K.op(K.dve, lambda: nc.vector.tensor_copy(out=self.idb[:], in_=self.idf[:]), reads=[self.b_id], writes=[self.b_id], acc=True)
        self.ones = self.sbt(es, "ones", [128, 128], F32)
        self.b_ones = K.buf("ones")
        K.op(K.pool, lambda: nc.gpsimd.memset(self.ones[:], 1.0), writes=[self.b_ones])
        self.c2 = self.sbt(es, "c2", [128, 128 + 512 + 384 + 64], F32)
        self.b_c2 = K.buf("c2")
        K.dma(K.sp, self.c2[:], self.cst2, writes=[self.b_c2])
        self.ones2 = self.c2[:, 0:128]
        self.cmask = self.c2[:, 128:640]
        self.rmask = lambda d: self.c2[:, 640 + d * 192:640 + (d + 1) * 192]
        self.idl = self.c2[:, 1024:1088]
        self.epsc = self.sbt(es, "epsc", [128, 4], F32)
        for i, v in enumerate([LN_EPS, RMS_EPS, GN_EPS, 1e-18]):
            K.op(K.pool, lambda: nc.gpsimd.memset(self.epsc[:, i:i + 1], v), writes=[self.b_ones], acc=True)

    def col(self, name, j=0):
        o, c = COLS[name]
        return self.pc[:, o + j:o + j + 1]

    def p0_weights(self):
        K = self.K
        P = K.pool
        hist = []
        for n, sh in WL.items():
            self.db[n] = K.buf("dram_" + n)
            blocks = [(self.ws[n][j], self.wl[n][j]) for j in range(sh[0])] if len(sh) == 4 else [(self.ws[n], self.wl[n])]
            for o, i in blocks:
                if len(hist) >= 2:
                    b, c = hist[-2]
                    K._wait(P, Ev(b.dsem, b.dkey, c))
                K.dma(P, o, i, writes=[self.db[n]], acc=True)
                hist.append((self.db[n], self.db[n].dcount))

    def ln_fm(self, z, bz, gname, bname, o32, bo32, o16, bo16, ps, bps, sqf, bsq, st, bst, ps2=None, bps2=None, st2=None, bst2=None):
        nc, K = self.nc, self.K
        for c in range(8):
            K.op(K.act, lambda: nc.scalar.activation(out=sqf(c), in_=z[:, c, :], func=AF.Square), reads=[bz[c]], writes=[bsq[c]])
        for c in range(8):
            K.op(K.pe, lambda: nc.tensor.matmul(ps[:], lhsT=self.ones[:], rhs=z[:, c, :], start=(c == 0), stop=(c == 7)),
                 reads=[bz[c], self.b_ones], writes=[bps], inc=(c == 7), acc=(c > 0))
        for c in range(8):
            K.op(K.pe, lambda: nc.tensor.matmul(ps2[:], lhsT=self.ones[:], rhs=sqf(c), start=(c == 0), stop=(c == 7)),
                 reads=[bsq[c], self.b_ones], writes=[bps2], inc=(c == 7), acc=(c > 0))
        K.op(K.act, lambda: nc.scalar.mul(out=st[:], in_=ps[:], mul=1.0 / D), reads=[bps], writes=[bst])
        K.op(K.act, lambda: nc.scalar.activation(out=st2[:], in_=st[:], func=AF.Square), reads=[bst], writes=[bst2])
        K.op(K.dve, lambda: nc.vector.scalar_tensor_tensor(out=st2[:], in0=ps2[:], scalar=1.0 / D, in1=st2[:], op0=ALU.mult, op1=ALU.subtract),
             reads=[bps2, bst2], writes=[bst2])
        K.op(K.act, lambda: nc.scalar.activation(out=st2[:], in_=st2[:], func=AF.Ln, bias=self.epsc[:, 0:1]), reads=[bst2, self.b_ones], writes=[bst2])
        K.op(K.act, lambda: nc.scalar.activation(out=st2[:], in_=st2[:], func=AF.Exp, scale=-0.5), reads=[bst2], writes=[bst2])
        for c in range(8):
            K.op(K.dve, lambda: nc.vector.tensor_tensor(out=z[:, c, :], in0=z[:, c, :], in1=st[:], op=ALU.subtract), reads=[bz[c], bst], writes=[bz[c]])
            K.op(K.dve, lambda: nc.vector.tensor_tensor(out=z[:, c, :], in0=z[:, c, :], in1=st2[:], op=ALU.mult), reads=[bz[c], bst2], writes=[bz[c]])
            K.op(K.act, lambda: nc.scalar.activation(out=o32[:, c, :], in_=z[:, c, :], func=AF.Identity,
                                                      scale=self.col(gname, c), bias=self.col(bname, c)),
                 reads=[bz[c], self.b_pc], writes=[bo32[c]])
            if o16 is not None:
                K.op(K.pool, lambda: nc.gpsimd.tensor_copy(out=o16[:, c, :], in_=o32[:, c, :]), reads=[bo32[c]], writes=[bo16[c]])

    def rms_fm(self, src, bsrc, nch, nfeat, gname, sqf, bsq, ps, bps, st, bst, outb, boutb):
        nc, K = self.nc, self.K
        for c in range(nch):
            K.op(K.act, lambda: nc.scalar.activation(out=sqf(c), in_=src(c), func=AF.Square), reads=[bsrc[c]], writes=[bsq[c]])
        for c in range(nch):
            K.op(K.pe, lambda: nc.tensor.matmul(ps[:], lhsT=self.ones[:], rhs=sqf(c), start=(c == 0), stop=(c == nch - 1)),
                 reads=[bsq[c], self.b_ones], writes=[bps], inc=(c == nch - 1), acc=(c > 0))
        K.op(K.act, lambda: nc.scalar.activation(out=st[:], in_=ps[:], func=AF.Ln, scale=1.0 / nfeat, bias=self.epsc[:, 1:2]),
             reads=[bps, self.b_ones], writes=[bst])
        K.op(K.act, lambda: nc.scalar.activation(out=st[:], in_=st[:], func=AF.Exp, scale=-0.5), reads=[bst], writes=[bst])
        for c in range(nch):
            K.op(K.dve, lambda: nc.vector.scalar_tensor_tensor(out=outb(c), in0=src(c), scalar=self.col(gname, c), in1=st[:], op0=ALU.mult, op1=ALU.mult),
                 reads=[bsrc[c], bst, self.b_pc], writes=[boutb[c]])

    def ffn(self, idx, xb, bxb, xa, bxa, hT, bhT, wgu, bwgu, wdn, bwdn, psA, bpsA, z, bz, extra={}):
        nc, K = self.nc, self.K
        dbg, dbd = self.db["wgu%d" % idx], self.db["wd%d" % idx]
        K.dma(K.sp, wgu[0][:], self.wgu_s[idx][0], reads=[dbg], writes=[bwgu[0]])
        pi = 0
        for j in range(11):
            if j + 1 < 11:
                K.dma(K.sp, wgu[(j + 1) % 2][:], self.wgu_s[idx][j + 1], reads=[dbg], writes=[bwgu[(j + 1) % 2]])
            else:
                K.dma(K.sp, wdn[0][:], self.wd_s[idx][0], reads=[dbd], writes=[bwdn[0]])
            wt, bwt = wgu[j % 2], bwgu[j % 2]
            for f in range(2):
                fc = j * 2 + f
                pg, bpg = psA[pi % 4], bpsA[pi % 4]
                pu, bpu = psA[(pi + 1) % 4], bpsA[(pi + 1) % 4]
                pi += 2
                for kc in range(8):
                    K.op(K.pe, lambda: nc.tensor.matmul(pg[:], lhsT=wt[:, kc, f * 128:(f + 1) * 128], rhs=xb[:, kc, :], start=(kc == 0), stop=(kc == 7)),
                         reads=[bwt, bxb[kc]], writes=[bpg], inc=(kc == 7), acc=(kc > 0), lhs=[bwt])
                for kc in range(8):
                    K.op(K.pe, lambda: nc.tensor.matmul(pu[:], lhsT=wt[:, kc, 256 + f * 128:256 + (f + 1) * 128], rhs=xb[:, kc, :], start=(kc == 0), stop=(kc == 7)),
                         reads=[bwt, bxb[kc]], writes=[bpu], inc=(kc == 7), acc=(kc > 0), lhs=[bwt])
                K.op(K.act, lambda: nc.scalar.activation(out=hT[:, fc, :], in_=pg[:], func=AF.Silu), reads=[bpg], writes=[bhT[fc]] + extra.get(fc, []))
                K.op(K.dve, lambda: nc.vector.tensor_tensor(out=hT[:, fc, :], in0=hT[:, fc, :], in1=pu[:], op=ALU.mult),
                     reads=[bhT[fc], bpu], writes=[bhT[fc]])
        for c in range(8):
            if c + 1 < 8:
                K.dma(K.sp, wdn[(c + 1) % 2][:], self.wd_s[idx][c + 1], reads=[dbd], writes=[bwdn[(c + 1) % 2]])
            wt, bwt = wdn[c % 2], bwdn[c % 2]
            pz, bpz = psA[c % 4], bpsA[c % 4]
            for kc in range(NFC):
                K.op(K.pe, lambda: nc.tensor.matmul(pz[:], lhsT=wt[:, kc, :], rhs=hT[:, kc, :], start=(kc == 0), stop=(kc == NFC - 1)),
                     reads=[bwt, bhT[kc]], writes=[bpz], inc=(kc == NFC - 1), acc=(kc > 0), lhs=[bwt])
            K.op(K.dve, lambda: nc.vector.scalar_tensor_tensor(out=z[:, c, :], in0=pz[:], scalar=0.5, in1=xa[:, c, :], op0=ALU.mult, op1=ALU.add),
                 reads=[bpz, bxa[c]], writes=[bz[c]])

    def p1(self, slot):
        nc, K = self.nc, self.K
        K.begin_scope("p1")
        with ExitStack() as es:
            sb = lambda n, sh, dt: self.sbt(es, "p1_%d_" % slot + n, sh, dt)
            xtok = sb("xtok", [128, 4, D], F32); bsq = K.bufs(8, "xtok_sq")
            sqf = lambda c: xtok[:, c // 2, (c % 2) * TT:(c % 2 + 1) * TT]
            xa = sb("xa", [128, 8, TT], F32); bxa = K.bufs(8, "xa")
            xb = sb("xb", [128, 8, TT], BF16); bxb = K.bufs(8, "xb")
            hT = sb("hT", [128, NFC, TT], BF16); bhT = K.bufs(NFC, "hT")
            bhTr = K.bufs(16, "hTr")
            z = sb("z", [128, 8, TT], F32); bz = K.bufs(8, "z")
            st = sb("st", [128, TT], F32); bst = K.buf("st")
            st2 = sb("st2", [128, TT], F32); bst2 = K.buf("st2")
            x1b = sb("x1b", [128, 8, TT], BF16); bx1b = K.bufs(8, "x1b")
            wgu = [sb("wgu%d" % i, [128, 8, 512], BF16) for i in range(2)]; bwgu = K.bufs(2, "wgu")
            wdn = [sb("wdn%d" % i, [128, NFC, 128], BF16) for i in range(2)]; bwdn = K.bufs(2, "wdn")
            ev = sb("ev", [128, 4, TT], F32); bev = K.bufs(4, "ev")
            wuq = sb("wuq", [128, 3, 1536], BF16); bwuq = K.buf("wuq")
            wukv = sb("wukv", [128, 2, 1024], BF16); bwukv = K.buf("wukv")
            vst = sb("vst", [128, 4, 520], BF16); bvst = K.buf("vst")
            ropet = sb("ropet", [96, 2, TT], F32); bropet = K.buf("ropet")
            K.dma(K.sp, wuq[:], self.ws["wuq"], reads=[self.db["wuq"]], writes=[bwuq])
            K.dma(K.sp, wukv[:], self.ws["wukv"], reads=[self.db["wukv"]], writes=[bwukv])
            K.op(K.pool, lambda: nc.gpsimd.memset(vst[:], 1.0), writes=[bvst])
            psA = [self.pst(es, "p1_%d_ps%%d" % slot % i, [128, TT], F32) for i in range(4)]; bpsA = K.bufs(4, "psA")
            psS = self.pst(es, "p1_%d_pss" % slot, [128, TT], F32); bpsS = K.buf("psS")
            psT = [self.pst(es, "p1_%d_pst%%d" % slot % i, [128, TT], F32) for i in range(2)]; bpsT = K.bufs(2, "psT")
            L = dict(locals())
            for t in range(self.NT):
                t0 = t * TT
                K.dma(K.sp, xtok[:], self.x[slot, t0:t0 + TT, :].rearrange("(n p) d -> p n d", p=128), writes=bsq)
                for kc in range(8):
                    p, bp = psT[kc % 2], bpsT[kc % 2]
                    for n in range(4):
                        K.op(K.pe, lambda: nc.tensor.transpose(out=p[:, n * 128:(n + 1) * 128], in_=xtok[:, n, kc * 128:(kc + 1) * 128], identity=self.idf[:]),
                             reads=bsq + [self.b_id], writes=[bp], inc=(n == 3), acc=(n > 0))
                    K.op(K.act, lambda: nc.scalar.mul(out=xa[:, kc, :], in_=p[:], mul=ALPHA), reads=[bp], writes=[bxa[kc]])
                    K.op(K.dve, lambda: nc.vector.tensor_copy(out=xb[:, kc, :], in_=p[:]), reads=[bp], writes=[bxb[kc]])
                self.ffn(0, xb, bxb, xa, bxa, hT, bhT, wgu, bwgu, wdn, bwdn, psA, bpsA, z, bz, extra={i: [bhTr[i]] for i in range(16)})
                self.ln_fm(z, bz, "ln1_g", "ln1_b", xa, bxa, x1b, bx1b, psS, bpsS, sqf, bsq, st, bst, psA[3], bpsA[3], st2, bst2)
                K.dma(K.pool, self.x1T.rearrange("(c p) s -> p c s", p=128)[:, :, t0:t0 + TT], xa[:], reads=bxa, writes=[self.db["x1T"]], acc=True, owner=bxa[0])
                self.p1_win(slot, t, L)

    def p1_win(self, slot, t, L):
        nc, K = self.nc, self.K
        x1b, bx1b, wbuf, bwbuf, psA, bpsA, psS, bpsS, ev, bev = [L[k] for k in ["x1b", "bx1b", "wgu", "bwgu", "psA", "bpsA", "psS", "bpsS", "ev", "bev"]]
        z, bz, xb, bxb, hT, bhT, st, bst, sqf, bsq, bhTr = [L[k] for k in ["z", "bz", "xb", "bxb", "hT", "bhT", "st", "bst", "sqf", "bsq", "bhTr"]]
        wuq, bwuq, wukv, bwukv, vst, bvst, ropet, bropet = [L[k] for k in ["wuq", "bwuq", "wukv", "bwukv", "vst", "bvst", "ropet", "bropet"]]
        t0 = t * TT
        dbw = self.db["win"]
        K.dma(K.sp, wbuf[0][:], self.win_s[0], reads=[dbw], writes=[bwbuf[0]])
        K.dma(K.sp, ropet[64:96, 0, :], self.ropec[:, t0:t0 + TT], writes=[bropet])
        K.dma(K.sp, ropet[64:96, 1, :], self.ropes[:, t0:t0 + TT], writes=[bropet], acc=True)
        stt = {"pi": 0, "ei": 0}

        def nextps():
            i = stt["pi"] % 4
            stt["pi"] += 1
            return psA[i], bpsA[i]

        def proj(wt, bwt, c0, width):
            p, bp = nextps()
            for kc in range(8):
                K.op(K.pe, lambda: nc.tensor.matmul(p[0:width, :], lhsT=wt[:, kc, c0:c0 + width], rhs=x1b[:, kc, :], start=(kc == 0), stop=(kc == 7)),
                     reads=[bwt, bx1b[kc]], writes=[bp], inc=(kc == 7), acc=(kc > 0), lhs=[bwt])
            return p, bp

        def rope(pp, bpp, psw, bpsw, out_ap, bout):
            t1, t2 = z[64:96, 3, :], z[64:96, 4, :]
            K.op(K.dve, lambda: nc.vector.tensor_tensor(out=t1, in0=pp[64:96, :], in1=ropet[64:96, 0, :], op=ALU.mult), reads=[bpp, bropet], writes=[bz[3]])
            K.op(K.dve, lambda: nc.vector.tensor_tensor(out=t2, in0=psw[64:96, :], in1=ropet[64:96, 1, :], op=ALU.mult), reads=[bpsw, bropet], writes=[bz[4]])
            K.op(K.pool, lambda: nc.gpsimd.tensor_tensor(out=out_ap, in0=t1, in1=t2, op=ALU.add), reads=[bz[3], bz[4]], writes=bout)

        deferred = []
        for blk in range(11):
            if blk + 1 < 11:
                K.dma(K.sp, wbuf[(blk + 1) % 2][:], self.win_s[blk + 1], reads=[dbw], writes=[bwbuf[(blk + 1) % 2]])
            wt, bwt = wbuf[blk % 2], bwbuf[blk % 2]
            if blk == 0:
                for c in range(3):
                    p, bp = proj(wt, bwt, c * 128, 128)
                    K.op(K.act, lambda: nc.scalar.copy(out=z[:, c, :], in_=p[:]), reads=[bp], writes=[bz[c]])
                self.rms_fm(lambda c: z[:, c, :], bz, 3, 384.0, "q_norm_g", sqf, bsq, psS, bpsS, st, bst, lambda c: xb[:, c, :], bxb)

                def q_part_b():
                  for h in range(H):
                      pp, bpp = nextps()
                      psw, bpsw = nextps()
                      for v_, (pt_, bpt_) in enumerate([(pp, bpp), (psw, bpsw)]):
                          for kc in range(3):
                              K.op(K.pe, lambda: nc.tensor.matmul(pt_[0:96, :], lhsT=wuq[:, kc, (v_ * 8 + h) * 96:(v_ * 8 + h + 1) * 96], rhs=xb[:, kc, :], start=(kc == 0), stop=(kc == 2)),
                                   reads=[bwuq, bxb[kc]], writes=[bpt_], inc=(kc == 2), acc=(kc > 0))
                      K.op(K.act, lambda: nc.scalar.copy(out=hT[0:64, h, :], in_=pp[0:64, :]), reads=[bpp], writes=[bhT[h]])
                      rope(pp, bpp, psw, bpsw, hT[64:96, h, :], [bhTr[h]])
                  K.dma(K.pool, self.qT.rearrange("h d s -> d h s")[:, :, t0:t0 + TT], hT[0:96, 0:8, :], reads=bhT[0:8] + bhTr[0:8], writes=[self.db["qT"]], acc=True, owner=bhT[0])
                deferred.append(q_part_b)
            elif blk == 1:
                for c in range(2):
                    p, bp = proj(wt, bwt, c * 128, 128)
                    K.op(K.act, lambda: nc.scalar.copy(out=z[:, 5 + c, :], in_=p[:]), reads=[bp], writes=[bz[5 + c]])
                self.rms_fm(lambda c: z[:, 5 + c, :], bz[5:7], 2, 256.0, "kv_norm_g", sqf, bsq, psS, bpsS, st, bst, lambda c: xb[:, 3 + c, :], bxb[3:5])

                def kv_part_b():
                  for h in range(H):
                      p, bp = nextps()
                      for kc in range(2):
                          K.op(K.pe, lambda: nc.tensor.matmul(p[0:64, :], lhsT=wukv[:, kc, h * 64:(h + 1) * 64], rhs=xb[:, 3 + kc, :], start=(kc == 0), stop=(kc == 1)),
                               reads=[bwukv, bxb[3 + kc]], writes=[bp], inc=(kc == 1), acc=(kc > 0))
                      K.op(K.act, lambda: nc.scalar.copy(out=hT[0:64, 8 + h, :], in_=p[0:64, :]), reads=[bp], writes=[bhT[8 + h]])
                  for n in range(4):
                      p, bp = nextps()
                      for kc in range(2):
                          K.op(K.pe, lambda: nc.tensor.matmul(p[:], lhsT=xb[:, 3 + kc, n * 128:(n + 1) * 128], rhs=wukv[:, kc, 512:1024], start=(kc == 0), stop=(kc == 1)),
                               reads=[bwukv, bxb[3 + kc]], writes=[bp], inc=(kc == 1), acc=(kc > 0))
                      K.op(K.dve, lambda: nc.vector.tensor_copy(out=vst[:, n, :].rearrange("p (h d) -> p h d", d=65)[:, :, 0:64], in_=p[:].rearrange("p (h d) -> p h d", d=64)), reads=[bp], writes=[bvst], acc=(n > 0))
                  K.dma(K.pool, self.vtok[t0:t0 + TT].rearrange("(n p) h d -> p n (h d)", p=128), vst[:], reads=[bvst], writes=[self.db["vtok"]], acc=True, owner=bvst)
                deferred.append(kv_part_b)
            elif blk == 2:
                pp, bpp = proj(wt, bwt, 0, 96)
                psw, bpsw = proj(wt, bwt, 96, 96)
                rope(pp, bpp, psw, bpsw, z[64:96, 7, :], [bz[7]])
                for h in range(H):
                    eng = K.act if h % 2 == 0 else K.pool
                    if h % 2 == 0:
                        K.op(K.act, lambda: nc.scalar.copy(out=hT[64:96, 8 + h, :], in_=z[64:96, 7, :]), reads=[bz[7]], writes=[bhTr[8 + h]])
                    else:
                        K.op(K.pool, lambda: nc.gpsimd.tensor_copy(out=hT[64:96, 8 + h, :], in_=z[64:96, 7, :]), reads=[bz[7]], writes=[bhTr[8 + h]])
                deferred.append(lambda: K.dma(K.pool, self.kT.rearrange("h d s -> d h s")[:, :, t0:t0 + TT], hT[0:96, 8:16, :], reads=bhT[8:16] + bhTr[8:16], writes=[self.db["kT"]], acc=True, owner=bhT[8]))
            elif blk in (3, 4, 5, 6):
                widths = [128] * 4 if blk < 6 else [128, 64, 128]
                c0 = 0
                for i, wd_ in enumerate(widths):
                    p, bp = proj(wt, bwt, c0, wd_)
                    e, be = ev[:, stt["ei"] % 4, :], bev[stt["ei"] % 4]
                    stt["ei"] += 1
                    K.op(K.act, lambda: nc.scalar.copy(out=e[0:wd_, :], in_=p[0:wd_, :]), reads=[bp], writes=[be])
                    r0 = (blk - 3) * 512 + c0
                    K.dma(K.pool, self.hrT[r0:r0 + wd_, t0:t0 + TT], e[0:wd_, :], reads=[be], writes=[self.db["hrT"]], acc=True, owner=be)
                    c0 += wd_
            else:
                for i in range(4):
                    gc = (blk - 7) * 4 + i
                    p, bp = proj(wt, bwt, i * 128, 128)
                    e, be = ev[:, stt["ei"] % 4, :], bev[stt["ei"] % 4]
                    stt["ei"] += 1
                    K.op(K.act, lambda: nc.scalar.activation(out=e, in_=p[:], func=AF.Sigmoid, bias=self.col("b_gate", gc)),
                         reads=[bp, self.b_pc], writes=[be])
                    K.dma(K.pool, self.gates[gc * 128:(gc + 1) * 128, t0:t0 + TT], e, reads=[be], writes=[self.db["gates"]], acc=True, owner=be)


        for f in deferred:
            f()

    def p2(self, slot):
        nc, K = self.nc, self.K
        K.begin_scope("p2")
        S = self.S
        NK = S // 128
        scale = float(96 ** -0.5)
        with ExitStack() as es:
            sb = lambda n, sh, dt: self.sbt(es, "p2_%d_" % slot + n, sh, dt)
            KT = sb("KT", [96, 8, S], BF16); bKT = K.buf("KT")
            Vt = sb("Vt", [128, NK, 520], BF16); bVt = K.buf("Vt")
            Qt = [sb("Qt%d" % i, [96, 8, TT], BF16) for i in range(2)]; bQt = K.bufs(2, "Qt")
            PT = [sb("PT%d" % i, [128, TT], BF16) for i in range(4)]; bPT = K.bufs(4, "PT")
            Otok = sb("Otok", [128, 4, 512], BF16); bOtok = K.bufs(8, "Otok")
            OTs = sb("OTs", [128, 4, TT], BF16); bOTs = K.bufs(4, "OTs")
            rs = sb("rs", [128, 2, 4], F32); brs = K.bufs(2, "rs")
            psS = [self.pst(es, "p2_%d_pss%%d" % slot % i, [128, TT], F32) for i in range(3)]; bpsS = K.bufs(3, "psS")
            psO = [self.pst(es, "p2_%d_pso%%d" % slot % i, [128, TT], F32) for i in range(2)]; bpsO = K.bufs(2, "psO")
            psT = self.pst(es, "p2_%d_pst" % slot, [128, 2 * TT], BF16); bpsT = K.buf("psT")
            for h in range(H):
                K.dma(K.sp, KT[:, h, :], self.kT[h], reads=[self.db["kT"]], writes=[bKT], acc=(h > 0))
            K.dma(K.sp, Vt[:], self.vtok.rearrange("(n p) h d -> p n (h d)", p=128), reads=[self.db["vtok"]], writes=[bVt])
            qTv = self.qT.rearrange("h d s -> d h s")
            K.dma(K.sp, Qt[0][:], qTv[:, :, 0:TT], reads=[self.db["qT"]], writes=[bQt[0]])
            items = [(t, h, kt) for t in range(self.NT) for h in range(H) for kt in range(NK)]

            def emit_S(i):
                t, h, kt = items[i]
                if h == 0 and kt == 0 and t + 1 < self.NT:
                    K.dma(K.sp, Qt[(t + 1) % 2][:], qTv[:, :, (t + 1) * TT:(t + 2) * TT], reads=[self.db["qT"]], writes=[bQt[(t + 1) % 2]])
                pS, bpS = psS[i % 3], bpsS[i % 3]
                K.op(K.pe, lambda: nc.tensor.matmul(pS[:], lhsT=KT[:, h, kt * 128:(kt + 1) * 128], rhs=Qt[t % 2][:, h, :], start=True, stop=True),
                     reads=[bKT, bQt[t % 2]], writes=[bpS], lhs=[bKT])

            emit_S(0)
            emit_S(1)
            for i, (t, h, kt) in enumerate(items):
                t0 = t * TT
                if i + 2 < len(items):
                    emit_S(i + 2)
                pS, bpS = psS[i % 3], bpsS[i % 3]
                P_, bP = PT[i % 4], bPT[i % 4]
                pO, bpO = psO[h % 2], bpsO[h % 2]
                K.op(K.act, lambda: nc.scalar.activation(out=P_[:], in_=pS[:], func=AF.Exp, scale=scale), reads=[bpS], writes=[bP])
                for qs in range(4):
                    K.op(K.pe, lambda: nc.tensor.matmul(pO[:, qs * 65:(qs + 1) * 65], lhsT=P_[:, qs * 128:(qs + 1) * 128], rhs=Vt[:, kt, h * 65:(h + 1) * 65],
                                                         start=(kt == 0 and qs == 0), stop=(kt == NK - 1 and qs == 3)),
                         reads=[bP, bVt], writes=[bpO], inc=(qs == 3), acc=not (kt == 0 and qs == 0))
                if kt == NK - 1:
                    r_, br = rs[:, h % 2, :], brs[h % 2]
                    K.op(K.dve, lambda: nc.vector.reciprocal(out=r_, in_=pO[:, 0:260].rearrange("p (q d) -> p q d", d=65)[:, :, 64]), reads=[bpO], writes=[br])
                    for qs in range(4):
                        K.op(K.dve, lambda: nc.vector.tensor_scalar(out=Otok[:, qs, h * 64:(h + 1) * 64], in0=pO[:, qs * 65:qs * 65 + 64], scalar1=rs[:, h % 2, qs:qs + 1], scalar2=None, op0=ALU.mult),
                             reads=[bpO, br], writes=[bOtok[h]], acc=(qs > 0))
                    if h == H - 1:
                        for c in range(4):
                            for qs in range(4):
                                K.op(K.pe, lambda: nc.tensor.transpose(out=psT[:, qs * 128:(qs + 1) * 128], in_=Otok[:, qs, c * 128:(c + 1) * 128], identity=self.idb[:]),
                                     reads=[bOtok[2 * c], bOtok[2 * c + 1], self.b_id], writes=[bpsT], inc=(qs == 3), acc=(qs > 0))
                            K.op(K.dve, lambda: nc.vector.tensor_copy(out=OTs[:, c, :], in_=psT[:, 0:TT]), reads=[bpsT], writes=[bOTs[c]])
                        K.dma(K.pool, self.OT.rearrange("(c p) s -> p c s", p=128)[:, :, t0:t0 + TT], OTs[:], reads=bOTs, writes=[self.db["OT"]], acc=True, owner=bOTs[0])

    def p3a(self, slot):
        nc, K = self.nc, self.K
        K.begin_scope("p3a")
        S, NT = self.S, self.NT
        with ExitStack() as es:
            sb = lambda n, sh, dt: self.sbt(es, "p3a_%d_" % slot + n, sh, dt)
            hr = sb("hr", [128, 15, TT + 2], F32); bhr = K.bufs(15, "hr")
            sh = sb("sh", [128, 15, TT], F32); bsh = K.bufs(15, "sh")
            lw = sb("lw", [128, 8, TT], F32); blw = K.bufs(8, "lw")
            eta = sb("eta", [128, 4, TT], F32); beta = K.bufs(4, "eta")
            o4 = sb("o4", [128, 3, 4, TT], F32); bo4 = [K.bufs(4, "o4_%d" % i) for i in range(3)]
            gT = sb("gT", [128, 4, TT], F32); bgT = K.bufs(4, "gT")
            bon = sb("bon", [128, 4, TT], F32); bbon = K.bufs(4, "bon")
            tmp = sb("tmp", [128, 4, TT], F32); btmp = K.bufs(4, "tmp")
            tb16 = sb("tb16", [128, 3, TT], BF16); btb = K.bufs(3, "tb16")
            vrt = sb("vrt", [128, 4, 512], BF16); bvrt = K.bufs(4, "vrt")
            wup = sb("wup", [128, 512], BF16); aup = sb("aup", [64, 512], BF16); gup = sb("gup", [128, 512], BF16); bw = K.buf("rwkvw")
            cc = sb("cc", [128, 15 + 4], F32); bcc = K.buf("cc")
            K.dma(K.sp, wup[:], self.ws["wup"][:, 0, :], reads=[self.db["wup"]], writes=[bw])
            K.dma(K.sp, aup[:], self.ws["aup"][:, 0, :], reads=[self.db["aup"]], writes=[bw], acc=True)
            K.dma(K.sp, gup[:], self.ws["gup"][:, 0, :], reads=[self.db["gup"]], writes=[bw], acc=True)
            o, _ = COLS["mu_prev"]; o2, _ = COLS["mu_next"]; oka, _ = COLS["k_a"]
            K.op(K.dve, lambda: nc.vector.tensor_tensor(out=cc[:, 0:15], in0=self.pc[:, o:o + 15], in1=self.pc[:, o2:o2 + 15], op=ALU.add), reads=[self.b_pc], writes=[bcc])
            K.op(K.dve, lambda: nc.vector.tensor_scalar(out=cc[:, 0:15], in0=cc[:, 0:15], scalar1=-1.0, scalar2=1.0, op0=ALU.mult, op1=ALU.add), reads=[bcc], writes=[bcc])
            K.op(K.dve, lambda: nc.vector.tensor_scalar(out=cc[:, 15:19], in0=self.pc[:, oka:oka + 4], scalar1=-1.0, scalar2=1.0, op0=ALU.mult, op1=ALU.add), reads=[self.b_pc, bcc], writes=[bcc])
            ps = [self.pst(es, "p3a_%d_ps%%d" % slot % i, [128, TT], F32) for i in range(5)]; bps = K.bufs(5, "ps3a")
            pi = [0]

            def nps():
                i = pi[0] % 5
                pi[0] += 1
                return ps[i], bps[i]

            dbh = self.db["hrT"]
            for t in range(NT):
                t0 = t * TT
                lo, hi = max(t0 - 1, 0), min(t0 + TT + 1, S)
                a_, b_ = lo - (t0 - 1), hi - (t0 - 1)
                K.dma(K.sp, hr[:, 0:12, a_:b_], self.hrT[0:1536].rearrange("(c p) s -> p c s", p=128)[:, :, lo:hi], reads=[dbh], writes=bhr[0:12])
                K.dma(K.sp, hr[:, 12, a_:b_], self.hrT[1536:1664, lo:hi], reads=[dbh], writes=[bhr[12]])
                K.dma(K.sp, hr[0:64, 13, a_:b_], self.hrT[1664:1728, lo:hi], reads=[dbh], writes=[bhr[13]])
                K.dma(K.sp, hr[:, 14, a_:b_], self.hrT[1728:1856, lo:hi], reads=[dbh], writes=[bhr[14]])
                if t == 0:
                    K.op(K.pool, lambda: nc.gpsimd.memset(hr[:, :, 0:1], 0.0), writes=bhr, acc=True)
                if t == NT - 1:
                    K.op(K.pool, lambda: nc.gpsimd.memset(hr[:, :, TT + 1:TT + 2], 0.0), writes=bhr, acc=True)
                for j, (ro, wd_) in enumerate(RW_CH):
                    K.op(K.act, lambda: nc.scalar.activation(out=sh[0:wd_, j, :], in_=hr[0:wd_, j, 1:TT + 1], func=AF.Identity, scale=cc[0:wd_, j:j + 1]),
                         reads=[bhr[j], bcc], writes=[bsh[j]])
                    K.op(K.dve, lambda: nc.vector.scalar_tensor_tensor(out=sh[0:wd_, j, :], in0=hr[0:wd_, j, 0:TT], scalar=self.pc[0:wd_, o + j:o + j + 1], in1=sh[0:wd_, j, :], op0=ALU.mult, op1=ALU.add),
                         reads=[bhr[j], bsh[j], self.b_pc], writes=[bsh[j]])
                    K.op(K.dve, lambda: nc.vector.scalar_tensor_tensor(out=sh[0:wd_, j, :], in0=hr[0:wd_, j, 2:TT + 2], scalar=self.pc[0:wd_, o2 + j:o2 + j + 1], in1=sh[0:wd_, j, :], op0=ALU.mult, op1=ALU.add),
                         reads=[bhr[j], bsh[j], self.b_pc], writes=[bsh[j]])
                r_ = lambda c: sh[:, c, :]
                k_ = lambda c: sh[:, 4 + c, :]
                v_ = lambda c: sh[:, 8 + c, :]
                K.op(K.act, lambda: nc.scalar.activation(out=tb16[:, 0, :], in_=sh[:, 12, :], func=AF.Tanh), reads=[bsh[12]], writes=[btb[0]])
                K.op(K.pool, lambda: nc.gpsimd.tensor_copy(out=tb16[0:64, 1, :], in_=sh[0:64, 13, :]), reads=[bsh[13]], writes=[btb[1]])
                K.op(K.act, lambda: nc.scalar.activation(out=tb16[:, 2, :], in_=sh[:, 14, :], func=AF.Sigmoid), reads=[bsh[14]], writes=[btb[2]])
                for d in range(2):
                    for c in range(4):
                        p, bp = nps()
                        K.op(K.pe, lambda: nc.tensor.matmul(p[:], lhsT=wup[64 * d:64 * d + 64, c * 128:(c + 1) * 128], rhs=tb16[64 * d:64 * d + 64, 0, :], start=True, stop=True),
                             reads=[bw, btb[0]], writes=[bp])
                        K.op(K.act, lambda: nc.scalar.activation(out=lw[:, d * 4 + c, :], in_=p[:], func=AF.Sigmoid, bias=self.col("w0", d * 4 + c)), reads=[bp, self.b_pc], writes=[blw[d * 4 + c]])
                        K.op(K.pool, lambda: nc.gpsimd.tensor_scalar(out=lw[:, d * 4 + c, :], in0=lw[:, d * 4 + c, :], scalar1=-WCONST, scalar2=None, op0=ALU.mult), reads=[blw[d * 4 + c]], writes=[blw[d * 4 + c]])
                    K.dma(K.pool, self.rwlw[d].rearrange("(c p) s -> p c s", p=128)[:, :, t0:t0 + TT], lw[:, d * 4:d * 4 + 4, :], reads=blw[d * 4:d * 4 + 4], writes=[self.db["rwlw"]], acc=True, owner=blw[d * 4])
                for c in range(4):
                    p, bp = nps()
                    K.op(K.pe, lambda: nc.tensor.matmul(p[:], lhsT=aup[0:64, c * 128:(c + 1) * 128], rhs=tb16[0:64, 1, :], start=True, stop=True), reads=[bw, btb[1]], writes=[bp])
                    K.op(K.act, lambda: nc.scalar.activation(out=eta[:, c, :], in_=p[:], func=AF.Sigmoid, bias=self.col("a0", c)), reads=[bp, self.b_pc], writes=[beta[c]])
                    p, bp = nps()
                    K.op(K.pe, lambda: nc.tensor.matmul(p[:], lhsT=gup[:, c * 128:(c + 1) * 128], rhs=tb16[:, 2, :], start=True, stop=True), reads=[bw, btb[2]], writes=[bp])
                    K.op(K.act, lambda: nc.scalar.copy(out=gT[:, c, :], in_=p[:]), reads=[bp], writes=[bgT[c]])
                    kk, bkk = tmp[:, c % 2, :], btmp[c % 2]
                    sq, bsq_ = tmp[:, 2 + c % 2, :], btmp[2 + c % 2]
                    K.op(K.act, lambda: nc.scalar.activation(out=kk, in_=k_(c), func=AF.Identity, scale=self.col("k_k", c)), reads=[bsh[4 + c], self.b_pc], writes=[bkk])
                    K.op(K.act, lambda: nc.scalar.activation(out=sq, in_=kk, func=AF.Square), reads=[bkk], writes=[bsq_])
                    p, bp = nps()
                    K.op(K.pe, lambda: nc.tensor.matmul(p[:], lhsT=self.ones2, rhs=sq, start=True, stop=True), reads=[self.b_c2, bsq_], writes=[bp])
                    K.op(K.act, lambda: nc.scalar.activation(out=sq, in_=p[:], func=AF.Ln, bias=self.epsc[:, 3:4]), reads=[bp, self.b_ones], writes=[bsq_])
                    K.op(K.act, lambda: nc.scalar.activation(out=sq, in_=sq, func=AF.Exp, scale=-0.5), reads=[bsq_], writes=[bsq_])
                    av, bav = o4[:, 1, c, :], bo4[1][c]
                    bv, bbv = o4[:, 2, c, :], bo4[2][c]
                    km, bkm = o4[:, 0, c, :], bo4[0][c]
                    K.op(K.dve, lambda: nc.vector.scalar_tensor_tensor(out=av, in0=kk, scalar=-1.0, in1=sq, op0=ALU.mult, op1=ALU.mult), reads=[bkk, bsq_], writes=[bav])
                    K.op(K.dve, lambda: nc.vector.scalar_tensor_tensor(out=bv, in0=av, scalar=-1.0, in1=eta[:, c, :], op0=ALU.mult, op1=ALU.mult), reads=[bav, beta[c]], writes=[bbv])
                    K.op(K.dve, lambda: nc.vector.tensor_scalar(out=km, in0=eta[:, c, :], scalar1=self.col("k_a", c), scalar2=cc[:, 15 + c:16 + c], op0=ALU.mult, op1=ALU.add), reads=[beta[c], self.b_pc, bcc], writes=[bkm])
                    K.op(K.pool, lambda: nc.gpsimd.tensor_tensor(out=km, in0=km, in1=k_(c), op=ALU.mult), reads=[bkm, bsh[4 + c]], writes=[bkm])
                    K.op(K.dve, lambda: nc.vector.scalar_tensor_tensor(out=kk, in0=r_(c), scalar=self.col("r_k", c), in1=km, op0=ALU.mult, op1=ALU.mult), reads=[bsh[c], bkm, self.b_pc], writes=[bkk])
                    p, bp = nps()
                    K.op(K.pe, lambda: nc.tensor.matmul(p[:], lhsT=self.ones2, rhs=kk, start=True, stop=True), reads=[self.b_c2, bkk], writes=[bp])
                    K.op(K.dve, lambda: nc.vector.tensor_tensor(out=bon[:, c, :], in0=p[:], in1=v_(c), op=ALU.mult), reads=[bp, bsh[8 + c]], writes=[bbon[c]])
                for n in range(4):
                    p, bp = nps()
                    for c in range(4):
                        K.op(K.pe, lambda: nc.tensor.transpose(out=p[:, c * 128:(c + 1) * 128], in_=sh[:, 8 + c, n * 128:(n + 1) * 128], identity=self.idf[:]),
                             reads=[bsh[8 + c], self.b_id], writes=[bp], inc=(c == 3), acc=(c > 0))
                    K.op(K.act, lambda: nc.scalar.copy(out=vrt[:, n, :], in_=p[:]), reads=[bp], writes=[bvrt[n]])
                K.dma(K.pool, self.rwv[t0:t0 + TT].rearrange("(n p) c -> p n c", p=128), vrt[:], reads=bvrt, writes=[self.db["rwv"]], acc=True, owner=bvrt[0])
                rw4v = lambda i: self.rw4[i].rearrange("(c p) s -> p c s", p=128)[:, :, t0:t0 + TT]
                K.dma(K.pool, rw4v(0), sh[:, 0:4, :], reads=bsh[0:4], writes=[self.db["rw4"]], acc=True, owner=bsh[0])
                for i in range(3):
                    K.dma(K.pool, rw4v(1 + i), o4[:, i, :, :], reads=bo4[i], writes=[self.db["rw4"]], acc=True, owner=bo4[i][0])
                K.dma(K.pool, self.rwg.rearrange("(c p) s -> p c s", p=128)[:, :, t0:t0 + TT], gT[:], reads=bgT, writes=[self.db["rwg"]], acc=True, owner=bgT[0])
                K.dma(K.pool, self.rwbon.rearrange("(c p) s -> p c s", p=128)[:, :, t0:t0 + TT], bon[:], reads=bbon, writes=[self.db["rwbon"]], acc=True, owner=bbon[0])

    def p3b(self, slot):
        nc, K = self.nc, self.K
        K.begin_scope("p3b")
        S, NT = self.S, self.NT
        with ExitStack() as es:
            sb = lambda n, sh, dt: self.sbt(es, "p3b_%d_" % slot + n, sh, dt)
            R2 = range(2)
            arT = [sb("arT%d" % d, [128, 4, 8, 128], BF16) for d in R2]; barT = [K.bufs(4, "arT%d_" % d) for d in R2]
            bkT = [sb("bkT%d" % d, [128, 4, 8, 128], BF16) for d in R2]; bbkT = [K.bufs(4, "bkT%d_" % d) for d in R2]
            btk = [sb("btk%d" % d, [128, 8, 2, 512], BF16) for d in R2]; bbtk = [K.bufs(16, "btk%d_" % d) for d in R2]
            vt = [sb("vt%d" % d, [128, 8, 512], BF16) for d in R2]; bvt = K.bufs(2, "vt")
            et = [sb("et%d" % d, [128, 4, 8], F32) for d in R2]; bet = K.bufs(2, "et")
            S32 = [sb("S32_%d" % d, [128, 4, 64], F32) for d in R2]; bS32 = K.bufs(2, "S32")
            Sb = [sb("Sb%d" % d, [128, 4, 64], BF16) for d in R2]; bSb = K.bufs(2, "Sb")
            inb = [sb("inb%d" % i, [128, 5, TT], F32) for i in range(2)]; binb = K.bufs(2, "inb")
            gt = sb("gt", [128, 5, TT], F32); bgt = K.bufs(5, "gt")
            Xs = [[sb("Xs%d_%d" % (d, q), [128, 2, 4, 128], BF16) for q in R2] for d in R2]; bXs = [[K.bufs(2, "Xs%d_%d_" % (d, q)) for q in R2] for d in R2]
            As = [sb("As%d" % d, [128, 2, 4, 64], BF16) for d in R2]; bAs = [K.bufs(2, "As%d_" % d) for d in R2]
            Ns = [sb("Ns%d" % d, [128, 2, 4, 64], BF16) for d in R2]; bNs = [K.bufs(2, "Ns%d_" % d) for d in R2]
            Ms = [[sb("Ms%d_%d" % (d, q), [128, 4, 64], BF16) for q in R2] for d in R2]; bMs = [K.bufs(2, "Ms%d_" % d) for d in R2]
            Ws = [sb("Ws%d" % d, [128, 4, 64], BF16) for d in R2]; bWs = K.bufs(2, "Ws")
            Us = [sb("Us%d" % d, [128, 4, 64], BF16) for d in R2]; bUs = K.bufs(2, "Us")
            yt = [sb("yt%d" % d, [128, 2, 256], F32) for d in R2]; byt = [K.bufs(2, "yt%d_" % d) for d in R2]
            stmp = sb("stmp", [128, 2, 256], F32); bstmp = K.bufs(2, "stmp")
            ps = [self.pst(es, "p3b_%d_ps%%d" % slot % i, [128, TT], F32) for i in range(6)]; bps = K.bufs(6, "ps3b")
            pst = self.pst(es, "p3b_%d_pst" % slot, [128, 2 * TT], BF16); bpst = K.buf("ps3bt")
            pi = [0]

            def nps2():
                i = pi[0] % 3
                pi[0] += 1
                return [(ps[2 * i], bps[2 * i]), (ps[2 * i + 1], bps[2 * i + 1])]

            for d in R2:
                K.op(K.pool, lambda: nc.gpsimd.memset(S32[d][:], 0.0), writes=[bS32[d]])
                K.op(K.pool, lambda: nc.gpsimd.memset(Sb[d][:], 0.0), writes=[bSb[d]])
            ii = [0]
            LIM = int(os.environ.get("P3B_LIM", "9"))
            PL = lambda l: slice(64 * l, 64 * l + 64)

            def prep(d, tile):
                t0 = tile * TT
                for l in range(2):
                    K.dma(K.sp, vt[d][PL(l), :, :], self.rwv[t0:t0 + TT].rearrange("(n p) c -> p n c", p=64), reads=[self.db["rwv"]], writes=[bvt[d]], acc=(l > 0))
                for c in range(4):
                    ib, bib = inb[ii[0] % 2], binb[ii[0] % 2]
                    ii[0] += 1
                    K.dma(K.sp, ib[:, 0:4, :], self.rw4[:, c * 128:(c + 1) * 128, t0:t0 + TT].rearrange("i p s -> p i s"), reads=[self.db["rw4"]], writes=[bib])
                    K.dma(K.sp, ib[:, 4, :], self.rwlw[d, c * 128:(c + 1) * 128, t0:t0 + TT], reads=[self.db["rwlw"]], writes=[bib], acc=True)
                    r_, k_, a_, b_, lw_ = [ib[:, i, :] for i in range(5)]
                    L, G, Er, Ei, Ea = [gt[:, i, :] for i in range(5)]
                    v3 = lambda ap: ap.rearrange("p (n t) -> p n t", t=64)
                    K.op(K.dve, lambda: nc.vector.tensor_tensor_scan(out=L, data0=self.cmask, data1=lw_, initial=0.0, op0=ALU.mult, op1=ALU.add),
                         reads=[bib, self.b_c2], writes=[bgt[0]])
                    Ltot = v3(L)[:, :, 63]
                    K.op(K.act, lambda: nc.scalar.activation(out=et[d][:, c, :], in_=Ltot, func=AF.Exp), reads=[bgt[0]], writes=[bet[d]], acc=(c > 0))
                    if d == 0:
                        Gs, bG = L, bgt[0]
                    else:
                        K.op(K.dve, lambda: nc.vector.tensor_tensor(out=G, in0=lw_, in1=L, op=ALU.subtract), reads=[bib, bgt[0]], writes=[bgt[1]])
                        K.op(K.dve, lambda: nc.vector.tensor_tensor(out=v3(G), in0=v3(G), in1=Ltot.unsqueeze(2).broadcast_to([128, 8, 64]), op=ALU.add), reads=[bgt[1], bgt[0]], writes=[bgt[1]])
                        Gs, bG = G, bgt[1]
                    K.op(K.act, lambda: nc.scalar.activation(out=Er, in_=Gs, func=AF.Exp), reads=[bG], writes=[bgt[2]])
                    K.op(K.act, lambda: nc.scalar.activation(out=Ei, in_=Gs, func=AF.Exp, scale=-1.0), reads=[bG], writes=[bgt[3]])
                    K.op(K.dve, lambda: nc.vector.tensor_tensor(out=Ea, in0=Gs, in1=lw_, op=ALU.subtract), reads=[bG, bib], writes=[bgt[4]])
                    K.op(K.act, lambda: nc.scalar.activation(out=Ea, in_=Ea, func=AF.Exp), reads=[bgt[4]], writes=[bgt[4]])
                    K.op(K.dve, lambda: nc.vector.tensor_tensor(out=arT[d][:, c, :, 0:64], in0=v3(a_), in1=v3(Ea), op=ALU.mult), reads=[bib, bgt[4]], writes=[barT[d][c]])
                    K.op(K.pool, lambda: nc.gpsimd.tensor_tensor(out=arT[d][:, c, :, 64:128], in0=v3(r_), in1=v3(Er), op=ALU.mult), reads=[bib, bgt[2]], writes=[barT[d][c]], acc=True)
                    K.op(K.dve, lambda: nc.vector.tensor_tensor(out=bkT[d][:, c, :, 0:64], in0=v3(b_), in1=v3(Ei), op=ALU.mult), reads=[bib, bgt[3]], writes=[bbkT[d][c]])
                    K.op(K.pool, lambda: nc.gpsimd.tensor_tensor(out=bkT[d][:, c, :, 64:128], in0=v3(k_), in1=v3(Ei), op=ALU.mult), reads=[bib, bgt[3]], writes=[bbkT[d][c]], acc=True)
                for n in range(8):
                    for wh in range(2):
                        for hp in range(4):
                            for l in range(2):
                                K.op(K.pe, lambda: nc.tensor.transpose(out=pst[PL(l), hp * 128:(hp + 1) * 128], in_=bkT[d][:, hp, n, wh * 64:(wh + 1) * 64], identity=self.idb[:]),
                                     reads=[bbkT[d][hp], self.b_id], writes=[bpst], inc=(hp == 3 and l == 1), acc=not (hp == 0 and l == 0))
                        if wh == 0:
                            K.op(K.act, lambda: nc.scalar.copy(out=btk[d][:, n, wh, :], in_=pst[:, 0:512]), reads=[bpst], writes=[bbtk[d][n * 2 + wh]])
                        else:
                            K.op(K.dve, lambda: nc.vector.tensor_copy(out=btk[d][:, n, wh, :], in_=pst[:, 0:512]), reads=[bpst], writes=[bbtk[d][n * 2 + wh]])

            def grp_(n, mm, evac, width=64):
                pr = nps2()
                for l in range(2):
                    p, bp = pr[l]
                    for hh in range(4):
                        terms = mm(l, hh)
                        for ti, (l_, r_, lb, rb) in enumerate(terms):
                            K.op(K.pe, lambda: nc.tensor.matmul(p[PL(l), hh * width:(hh + 1) * width], lhsT=l_, rhs=r_, start=(ti == 0), stop=(ti == len(terms) - 1)),
                                 reads=lb + rb, writes=[bp], inc=(hh == 3 and ti == len(terms) - 1), acc=not (hh == 0 and ti == 0), lhs=lb)
                for l in range(2):
                    p, bp = pr[l]
                    evac(l, p[PL(l), 0:4 * width].rearrange("p (h t) -> p h t", t=width), bp)

            v64 = lambda ap: ap.rearrange("p (h t) -> p h t", t=64)

            def pre(d, n, q):
                XS, bX, MS, bM = Xs[d][q], bXs[d][q], Ms[d][q], bMs[d][q]
                mX = self.rmask(d)[:, 0:128].unsqueeze(1).broadcast_to([128, 4, 128])
                mA = self.rmask(d)[:, 128:192].unsqueeze(1).broadcast_to([128, 4, 64])
                idl4 = self.idl.unsqueeze(1).broadcast_to([128, 4, 64])
                AR, BK = arT[d], bkT[d]
                grp = lambda mm, evac, width=64: grp_(n, mm, evac, width)
                ar = lambda l, hh, c0: AR[PL(l), hh, n, c0:c0 + 64]
                bk = lambda l, hh, c0: BK[PL(l), hh, n, c0:c0 + 64]
                for wh in range(2):
                    def ev_X(l, pv, bp, wh=wh):
                        K.op(K.dve, lambda: nc.vector.tensor_tensor(out=XS[PL(l), wh, :, :], in0=pv, in1=mX[PL(l)], op=ALU.mult), reads=[bp, self.b_c2], writes=[bX[wh]], acc=(l > 0))
                    grp(lambda l, hh: [(bk(l, hh, wh * 64), AR[PL(l), hh, n, :], [bbkT[d][hh]], [barT[d][hh]])], ev_X, width=128)
                    yield

                def ev_A0(l, pv, bp):
                    K.op(K.act, lambda: nc.scalar.copy(out=As[d][PL(l), 0, :, :], in_=pv), reads=[bp], writes=[bAs[d][0]], acc=(l > 0))
                grp(lambda l, hh: [(ar(l, hh, 0), bk(l, hh, 0), [barT[d][hh]], [bbkT[d][hh]])], ev_A0)
                K.op(K.pool, lambda: nc.gpsimd.tensor_tensor(out=As[d][:, 0, :, :], in0=As[d][:, 0, :, :], in1=mA, op=ALU.mult), reads=[bAs[d][0], self.b_c2], writes=[bAs[d][0]])
                yield
                Ncur = lambda l, hh: XS[PL(l), 0, hh, 0:64]
                bNcur = [bX[0]]
                Acur = lambda l, hh: As[d][PL(l), 0, hh, :]
                bAcur = [bAs[d][0]]
                K.op(K.dve, lambda: nc.vector.tensor_tensor(out=MS[:], in0=XS[:, 0, :, 0:64], in1=idl4, op=ALU.add), reads=bNcur + [self.b_c2], writes=[bM])
                for lv in range(5):
                    o_ = (lv + 1) % 2
                    Nc, Ac, bNc, bAc = Ncur, Acur, bNcur, bAcur

                    def ev_A(l, pv, bp, o_=o_):
                        K.op(K.act, lambda: nc.scalar.copy(out=As[d][PL(l), o_, :, :], in_=pv), reads=[bp], writes=[bAs[d][o_]], acc=(l > 0))

                    def ev_N(l, pv, bp, o_=o_):
                        K.op(K.act, lambda: nc.scalar.copy(out=Ns[d][PL(l), o_, :, :], in_=pv), reads=[bp], writes=[bNs[d][o_]], acc=(l > 0))
                    grp(lambda l, hh: [(Nc(l, hh), Ac(l, hh), bNc, bAc)], ev_A)
                    yield
                    if lv < 4:
                        grp(lambda l, hh: [(Ac(l, hh), Nc(l, hh), bAc, bNc)], ev_N)
                        yield
                    Acur = (lambda oo: (lambda l, hh: As[d][PL(l), oo, hh, :]))(o_)
                    bAcur = [bAs[d][o_]]
                    Ncur = (lambda oo: (lambda l, hh: Ns[d][PL(l), oo, hh, :]))(o_)
                    bNcur = [bNs[d][o_]]
                    An, bAn = Acur, bAcur

                    def ev_M(l, pv, bp):
                        K.op(K.dve, lambda: nc.vector.tensor_tensor(out=MS[PL(l)], in0=MS[PL(l)], in1=pv, op=ALU.add), reads=[bM, bp], writes=[bM], acc=True)
                    grp(lambda l, hh: [(An(l, hh), MS[PL(l), hh, :], bAn, [bM])], ev_M)
                    yield

            def seq(d, tile, n, q, yi):
                XS, bX, MS, bM = Xs[d][q], bXs[d][q], Ms[d][q], bMs[d][q]
                AR = arT[d]
                grp = lambda mm, evac, width=64: grp_(n, mm, evac, width)
                ar = lambda l, hh, c0: AR[PL(l), hh, n, c0:c0 + 64]
                V = lambda l, hh: vt[d][PL(l), n, (2 * hh + l) * 64:(2 * hh + l + 1) * 64]
                St = lambda l, hh: Sb[d][PL(l), hh, :]

                def ev_W(l, pv, bp):
                    K.op(K.act, lambda: nc.scalar.copy(out=Ws[d][PL(l)], in_=pv), reads=[bp], writes=[bWs[d]], acc=(l > 0))
                grp(lambda l, hh: [(XS[PL(l), 1, hh, 0:64], V(l, hh), [bX[1]], [bvt[d]]),
                                   (ar(l, hh, 0), St(l, hh), [barT[d][hh]], [bSb[d]])], ev_W)
                yield

                def ev_U(l, pv, bp):
                    K.op(K.act, lambda: nc.scalar.copy(out=Us[d][PL(l)], in_=pv), reads=[bp], writes=[bUs[d]], acc=(l > 0))
                grp(lambda l, hh: [(MS[PL(l), hh, :], Ws[d][PL(l), hh, :], [bM], [bWs[d]])], ev_U)
                yield
                r0 = tile * TT + n * 64

                def ev_Y(l, pv, bp):
                    K.op(K.act, lambda: nc.scalar.copy(out=v64(yt[d][PL(l), yi % 2, :]), in_=pv), reads=[bp], writes=[byt[d][yi % 2]], acc=(l > 0))
                grp(lambda l, hh: [(ar(l, hh, 64), St(l, hh), [barT[d][hh]], [bSb[d]]),
                                   (XS[PL(l), 0, hh, 64:128], Us[d][PL(l), hh, :], [bX[0]], [bUs[d]]),
                                   (XS[PL(l), 1, hh, 64:128], V(l, hh), [bX[1]], [bvt[d]])], ev_Y)
                for l in range(2):
                    K.dma(K.pool, self.rwy[d, r0:r0 + 64, :].rearrange("t (hh two i) -> t hh two i", two=2, i=64)[:, :, l, :], v64(yt[d][PL(l), yi % 2, :]),
                          reads=[byt[d][yi % 2]], writes=[self.db["rwy%d" % d]], acc=True, owner=byt[d][yi % 2])
                yield

                def ev_S(l, pv, bp):
                    K.op(K.dve, lambda: nc.vector.tensor_tensor(out=v64(stmp[PL(l), d, :]), in0=pv, in1=S32[d][PL(l)], op=ALU.add), reads=[bp, bS32[d]], writes=[bstmp[d]], acc=(l > 0))
                grp(lambda l, hh: [(btk[d][PL(l), n, 0, (2 * hh + l) * 64:(2 * hh + l + 1) * 64], Us[d][PL(l), hh, :], [bbtk[d][n * 2]], [bUs[d]]),
                                   (btk[d][PL(l), n, 1, (2 * hh + l) * 64:(2 * hh + l + 1) * 64], V(l, hh), [bbtk[d][n * 2 + 1]], [bvt[d]])], ev_S)
                K.op(K.dve, lambda: nc.vector.tensor_tensor(out=S32[d][:], in0=v64(stmp[:, d, :]), in1=et[d][:, :, n:n + 1].broadcast_to([128, 4, 64]), op=ALU.mult),
                     reads=[bstmp[d], bet[d]], writes=[bS32[d]])
                K.op(K.act, lambda: nc.scalar.copy(out=Sb[d][:], in_=S32[d][:]), reads=[bS32[d]], writes=[bSb[d]])
                yield

            def run_rr(gens):
                gens = list(gens)
                while gens:
                    for g in list(gens):
                        try:
                            next(g)
                        except StopIteration:
                            gens.remove(g)

            yi = [0, 0]
            cn = lambda d, i: i if d == 0 else 7 - i
            for step in range(NT):
                tiles = [step, NT - 1 - step]
                for d in R2:
                    prep(d, tiles[d])
                run_rr([pre(d, cn(d, 0), 0) for d in R2])
                for i in range(8):
                    gens = [seq(d, tiles[d], cn(d, i), i % 2, yi[d]) for d in R2]
                    if i + 1 < 8:
                        gens = [g for pair in zip(gens, [pre(d, cn(d, i + 1), (i + 1) % 2) for d in R2]) for g in pair]
                    run_rr(gens)
                    for d in R2:
                        yi[d] += 1

    def p3c(self, slot):
        nc, K = self.nc, self.K
        K.begin_scope("p3c")
        with ExitStack() as es:
            sb = lambda n, sh, dt: self.sbt(es, "p3c_%d_" % slot + n, sh, dt)
            yf = sb("yf", [128, 4, 512], F32); byf = K.buf("yf")
            yb = sb("yb", [128, 4, 512], F32); byb = K.buf("yb")
            sq = sb("sq", [128, 4, 512], F32); bsq = K.buf("sq")
            stt = sb("stt", [128, 2, 32], F32); bstt = K.buf("stt")
            bon = sb("bon", [128, 4, TT], F32); bbon = K.buf("bon")
            gg = sb("gg", [128, 4, TT], F32); bgg = K.buf("gg")
            t1 = sb("t1", [128, 2, TT], F32); bt1 = K.bufs(2, "t1")
            ro = sb("ro", [128, 4, TT], BF16); bro = K.bufs(4, "ro")
            ps = [self.pst(es, "p3c_%d_ps%%d" % slot % i, [128, TT], F32) for i in range(2)]; bps = K.bufs(2, "ps3c")
            g3 = lambda ap: ap.rearrange("p n (h i) -> p (n h) i", i=64)
            for t in range(self.NT):
                t0 = t * TT
                K.dma(K.sp, yf[:], self.rwy[0, t0:t0 + TT].rearrange("(n p) c -> p n c", p=128), reads=[self.db["rwy0"]], writes=[byf])
                K.dma(K.sp, yb[:], self.rwy[1, t0:t0 + TT].rearrange("(n p) c -> p n c", p=128), reads=[self.db["rwy1"]], writes=[byb])
                K.dma(K.sp, bon[:], self.rwbon.rearrange("(c p) s -> p c s", p=128)[:, :, t0:t0 + TT], reads=[self.db["rwbon"]], writes=[bbon])
                K.dma(K.sp, gg[:], self.rwg.rearrange("(c p) s -> p c s", p=128)[:, :, t0:t0 + TT], reads=[self.db["rwg"]], writes=[bgg])
                K.op(K.dve, lambda: nc.vector.tensor_tensor(out=yf[:], in0=yf[:], in1=yb[:], op=ALU.add), reads=[byf, byb], writes=[byf])
                mean, var = stt[:, 0, :], stt[:, 1, :]
                K.op(K.dve, lambda: nc.vector.tensor_reduce(out=mean, in_=g3(yf[:]), axis=AX.X, op=ALU.add), reads=[byf], writes=[bstt])
                K.op(K.dve, lambda: nc.vector.tensor_scalar(out=mean, in0=mean, scalar1=1.0 / 64, scalar2=None, op0=ALU.mult), reads=[bstt], writes=[bstt])
                K.op(K.dve, lambda: nc.vector.tensor_tensor(out=g3(yf[:]), in0=g3(yf[:]), in1=mean.unsqueeze(2).broadcast_to([128, 32, 64]), op=ALU.subtract), reads=[byf, bstt], writes=[byf])
                K.op(K.act, lambda: nc.scalar.activation(out=sq[:], in_=yf[:], func=AF.Square), reads=[byf], writes=[bsq])
                K.op(K.dve, lambda: nc.vector.tensor_reduce(out=var, in_=g3(sq[:]), axis=AX.X, op=ALU.add), reads=[bsq], writes=[bstt], acc=True)
                K.op(K.act, lambda: nc.scalar.activation(out=var, in_=var, func=AF.Ln, scale=1.0 / 64, bias=self.epsc[:, 2:3]), reads=[bstt, self.b_ones], writes=[bstt], acc=True)
                K.op(K.act, lambda: nc.scalar.activation(out=var, in_=var, func=AF.Exp, scale=-0.5), reads=[bstt], writes=[bstt], acc=True)
                K.op(K.dve, lambda: nc.vector.tensor_tensor(out=g3(yf[:]), in0=g3(yf[:]), in1=var.unsqueeze(2).broadcast_to([128, 32, 64]), op=ALU.mult), reads=[byf, bstt], writes=[byf])
                for c in range(4):
                    p, bp = ps[c % 2], bps[c % 2]
                    for n in range(4):
                        K.op(K.pe, lambda: nc.tensor.transpose(out=p[:, n * 128:(n + 1) * 128], in_=yf[:, n, c * 128:(c + 1) * 128], identity=self.idf[:]),
                             reads=[byf, self.b_id], writes=[bp], inc=(n == 3), acc=(n > 0))
                    tt_, btt = t1[:, c % 2, :], bt1[c % 2]
                    K.op(K.act, lambda: nc.scalar.activation(out=tt_, in_=p[:], func=AF.Identity, scale=self.col("lnx_g", c), bias=self.col("lnx_b", c)), reads=[bp, self.b_pc], writes=[btt])
                    K.op(K.dve, lambda: nc.vector.tensor_tensor(out=tt_, in0=tt_, in1=bon[:, c, :], op=ALU.add), reads=[btt, bbon], writes=[btt])
                    K.op(K.pool, lambda: nc.gpsimd.tensor_tensor(out=ro[:, c, :], in0=tt_, in1=gg[:, c, :], op=ALU.mult), reads=[btt, bgg], writes=[bro[c]])
                K.dma(K.pool, self.rwo.rearrange("(c p) s -> p c s", p=128)[:, :, t0:t0 + TT], ro[:], reads=bro, writes=[self.db["rwo"]], acc=True, owner=bro[0])

    def p4m(self, slot):
        nc, K = self.nc, self.K
        K.begin_scope("p4m")
        with ExitStack() as es:
            sb = lambda n, sh, dt: self.sbt(es, "p4m_%d_" % slot + n, sh, dt)
            m = sb("m", [128, 2, D], F32); bm = K.buf("m")
            sq = sb("sq", [128, 2, D], F32); bsq = K.buf("sq")
            gb = sb("gb", [128, 2, D], F32); bgb = K.buf("gb")
            stt = sb("stt", [128, 2, 2], F32); bstt = K.buf("stt")
            mT = sb("mT", [128, 8, 256], BF16); bmT = K.bufs(8, "mT")
            wck = sb("wck", [128, 8, 1024], BF16); bwck = K.buf("wck")
            km = sb("km", [128, 4, 256], BF16); bkm = K.bufs(4, "km")
            vm = sb("vm", [128, 2, 516], BF16); bvm = K.buf("vm")
            ps = [self.pst(es, "p4m_%d_ps%%d" % slot % i, [128, TT], F32) for i in range(2)]; bps = K.bufs(2, "ps4m")
            K.dma(K.sp, m[:], self.mem[slot].rearrange("(n p) d -> p n d", p=128), writes=[bm])
            K.dma(K.sp, gb[:, 0, :], self.memgb[0:1, :].partition_broadcast(128), writes=[bgb])
            K.dma(K.sp, gb[:, 1, :], self.memgb[1:2, :].partition_broadcast(128), writes=[bgb], acc=True)
            K.dma(K.sp, wck[:], self.ws["wckv"], reads=[self.db["wckv"]], writes=[bwck])
            K.op(K.pool, lambda: nc.gpsimd.memset(vm[:], 1.0), writes=[bvm])
            mean, var = stt[:, 0, :], stt[:, 1, :]
            K.op(K.dve, lambda: nc.vector.tensor_reduce(out=mean, in_=m[:], axis=AX.X, op=ALU.add), reads=[bm], writes=[bstt])
            K.op(K.dve, lambda: nc.vector.tensor_scalar(out=mean, in0=mean, scalar1=1.0 / D, scalar2=None, op0=ALU.mult), reads=[bstt], writes=[bstt])
            K.op(K.dve, lambda: nc.vector.tensor_tensor(out=m[:], in0=m[:], in1=mean.unsqueeze(2).broadcast_to([128, 2, D]), op=ALU.subtract), reads=[bm, bstt], writes=[bm])
            K.op(K.act, lambda: nc.scalar.activation(out=sq[:], in_=m[:], func=AF.Square), reads=[bm], writes=[bsq])
            K.op(K.dve, lambda: nc.vector.tensor_reduce(out=var, in_=sq[:], axis=AX.X, op=ALU.add), reads=[bsq], writes=[bstt], acc=True)
            K.op(K.act, lambda: nc.scalar.activation(out=var, in_=var, func=AF.Ln, scale=1.0 / D, bias=self.epsc[:, 0:1]), reads=[bstt, self.b_ones], writes=[bstt], acc=True)
            K.op(K.act, lambda: nc.scalar.activation(out=var, in_=var, func=AF.Exp, scale=-0.5), reads=[bstt], writes=[bstt], acc=True)
            K.op(K.dve, lambda: nc.vector.tensor_tensor(out=m[:], in0=m[:], in1=var.unsqueeze(2).broadcast_to([128, 2, D]), op=ALU.mult), reads=[bm, bstt], writes=[bm])
            K.op(K.dve, lambda: nc.vector.tensor_tensor(out=m[:], in0=m[:], in1=gb[:, 0:1, :].broadcast_to([128, 2, D]), op=ALU.mult), reads=[bm, bgb], writes=[bm])
            K.op(K.dve, lambda: nc.vector.tensor_tensor(out=m[:], in0=m[:], in1=gb[:, 1:2, :].broadcast_to([128, 2, D]), op=ALU.add), reads=[bm, bgb], writes=[bm])
            for kc in range(8):
                p, bp = ps[kc % 2], bps[kc % 2]
                for n in range(2):
                    K.op(K.pe, lambda: nc.tensor.transpose(out=p[:, n * 128:(n + 1) * 128], in_=m[:, n, kc * 128:(kc + 1) * 128], identity=self.idf[:]),
                         reads=[bm, self.b_id], writes=[bp], inc=(n == 1), acc=(n > 0))
                K.op(K.act, lambda: nc.scalar.copy(out=mT[:, kc, :], in_=p[:, 0:256]), reads=[bp], writes=[bmT[kc]])
            for hc in range(4):
                p, bp = ps[hc % 2], bps[hc % 2]
                for kc in range(8):
                    K.op(K.pe, lambda: nc.tensor.matmul(p[:, 0:256], lhsT=wck[:, kc, hc * 128:(hc + 1) * 128], rhs=mT[:, kc, :], start=(kc == 0), stop=(kc == 7)),
                         reads=[bwck, bmT[kc]], writes=[bp], inc=(kc == 7), acc=(kc > 0))
                K.op(K.act, lambda: nc.scalar.copy(out=km[:, hc, :], in_=p[:, 0:256]), reads=[bp], writes=[bkm[hc]])
            for mt in range(2):
                p, bp = ps[mt % 2], bps[mt % 2]
                for kc in range(8):
                    K.op(K.pe, lambda: nc.tensor.matmul(p[:], lhsT=mT[:, kc, mt * 128:(mt + 1) * 128], rhs=wck[:, kc, 512:1024], start=(kc == 0), stop=(kc == 7)),
                         reads=[bwck, bmT[kc]], writes=[bp], inc=(kc == 7), acc=(kc > 0))
                K.op(K.dve, lambda: nc.vector.tensor_copy(out=vm[:, mt, :].rearrange("p (h d) -> p h d", d=129)[:, :, 0:128], in_=p[:].rearrange("p (h d) -> p h d", d=128)),
                     reads=[bp], writes=[bvm], acc=True)
            K.dma(K.pool, self.kmT, km[:], reads=bkm, writes=[self.db["kmT"]], owner=bkm[0])
            K.dma(K.pool, self.vms, vm[:], reads=[bvm], writes=[self.db["vms"]], owner=bvm)

    def p4(self, slot):
        nc, K = self.nc, self.K
        K.begin_scope("p4")
        scale_c = float(128 ** -0.5)
        with ExitStack() as es:
            sb = lambda n, sh, dt: self.sbt(es, "p4_%d_" % slot + n, sh, dt)
            xr = sb("xr", [128, 8, TT], F32); bxr = K.bufs(8, "xr")
            z = sb("z", [128, 8, TT], F32); bz = K.bufs(8, "z")
            xb = sb("xb", [128, 8, TT], BF16); bxb = K.bufs(8, "xb")
            st = sb("st", [128, TT], F32); bst = K.buf("st")
            st2 = sb("st2", [128, TT], F32); bst2 = K.buf("st2")
            ytok = sb("ytok", [128, 4, D], F32); bsq = K.bufs(8, "ytok_sq")
            sqf = lambda c: ytok[:, c // 2, (c % 2) * TT:(c % 2 + 1) * TT]
            hT = sb("hT", [128, NFC, TT], BF16); bhT = K.bufs(NFC, "hT")
            wgu = [sb("wgu%d" % i, [128, 8, 512], BF16) for i in range(2)]; bwgu = K.bufs(2, "wgu")
            wdn = [sb("wdn%d" % i, [128, NFC, 128], BF16) for i in range(2)]; bwdn = K.bufs(2, "wdn")
            OTt = sb("OTt", [128, 4, TT], BF16); bOT = K.buf("OTt")
            RWt = sb("RWt", [128, 4, TT], BF16); bRW = K.buf("RWt")
            gts = sb("gts", [128, 2, 2, TT], F32); bgts = K.bufs(2, "gts")
            tmpf = sb("tmpf", [128, 2, TT], F32); btmp = K.bufs(2, "tmpf")
            qc = sb("qc", [128, 4, TT], BF16); bqc = K.bufs(4, "qc")
            km = sb("km", [128, 4, 256], BF16); bkm = K.buf("km")
            vm = sb("vm", [128, 2, 516], BF16); bvm = K.buf("vm")
            PTc = [sb("PTc%d" % i, [128, TT], BF16) for i in range(2)]; bPTc = K.bufs(2, "PTc")
            oct_ = sb("oct", [128, 4, 512], BF16); boct = K.bufs(4, "oct")
            ocT = sb("ocT", [128, 4, TT], BF16); bocT = K.bufs(4, "ocT")
            rs = sb("rs", [128, 4], F32); brs = K.buf("rs")
            psA = [self.pst(es, "p4_%d_ps%%d" % slot % i, [128, TT], F32) for i in range(4)]; bpsA = K.bufs(4, "psA4")
            psS = self.pst(es, "p4_%d_pss" % slot, [128, TT], F32); bpsS = K.buf("psS4")
            psO = [self.pst(es, "p4_%d_pso%%d" % slot % i, [128, TT], F32) for i in range(2)]; bpsO = K.bufs(2, "psO4")
            psT = self.pst(es, "p4_%d_pst" % slot, [128, 2 * TT], BF16); bpsT = K.buf("psT4")
            K.dma(K.sp, km[:], self.kmT, reads=[self.db["kmT"]], writes=[bkm])
            K.dma(K.sp, vm[:], self.vms, reads=[self.db["vms"]], writes=[bvm])
            w4 = lambda wt: wt[:].rearrange("p a b -> p (a b)").rearrange("p (k c) -> p k c", k=4)
            pi = [0]

            def nps():
                i = pi[0] % 4
                pi[0] += 1
                return psA[i], bpsA[i]

            for t in range(self.NT):
                t0 = t * TT
                K.dma(K.sp, OTt[:], self.OT.rearrange("(c p) s -> p c s", p=128)[:, :, t0:t0 + TT], reads=[self.db["OT"]], writes=[bOT])
                K.dma(K.sp, RWt[:], self.rwo.rearrange("(c p) s -> p c s", p=128)[:, :, t0:t0 + TT], reads=[self.db["rwo"]], writes=[bRW])
                K.dma(K.sp, xr[:], self.x1T.rearrange("(c p) s -> p c s", p=128)[:, :, t0:t0 + TT], reads=[self.db["x1T"]], writes=bxr)
                K.dma(K.sp, wgu[0][:], self.ws["pmla"].rearrange("p k (a b) -> p (k a) b", a=2), reads=[self.db["pmla"]], writes=[bwgu[0]])
                K.dma(K.sp, wgu[1][:], self.ws["prwkv"].rearrange("p k (a b) -> p (k a) b", a=2), reads=[self.db["prwkv"]], writes=[bwgu[1]])
                WA, WB = w4(wgu[0]), w4(wgu[1])
                for c in range(8):
                    g_, bg = gts[:, c % 2], bgts[c % 2]
                    K.dma(K.sp, g_[:, 0, :], self.gates[c * 128:(c + 1) * 128, t0:t0 + TT], reads=[self.db["gates"]], writes=[bg])
                    K.dma(K.sp, g_[:, 1, :], self.gates[1024 + c * 128:1024 + (c + 1) * 128, t0:t0 + TT], reads=[self.db["gates"]], writes=[bg], acc=True)
                    pa, bpa = nps()
                    pb, bpb = nps()
                    for k in range(4):
                        K.op(K.pe, lambda: nc.tensor.matmul(pa[:], lhsT=WA[:, k, c * 128:(c + 1) * 128], rhs=OTt[:, k, :], start=(k == 0), stop=(k == 3)),
                             reads=[bwgu[0], bOT], writes=[bpa], inc=(k == 3), acc=(k > 0))
                    for k in range(4):
                        K.op(K.pe, lambda: nc.tensor.matmul(pb[:], lhsT=WB[:, k, c * 128:(c + 1) * 128], rhs=RWt[:, k, :], start=(k == 0), stop=(k == 3)),
                             reads=[bwgu[1], bRW], writes=[bpb], inc=(k == 3), acc=(k > 0))
                    K.op(K.dve, lambda: nc.vector.tensor_tensor(out=tmpf[:, 0, :], in0=pa[:], in1=g_[:, 0, :], op=ALU.mult), reads=[bpa, bg], writes=[btmp[0]])
                    K.op(K.dve, lambda: nc.vector.tensor_tensor(out=tmpf[:, 1, :], in0=pb[:], in1=g_[:, 1, :], op=ALU.mult), reads=[bpb, bg], writes=[btmp[1]])
                    K.op(K.pool, lambda: nc.gpsimd.tensor_tensor(out=xb[:, c, :], in0=tmpf[:, 0, :], in1=tmpf[:, 1, :], op=ALU.add), reads=btmp, writes=[bxb[c]])
                for half in range(2):
                    K.dma(K.sp, wgu[half][:], self.ws["wo"][:, :, half * 512:(half + 1) * 512], reads=[self.db["wo"]], writes=[bwgu[half]])
                for c in range(8):
                    wt, bwt = wgu[c // 4], bwgu[c // 4]
                    pz, bpz = nps()
                    for k in range(8):
                        K.op(K.pe, lambda: nc.tensor.matmul(pz[:], lhsT=wt[:, k, (c % 4) * 128:(c % 4 + 1) * 128], rhs=xb[:, k, :], start=(k == 0), stop=(k == 7)),
                             reads=[bwt, bxb[k]], writes=[bpz], inc=(k == 7), acc=(k > 0))
                    K.op(K.dve, lambda: nc.vector.scalar_tensor_tensor(out=z[:, c, :], in0=xr[:, c, :], scalar=ALPHA, in1=pz[:], op0=ALU.mult, op1=ALU.add),
                         reads=[bpz, bxr[c]], writes=[bz[c]])
                self.ln_fm(z, bz, "ln2_g", "ln2_b", xr, bxr, xb, bxb, psS, bpsS, sqf, bsq, st, bst, psA[3], bpsA[3], st2, bst2)
                K.dma(K.sp, wgu[0][:], self.ws["wcq"], reads=[self.db["wcq"]], writes=[bwgu[0]])
                K.dma(K.sp, wgu[1][:], self.ws["wco"].rearrange("p k (a b) -> p (k a) b", a=2), reads=[self.db["wco"]], writes=[bwgu[1]])
                for hc in range(4):
                    p, bp = nps()
                    for k in range(8):
                        K.op(K.pe, lambda: nc.tensor.matmul(p[:], lhsT=wgu[0][:, k, hc * 128:(hc + 1) * 128], rhs=xb[:, k, :], start=(k == 0), stop=(k == 7)),
                             reads=[bwgu[0], bxb[k]], writes=[bp], inc=(k == 7), acc=(k > 0))
                    K.op(K.act, lambda: nc.scalar.copy(out=qc[:, hc, :], in_=p[:]), reads=[bp], writes=[bqc[hc]])
                si = 0
                for hc in range(4):
                    for mt in range(2):
                        pS_, bpS_ = nps()
                        P_, bP = PTc[si % 2], bPTc[si % 2]
                        si += 1
                        K.op(K.pe, lambda: nc.tensor.matmul(pS_[:], lhsT=km[:, hc, mt * 128:(mt + 1) * 128], rhs=qc[:, hc, :], start=True, stop=True), reads=[bkm, bqc[hc]], writes=[bpS_])
                        K.op(K.act, lambda: nc.scalar.activation(out=P_[:], in_=pS_[:], func=AF.Exp, scale=scale_c), reads=[bpS_], writes=[bP])
                        for qs in range(4):
                            po, bpo = psO[qs // 2], bpsO[qs // 2]
                            K.op(K.pe, lambda: nc.tensor.matmul(po[:, (qs % 2) * 129:(qs % 2 + 1) * 129], lhsT=P_[:, qs * 128:(qs + 1) * 128], rhs=vm[:, mt, hc * 129:(hc + 1) * 129],
                                                                 start=(mt == 0 and qs % 2 == 0), stop=(mt == 1 and qs % 2 == 1)),
                                 reads=[bP, bvm], writes=[bpo], inc=(qs % 2 == 1), acc=not (mt == 0 and qs % 2 == 0))
                    for qs in range(4):
                        po, bpo = psO[qs // 2], bpsO[qs // 2]
                        o0 = (qs % 2) * 129
                        K.op(K.dve, lambda: nc.vector.reciprocal(out=rs[:, qs:qs + 1], in_=po[:, o0 + 128:o0 + 129]), reads=[bpo], writes=[brs], acc=(qs > 0))
                        K.op(K.dve, lambda: nc.vector.tensor_scalar(out=oct_[:, qs, hc * 128:(hc + 1) * 128], in0=po[:, o0:o0 + 128], scalar1=rs[:, qs:qs + 1], scalar2=None, op0=ALU.mult),
                             reads=[bpo, brs], writes=[boct[hc]], acc=(qs > 0))
                for hc in range(4):
                    for qs in range(4):
                        K.op(K.pe, lambda: nc.tensor.transpose(out=psT[:, qs * 128:(qs + 1) * 128], in_=oct_[:, qs, hc * 128:(hc + 1) * 128], identity=self.idb[:]),
                             reads=[boct[hc], self.b_id], writes=[bpsT], inc=(qs == 3), acc=(qs > 0))
                    K.op(K.dve, lambda: nc.vector.tensor_copy(out=ocT[:, hc, :], in_=psT[:, 0:TT]), reads=[bpsT], writes=[bocT[hc]])
                WC = w4(wgu[1])
                for c in range(8):
                    pz, bpz = nps()
                    for k in range(4):
                        K.op(K.pe, lambda: nc.tensor.matmul(pz[:], lhsT=WC[:, k, c * 128:(c + 1) * 128], rhs=ocT[:, k, :], start=(k == 0), stop=(k == 3)),
                             reads=[bwgu[1], bocT[k]], writes=[bpz], inc=(k == 3), acc=(k > 0))
                    K.op(K.dve, lambda: nc.vector.scalar_tensor_tensor(out=z[:, c, :], in0=xr[:, c, :], scalar=ALPHA, in1=pz[:], op0=ALU.mult, op1=ALU.add),
                         reads=[bpz, bxr[c]], writes=[bz[c]])
                self.ln_fm(z, bz, "ln3_g", "ln3_b", xr, bxr, xb, bxb, psS, bpsS, sqf, bsq, st, bst, psA[3], bpsA[3], st2, bst2)
                for c in range(8):
                    K.op(K.pool, lambda: nc.gpsimd.tensor_scalar(out=xr[:, c, :], in0=xr[:, c, :], scalar1=ALPHA, scalar2=None, op0=ALU.mult), reads=[bxr[c]], writes=[bxr[c]])
                self.ffn(1, xb, bxb, xr, bxr, hT, bhT, wgu, bwgu, wdn, bwdn, psA, bpsA, z, bz)
                self.ln_fm(z, bz, "ln4_g", "ln4_b", xr, bxr, None, None, psS, bpsS, sqf, bsq, st, bst, psA[3], bpsA[3], st2, bst2)
                for n in range(4):
                    for hf in range(2):
                        p, bp = nps()
                        for cc in range(4):
                            c = hf * 4 + cc
                            K.op(K.pe, lambda: nc.tensor.transpose(out=p[:, cc * 128:(cc + 1) * 128], in_=xr[:, c, n * 128:(n + 1) * 128], identity=self.idf[:]),
                                 reads=[bxr[c], self.b_id], writes=[bp], inc=(cc == 3), acc=(cc > 0))
                        if hf == 0:
                            K.op(K.act, lambda: nc.scalar.copy(out=ytok[:, n, 0:512], in_=p[:]), reads=[bp], writes=[bsq[2 * n]])
                        else:
                            K.op(K.dve, lambda: nc.vector.tensor_copy(out=ytok[:, n, 512:1024], in_=p[:]), reads=[bp], writes=[bsq[2 * n + 1]])
                K.dma(K.pool, self.y[slot, t0:t0 + TT, :].rearrange("(n p) d -> p n d", p=128), ytok[:], reads=bsq, writes=[self.db["y"]], acc=True, owner=bsq[0])

def _prep_consts(S):
    inv = 1.0 / (10000.0 ** (np.arange(0, 32, 2, dtype=np.float32) / 32.0))
    ang = np.arange(S, dtype=np.float32)[:, None] * inv[None, :].astype(np.float32)
    c, s = np.cos(ang).astype(np.float32), np.sin(ang).astype(np.float32)
    ropec = np.concatenate([c, c], 1).T.copy()
    ropes = np.concatenate([-s, s], 1).T.copy()
    return ropec, ropes


def _cst2():
    c = np.zeros((128, 128 + 512 + 384 + 64), np.float32)
    c[0:64, 0:64] = 1.0
    c[64:128, 64:128] = 1.0
    cm = np.ones(512, np.float32)
    cm[::64] = 0.0
    c[:, 128:640] = cm[None, :]
    j = np.arange(64)[:, None]
    t = np.arange(64)[None, :]
    for d in range(2):
        strict = (j < t) if d == 0 else (j > t)
        incl = (j <= t) if d == 0 else (j >= t)
        base = 640 + d * 192
        for l0 in (0, 64):
            c[l0:l0 + 64, base:base + 64] = strict
            c[l0:l0 + 64, base + 64:base + 128] = incl
            c[l0:l0 + 64, base + 128:base + 192] = strict.T
    c[0:64, 1024:1088] = np.eye(64)
    c[64:128, 1024:1088] = np.eye(64)
    return c


def make_in_map(p, x, mem, S):
    ropec, ropes = _prep_consts(S)
    im = {"x": np.ascontiguousarray(x, np.float32), "mem": np.ascontiguousarray(mem, np.float32),
          "pcols": _pack_cols(p), "ident": np.eye(128, dtype=np.float32), "cst2": _cst2(), "ropec": ropec, "ropes": ropes,
          "memgb": np.stack([p["mem_g"][0], p["mem_b"][0]]).astype(np.float32)}
    im.update(_layout_weights(p))
    return im


_SEQ_MAP = None


def _slot_map():
    m = []
    seqs = [("p", i) for i in range(16)] + [("s", i) for i in range(4)]
    k = 0
    for c in range(NCORE):
        n = 3 if c < 4 else 2
        sl = seqs[k:k + n]
        k += n
        while len(sl) < NSLOT:
            sl = sl + [sl[-1]]
        m.append(sl)
    return m


def kernel(**inputs):
    S = 4096
    p = {k: np.asarray(v) for k, v in inputs.items() if k not in ("x_prompt", "x_sample", "mem_prompt", "mem_sample")}
    xs = {"p": np.asarray(inputs["x_prompt"], np.float32), "s": np.asarray(inputs["x_sample"], np.float32)}
    ms = {"p": np.asarray(inputs["mem_prompt"], np.float32), "s": np.asarray(inputs["mem_sample"], np.float32)}
    smap = _slot_map()
    B = Builder(S, NSLOT, debug=False)
    nc = B.build()
    shared = make_in_map(p, np.zeros((0,), np.float32), np.zeros((0,), np.float32), S)
    in_maps = []
    for c in range(NCORE):
        im = dict(shared)
        im["x"] = np.ascontiguousarray(np.stack([xs[g][i] for g, i in smap[c]]))
        im["mem"] = np.ascontiguousarray(np.stack([ms[g][i] for g, i in smap[c]]))
        in_maps.append(im)
    res = run_bass_kernel_spmd(nc, in_maps, core_ids=list(range(NCORE)))
    yp = np.zeros_like(xs["p"])
    ysm = np.zeros_like(xs["s"])
    done = set()
    for c in range(NCORE):
        y = res.results[c]["y"]
        for sl, (g, i) in enumerate(smap[c]):
            if (g, i) in done:
                continue
            done.add((g, i))
            (yp if g == "p" else ysm)[i] = y[sl]
    return (yp, ysm)
```

```python
import os
import numpy as np
import concourse.bass as bass
import concourse.mybir as mybir
from concourse.bass_utils import run_bass_kernel_spmd
from contextlib import ExitStack

F32 = mybir.dt.float32
BF16 = mybir.dt.bfloat16
AF = mybir.ActivationFunctionType
ALU = mybir.AluOpType
AX = mybir.AxisListType

D = 1024
DFF = 2816
NFC = 22
TT = 512
H = 8
ALPHA = float(2 ** 0.25)
LN_EPS = 1e-5
RMS_EPS = 1e-6
GN_EPS = 64e-5
WCONST = float(np.exp(-0.5))
OFF_KV = 384
OFF_RWKV = 672
OFF_GATE = 2528
NCORE = 8
NSLOT = 3


class Ev:
    __slots__ = ("sem", "val", "key")

    def __init__(self, sem, key, val=None):
        self.sem = sem
        self.key = key
        self.val = val


class Buf:
    __slots__ = ("name", "w", "r", "dsem", "dkey", "dcount", "psum")

    def __init__(self, name):
        self.name = name
        self.psum = name.startswith("ps")
        self.w = {}
        self.r = {}
        self.dsem = None
        self.dkey = None
        self.dcount = 0


class Iss:
    def __init__(self, K, name, eng):
        self.name = name
        self.eng = eng
        self.sem = K.new_sem("e_" + name)
        self.key = "e_" + name
        self.count = 0
        self.seen = {}
        self.cur = Ev(self.sem, self.key)
        self.ninstr = 0


class Kern:
    def __init__(self, nc, es):
        self.nc = nc
        self.es = es
        self.nsem = 0
        self.pe = Iss(self, "pe", nc.tensor)
        self.act = Iss(self, "act", nc.scalar)
        self.dve = Iss(self, "dve", nc.vector)
        self.pool = Iss(self, "pool", nc.gpsimd)
        self.sp = Iss(self, "sp", nc.sync)
        self.all = [self.pe, self.act, self.dve, self.pool, self.sp]
        self.dbufs = []
        self.nbuf = 0

    def new_sem(self, name):
        self.nsem += 1
        return self.es.enter_context(self.nc.semaphore(name))

    def begin_scope(self, scope):
        self.scope = scope
        self.scnt = {}

    def buf(self, name=None):
        name = name or "b"
        scope = getattr(self, "scope", None)
        if scope is None:
            self.nbuf += 1
            return Buf("%s_%d" % (name, self.nbuf))
        k = self.scnt.get(name, 0)
        self.scnt[name] = k + 1
        key = (scope, name, k)
        cache = self.__dict__.setdefault("bcache", {})
        if key not in cache:
            self.nbuf += 1
            cache[key] = Buf("%s_%d" % (name, self.nbuf))
        return cache[key]

    def bufs(self, n, name="b"):
        return [self.buf("%s%d" % (name, i)) for i in range(n)]

    def _wait(self, iss, ev):
        assert ev.val is not None, "waiting on unresolved event (%s)" % ev.key
        if iss.seen.get(ev.key, 0) >= ev.val:
            return
        iss.eng.wait_ge(ev.sem, ev.val)
        iss.seen[ev.key] = ev.val
        iss.ninstr += 1

    def _need(self, iss, ev, out):
        assert ev.val is not None, "waiting on unresolved event (%s)" % ev.key
        if iss.seen.get(ev.key, 0) >= ev.val:
            return
        iss.seen[ev.key] = ev.val
        for i, o in enumerate(out):
            if o.key == ev.key:
                if o.val < ev.val:
                    out[i] = ev
                return
        out.append(ev)

    def _deps(self, iss, reads, writes, acc, inline=False):
        out = []
        for b in reads:
            for ev in b.w.values():
                self._need(iss, ev, out)
            if b.psum:
                for ev in b.r.values():
                    if ev.key != iss.key:
                        self._need(iss, ev, out)
        for b in writes:
            for ev in b.r.values():
                self._need(iss, ev, out)
            if not acc:
                for ev in b.w.values():
                    self._need(iss, ev, out)
        last = out.pop() if (inline and out) else None
        for ev in out:
            iss.eng.wait_ge(ev.sem, ev.val)
            iss.ninstr += 1
        return last

    def op(self, iss, fn, reads=(), writes=(), inc=True, acc=False, lhs=None):
        last = self._deps(iss, reads, writes, acc, inline=True)
        ins = fn()
        if last is not None:
            ins.wait_op(last.sem, last.val, "sem-ge")
        iss.ninstr += 1
        ev = iss.cur
        if inc:
            iss.count += 1
            ins.then_inc(iss.sem, 1)
            ev.val = iss.count
            iss.cur = Ev(iss.sem, iss.key)
        for b in reads:
            b.r[ev.key] = ev
        for b in writes:
            if not acc:
                b.w = {}
                b.r = {}
            b.w[ev.key] = ev
        return ins

    def dma(self, iss, out, in_, reads=(), writes=(), acc=False, owner=None, **kw):
        self._deps(iss, reads, writes, acc)
        b0 = owner if owner is not None else writes[0]
        if b0.dsem is None:
            b0.dkey = "d_%s" % b0.name
            b0.dsem = self.new_sem(b0.dkey)
            self.dbufs.append(b0)
        ins = iss.eng.dma_start(out=out, in_=in_, **kw)
        iss.ninstr += 1
        b0.dcount += 16
        ins.then_inc(b0.dsem, 16)
        ev = Ev(b0.dsem, b0.dkey, b0.dcount)
        for b in reads:
            b.r[ev.key] = ev
        for b in writes:
            if not acc:
                b.w = {}
                b.r = {}
            b.w[ev.key] = ev
        return ins

    def barrier(self):
        for iss in self.all:
            for o in self.all:
                if o is not iss and o.count > 0:
                    self._wait(iss, Ev(o.sem, o.key, o.count))
            for b in self.dbufs:
                if b.dcount > 0:
                    self._wait(iss, Ev(b.dsem, b.dkey, b.dcount))


def _col_layout():
    off = {}
    n = 0
    for name, c in [("ln1_g", 8), ("ln1_b", 8), ("ln2_g", 8), ("ln2_b", 8), ("ln3_g", 8), ("ln3_b", 8),
                    ("ln4_g", 8), ("ln4_b", 8), ("b_gate", 16), ("q_norm_g", 3), ("kv_norm_g", 2),
                    ("mu_prev", 15), ("mu_next", 15), ("w0", 8), ("a0", 4), ("k_k", 4), ("k_a", 4), ("r_k", 4),
                    ("lnx_g", 4), ("lnx_b", 4)]:
        off[name] = (n, c)
        n += c
    return off, n


COLS, NCOL = _col_layout()
RW_CH = [(i * 128, 128) for i in range(12)] + [(1536, 128), (1664, 64), (1728, 128)]


def _pack_cols(p):
    a = np.zeros((128, NCOL), np.float32)

    def put(name, vec):
        o, c = COLS[name]
        v = np.asarray(vec, np.float32).reshape(-1)
        a[:, o:o + c] = v.reshape(c, 128).T

    for nm in ["ln1_g", "ln1_b", "ln2_g", "ln2_b", "ln3_g", "ln3_b", "ln4_g", "ln4_b", "b_gate", "q_norm_g",
               "kv_norm_g", "a0", "k_k", "k_a", "r_k", "lnx_g", "lnx_b", "w0"]:
        put(nm, p[nm][0])
    for nm in ["mu_prev", "mu_next"]:
        o, c = COLS[nm]
        v = np.asarray(p[nm][0], np.float32)
        for j, (ro, wd) in enumerate(RW_CH):
            a[:wd, o + j] = v[ro:ro + wd]
    return a


WL = {"wgu0": [11, 128, 8, 512], "wgu1": [11, 128, 8, 512], "wd0": [8, 128, NFC, 128], "wd1": [8, 128, NFC, 128],
      "win": [11, 128, 8, 512], "wuq": [128, 3, 1536], "wukv": [128, 2, 1024], "pmla": [128, 4, 1024],
      "prwkv": [128, 4, 1024], "wo": [128, 8, 1024], "wcq": [128, 8, 512], "wckv": [128, 8, 1024],
      "wco": [128, 4, 1024], "wup": [128, 1, 512], "aup": [64, 1, 512], "gup": [128, 1, 512]}


def _layout_weights(p):
    g = lambda n: np.asarray(p[n][0], np.float32)
    kp = lambda a: a.reshape(a.shape[0] // 128, 128, a.shape[1]).transpose(1, 0, 2)
    o = {}
    for i, pre in enumerate(["ffn1", "ffn2"]):
        gu = kp(g(pre + "_wgu"))
        blk = np.zeros((11, 128, 8, 512), np.float32)
        for j in range(11):
            blk[j, :, :, 0:256] = gu[:, :, j * 256:(j + 1) * 256]
            blk[j, :, :, 256:512] = gu[:, :, DFF + j * 256:DFF + (j + 1) * 256]
        o["wgu%d" % i] = blk
        wd = kp(g(pre + "_wd"))
        o["wd%d" % i] = np.stack([wd[:, :, c * 128:(c + 1) * 128] for c in range(8)])
    win = kp(g("w_in"))
    blk = np.zeros((11, 128, 8, 512), np.float32)
    blk[0, :, :, 0:384] = win[:, :, 0:384]
    blk[1, :, :, 0:256] = win[:, :, 384:640]
    blk[2, :, :, 64:96] = win[:, :, 640:672]
    blk[2, :, :, 160:176] = win[:, :, 656:672]
    blk[2, :, :, 176:192] = win[:, :, 640:656]
    for j in range(3):
        blk[3 + j] = win[:, :, OFF_RWKV + j * 512:OFF_RWKV + (j + 1) * 512]
    blk[6, :, :, 0:320] = win[:, :, OFF_RWKV + 1536:OFF_RWKV + 1856]
    for j in range(4):
        blk[7 + j] = win[:, :, OFF_GATE + j * 512:OFF_GATE + (j + 1) * 512]
    o["win"] = blk
    uq = kp(g("w_uq")).reshape(128, 3, 8, 96)
    uq2 = np.zeros((128, 3, 2, 8, 96), np.float32)
    uq2[:, :, 0] = uq
    uq2[:, :, 1, :, 64:80] = uq[:, :, :, 80:96]
    uq2[:, :, 1, :, 80:96] = uq[:, :, :, 64:80]
    o["wuq"] = uq2.reshape(128, 3, 1536)
    ukv = kp(g("w_ukv")).reshape(128, 2, 8, 128)
    o["wukv"] = np.concatenate([ukv[..., 0:64].reshape(128, 2, 512), ukv[..., 64:128].reshape(128, 2, 512)], -1)
    o["pmla"] = kp(g("p_mla"))
    o["prwkv"] = kp(g("p_rwkv"))
    o["wo"] = kp(g("w_o"))
    o["wcq"] = kp(g("w_cq"))
    o["wckv"] = kp(g("w_ckv"))
    o["wco"] = kp(g("w_co"))
    o["wup"] = g("w_up").reshape(128, 1, 512)
    o["aup"] = g("a_up").reshape(64, 1, 512)
    o["gup"] = g("g_up").reshape(128, 1, 512)
    return {k + "_f": np.ascontiguousarray(v) for k, v in o.items()}


class Builder:
    def __init__(self, S, nslot, debug=False):
        self.S = S
        self.nslot = nslot
        self.NT = S // TT
        self.debug = debug

    def declare(self, nc):
        S, ns = self.S, self.nslot
        I = lambda n, sh, dt=F32: nc.dram_tensor(n, sh, dt, kind="ExternalInput").ap()
        self.x = I("x", [ns, S, D])
        self.mem = I("mem", [ns, 256, D])
        self.wl = {n: I(n + "_f", sh) for n, sh in WL.items()}
        self.pcols = I("pcols", [128, NCOL])
        self.ident = I("ident", [128, 128])
        self.ropec = I("ropec", [32, S])
        self.ropes = I("ropes", [32, S])
        self.memgb = I("memgb", [2, D])
        self.cst2 = I("cst2", [128, 128 + 512 + 384 + 64])
        self.y = nc.dram_tensor("y", [ns, S, D], F32, kind="ExternalOutput").ap()
        kind = "ExternalOutput" if self.debug else "Internal"
        Sc = lambda n, sh, dt=F32: nc.dram_tensor(n, sh, dt, kind=kind).ap()
        self.Sc = Sc
        self.ws = {n: Sc(n + "_s", sh, BF16) for n, sh in WL.items()}
        self.wgu_s = [self.ws["wgu0"], self.ws["wgu1"]]
        self.wd_s = [self.ws["wd0"], self.ws["wd1"]]
        self.win_s = self.ws["win"]
        self.x1T = Sc("x1T", [D, S])
        self.qT = Sc("qT", [8, 96, S], BF16)
        self.kT = Sc("kT", [8, 96, S], BF16)
        self.vtok = Sc("vtok", [S, 8, 65], BF16)
        self.gates = Sc("gatesT", [2048, S])
        self.hrT = Sc("hrT", [1856, S])
        self.OT = Sc("OT", [512, S], BF16)
        self.rw4 = Sc("rw4", [4, 512, S])
        self.rwlw = Sc("rwlw", [2, 512, S])
        self.rwg = Sc("rwg", [512, S])
        self.rwbon = Sc("rwbon", [512, S])
        self.rwv = Sc("rwv", [S, 512], BF16)
        self.rwy = Sc("rwy", [2, S, 512])
        self.kmT = Sc("kmT_s", [128, 4, 256], BF16)
        self.vms = Sc("vm_s", [128, 2, 516], BF16)
        self.rwo = Sc("rwoT", [512, S], BF16)

    def build(self):
        nc = bass.Bass("TRN2", target_bir_lowering=False)
        self.nc = nc
        self.declare(nc)
        with ExitStack() as es:
            K = Kern(nc, es)
            self.K = K
            self.es = es
            self.db = {n: K.buf("dram_" + n) for n in ["x1T", "qT", "kT", "vtok", "gates", "hrT", "y", "OT", "rwo", "rw4", "rwlw", "rwg", "rwbon", "rwv", "rwy0", "rwy1", "kmT", "vms"]}
            self.consts()
            stage = getattr(self, "stage", "all")
            if stage not in ("c", "p1a_nop0"):
                self.p0_weights()
            for slot in range(self.nslot):
                if stage not in ("p0", "c"):
                    self.p1(slot)
                K.barrier()
                if stage in ("p2", "all"):
                    self.p2(slot)
                    K.barrier()
                if stage in ("p3a", "p3b", "p3c", "all"):
                    self.p3a(slot)
                    K.barrier()
                if stage in ("p3b", "p3c", "all"):
                    self.p3b(slot)
                    K.barrier()
                if stage in ("p3c", "all"):
                    self.p3c(slot)
                    K.barrier()
                if stage in ("all",):
                    self.p4m(slot)
                    K.barrier()
                    self.p4(slot)
                    K.barrier()
            K.barrier()
        return nc

    def sbt(self, es, name, shape, dt):
        return es.enter_context(self.nc.sbuf_tensor(name, shape, dt))

    def pst(self, es, name, shape, dt):
        return es.enter_context(self.nc.psum_tensor(name, shape, dt))

    def consts(self):
        nc, K, es = self.nc, self.K, self.es
        self.pc = self.sbt(es, "pc", [128, NCOL], F32)
        self.b_pc = K.buf("pc")
        K.dma(K.sp, self.pc[:], self.pcols, writes=[self.b_pc])
        self.idf = self.sbt(es, "idf", [128, 128], F32)
        self.idb = self.sbt(es, "idb", [128, 128], BF16)
        self.b_id = K.buf("id")
        K.dma(K.sp, self.idf[:], self.ident, writes=[self.b_id])
        K.op(K.dve, lambda: nc.vector.tensor_copy(out=self.idb[:], in_=self.idf[:]), reads=[self.b_id], writes=[self.b_id], acc=True)
        self.ones = self.sbt(es, "ones", [128, 128], F32)
        self.b_ones = K.buf("ones")
        K.op(K.pool, lambda: nc.gpsimd.memset(self.ones[:], 1.0), writes=[self.b_ones])
        self.c2 = self.sbt(es, "c2", [128, 128 + 512 + 384 + 64], F32)
        self.b_c2 = K.buf("c2")
        K.dma(K.sp, self.c2[:], self.cst2, writes=[self.b_c2])
        self.ones2 = self.c2[:, 0:128]
        self.cmask = self.c2[:, 128:640]
        self.rmask = lambda d: self.c2[:, 640 + d * 192:640 + (d + 1) * 192]
        self.idl = self.c2[:, 1024:1088]
        self.epsc = self.sbt(es, "epsc", [128, 4], F32)
        for i, v in enumerate([LN_EPS, RMS_EPS, GN_EPS, 1e-18]):
            K.op(K.pool, lambda: nc.gpsimd.memset(self.epsc[:, i:i + 1], v), writes=[self.b_ones], acc=True)

    def col(self, name, j=0):
        o, c = COLS[name]
        return self.pc[:, o + j:o + j + 1]

    def p0_weights(self):
        K = self.K
        P = K.pool
        hist = []
        for n, sh in WL.items():
            self.db[n] = K.buf("dram_" + n)
            blocks = [(self.ws[n][j], self.wl[n][j]) for j in range(sh[0])] if len(sh) == 4 else [(self.ws[n], self.wl[n])]
            for o, i in blocks:
                if len(hist) >= 2:
                    b, c = hist[-2]
                    K._wait(P, Ev(b.dsem, b.dkey, c))
                K.dma(P, o, i, writes=[self.db[n]], acc=True)
                hist.append((self.db[n], self.db[n].dcount))

    def ln_fm(self, z, bz, gname, bname, o32, bo32, o16, bo16, ps, bps, sqf, bsq, st, bst):
        nc, K = self.nc, self.K
        for c in range(8):
            K.op(K.pe, lambda: nc.tensor.matmul(ps[:], lhsT=self.ones[:], rhs=z[:, c, :], start=(c == 0), stop=(c == 7)),
                 reads=[bz[c], self.b_ones], writes=[bps], inc=(c == 7), acc=(c > 0))
        K.op(K.act, lambda: nc.scalar.mul(out=st[:], in_=ps[:], mul=1.0 / D), reads=[bps], writes=[bst])
        for c in range(8):
            K.op(K.dve, lambda: nc.vector.tensor_tensor(out=z[:, c, :], in0=z[:, c, :], in1=st[:], op=ALU.subtract),
                 reads=[bz[c], bst], writes=[bz[c]])
            K.op(K.act, lambda: nc.scalar.activation(out=sqf(c), in_=z[:, c, :], func=AF.Square), reads=[bz[c]], writes=[bsq[c]])
        for c in range(8):
            K.op(K.pe, lambda: nc.tensor.matmul(ps[:], lhsT=self.ones[:], rhs=sqf(c), start=(c == 0), stop=(c == 7)),
                 reads=[bsq[c], self.b_ones], writes=[bps], inc=(c == 7), acc=(c > 0))
        K.op(K.act, lambda: nc.scalar.activation(out=st[:], in_=ps[:], func=AF.Ln, scale=1.0 / D, bias=self.epsc[:, 0:1]),
             reads=[bps, self.b_ones], writes=[bst])
        K.op(K.act, lambda: nc.scalar.activation(out=st[:], in_=st[:], func=AF.Exp, scale=-0.5), reads=[bst], writes=[bst])
        for c in range(8):
            K.op(K.dve, lambda: nc.vector.tensor_tensor(out=z[:, c, :], in0=z[:, c, :], in1=st[:], op=ALU.mult),
                 reads=[bz[c], bst], writes=[bz[c]])
            K.op(K.act, lambda: nc.scalar.activation(out=o32[:, c, :], in_=z[:, c, :], func=AF.Identity,
                                                      scale=self.col(gname, c), bias=self.col(bname, c)),
                 reads=[bz[c], self.b_pc], writes=[bo32[c]])
            if o16 is not None:
                K.op(K.pool, lambda: nc.gpsimd.tensor_copy(out=o16[:, c, :], in_=o32[:, c, :]), reads=[bo32[c]], writes=[bo16[c]])

    def rms_fm(self, src, bsrc, nch, nfeat, gname, sqf, bsq, ps, bps, st, bst, outb, boutb):
        nc, K = self.nc, self.K
        for c in range(nch):
            K.op(K.act, lambda: nc.scalar.activation(out=sqf(c), in_=src(c), func=AF.Square), reads=[bsrc[c]], writes=[bsq[c]])
        for c in range(nch):
            K.op(K.pe, lambda: nc.tensor.matmul(ps[:], lhsT=self.ones[:], rhs=sqf(c), start=(c == 0), stop=(c == nch - 1)),
                 reads=[bsq[c], self.b_ones], writes=[bps], inc=(c == nch - 1), acc=(c > 0))
        K.op(K.act, lambda: nc.scalar.activation(out=st[:], in_=ps[:], func=AF.Ln, scale=1.0 / nfeat, bias=self.epsc[:, 1:2]),
             reads=[bps, self.b_ones], writes=[bst])
        K.op(K.act, lambda: nc.scalar.activation(out=st[:], in_=st[:], func=AF.Exp, scale=-0.5), reads=[bst], writes=[bst])
        for c in range(nch):
            K.op(K.dve, lambda: nc.vector.scalar_tensor_tensor(out=outb(c), in0=src(c), scalar=self.col(gname, c), in1=st[:], op0=ALU.mult, op1=ALU.mult),
                 reads=[bsrc[c], bst, self.b_pc], writes=[boutb[c]])

    def ffn(self, idx, xb, bxb, xa, bxa, hT, bhT, wgu, bwgu, wdn, bwdn, psA, bpsA, z, bz, extra={}):
        nc, K = self.nc, self.K
        dbg, dbd = self.db["wgu%d" % idx], self.db["wd%d" % idx]
        K.dma(K.sp, wgu[0][:], self.wgu_s[idx][0], reads=[dbg], writes=[bwgu[0]])
        pi = 0
        for j in range(11):
            if j + 1 < 11:
                K.dma(K.sp, wgu[(j + 1) % 2][:], self.wgu_s[idx][j + 1], reads=[dbg], writes=[bwgu[(j + 1) % 2]])
            else:
                K.dma(K.sp, wdn[0][:], self.wd_s[idx][0], reads=[dbd], writes=[bwdn[0]])
            wt, bwt = wgu[j % 2], bwgu[j % 2]
            for f in range(2):
                fc = j * 2 + f
                pg, bpg = psA[pi % 4], bpsA[pi % 4]
                pu, bpu = psA[(pi + 1) % 4], bpsA[(pi + 1) % 4]
                pi += 2
                for kc in range(8):
                    K.op(K.pe, lambda: nc.tensor.matmul(pg[:], lhsT=wt[:, kc, f * 128:(f + 1) * 128], rhs=xb[:, kc, :], start=(kc == 0), stop=(kc == 7)),
                         reads=[bwt, bxb[kc]], writes=[bpg], inc=(kc == 7), acc=(kc > 0), lhs=[bwt])
                for kc in range(8):
                    K.op(K.pe, lambda: nc.tensor.matmul(pu[:], lhsT=wt[:, kc, 256 + f * 128:256 + (f + 1) * 128], rhs=xb[:, kc, :], start=(kc == 0), stop=(kc == 7)),
                         reads=[bwt, bxb[kc]], writes=[bpu], inc=(kc == 7), acc=(kc > 0), lhs=[bwt])
                K.op(K.act, lambda: nc.scalar.activation(out=hT[:, fc, :], in_=pg[:], func=AF.Silu), reads=[bpg], writes=[bhT[fc]] + extra.get(fc, []))
                K.op(K.dve, lambda: nc.vector.tensor_tensor(out=hT[:, fc, :], in0=hT[:, fc, :], in1=pu[:], op=ALU.mult),
                     reads=[bhT[fc], bpu], writes=[bhT[fc]])
        for c in range(8):
            if c + 1 < 8:
                K.dma(K.sp, wdn[(c + 1) % 2][:], self.wd_s[idx][c + 1], reads=[dbd], writes=[bwdn[(c + 1) % 2]])
            wt, bwt = wdn[c % 2], bwdn[c % 2]
            pz, bpz = psA[c % 4], bpsA[c % 4]
            for kc in range(NFC):
                K.op(K.pe, lambda: nc.tensor.matmul(pz[:], lhsT=wt[:, kc, :], rhs=hT[:, kc, :], start=(kc == 0), stop=(kc == NFC - 1)),
                     reads=[bwt, bhT[kc]], writes=[bpz], inc=(kc == NFC - 1), acc=(kc > 0), lhs=[bwt])
            K.op(K.dve, lambda: nc.vector.scalar_tensor_tensor(out=z[:, c, :], in0=pz[:], scalar=0.5, in1=xa[:, c, :], op0=ALU.mult, op1=ALU.add),
                 reads=[bpz, bxa[c]], writes=[bz[c]])

    def p1(self, slot):
        nc, K = self.nc, self.K
        K.begin_scope("p1")
        with ExitStack() as es:
            sb = lambda n, sh, dt: self.sbt(es, "p1_%d_" % slot + n, sh, dt)
            xtok = sb("xtok", [128, 4, D], F32); bsq = K.bufs(8, "xtok_sq")
            sqf = lambda c: xtok[:, c // 2, (c % 2) * TT:(c % 2 + 1) * TT]
            xa = sb("xa", [128, 8, TT], F32); bxa = K.bufs(8, "xa")
            xb = sb("xb", [128, 8, TT], BF16); bxb = K.bufs(8, "xb")
            hT = sb("hT", [128, NFC, TT], BF16); bhT = K.bufs(NFC, "hT")
            bhTr = K.bufs(16, "hTr")
            z = sb("z", [128, 8, TT], F32); bz = K.bufs(8, "z")
            st = sb("st", [128, TT], F32); bst = K.buf("st")
            st2 = sb("st2", [128, TT], F32); bst2 = K.buf("st2")
            x1b = sb("x1b", [128, 8, TT], BF16); bx1b = K.bufs(8, "x1b")
            wgu = [sb("wgu%d" % i, [128, 8, 512], BF16) for i in range(2)]; bwgu = K.bufs(2, "wgu")
            wdn = [sb("wdn%d" % i, [128, NFC, 128], BF16) for i in range(2)]; bwdn = K.bufs(2, "wdn")
            ev = sb("ev", [128, 8, TT], F32); bev = K.bufs(8, "ev")
            wuq = sb("wuq", [128, 3, 1536], BF16); bwuq = K.buf("wuq")
            wukv = sb("wukv", [128, 2, 1024], BF16); bwukv = K.buf("wukv")
            vst = sb("vst", [128, 4, 520], BF16); bvst = K.buf("vst")
            ropet = sb("ropet", [96, 2, TT], F32); bropet = K.buf("ropet")
            K.dma(K.sp, wuq[:], self.ws["wuq"], reads=[self.db["wuq"]], writes=[bwuq])
            K.dma(K.sp, wukv[:], self.ws["wukv"], reads=[self.db["wukv"]], writes=[bwukv])
            K.op(K.pool, lambda: nc.gpsimd.memset(vst[:], 1.0), writes=[bvst])
            psA = [self.pst(es, "p1_%d_ps%%d" % slot % i, [128, TT], F32) for i in range(4)]; bpsA = K.bufs(4, "psA")
            psS = self.pst(es, "p1_%d_pss" % slot, [128, TT], F32); bpsS = K.buf("psS")
            psT = [self.pst(es, "p1_%d_pst%%d" % slot % i, [128, TT], F32) for i in range(2)]; bpsT = K.bufs(2, "psT")
            L = dict(locals())
            for t in range(self.NT):
                t0 = t * TT
                K.dma(K.sp, xtok[:], self.x[slot, t0:t0 + TT, :].rearrange("(n p) d -> p n d", p=128), writes=bsq)
                for kc in range(8):
                    p, bp = psT[kc % 2], bpsT[kc % 2]
                    for n in range(4):
                        K.op(K.pe, lambda: nc.tensor.transpose(out=p[:, n * 128:(n + 1) * 128], in_=xtok[:, n, kc * 128:(kc + 1) * 128], identity=self.idf[:]),
                             reads=bsq + [self.b_id], writes=[bp], inc=(n == 3), acc=(n > 0))
                    K.op(K.act, lambda: nc.scalar.mul(out=xa[:, kc, :], in_=p[:], mul=ALPHA), reads=[bp], writes=[bxa[kc]])
                    K.op(K.dve, lambda: nc.vector.tensor_copy(out=xb[:, kc, :], in_=p[:]), reads=[bp], writes=[bxb[kc]])
                self.ffn(0, xb, bxb, xa, bxa, hT, bhT, wgu, bwgu, wdn, bwdn, psA, bpsA, z, bz, extra={i: [bhTr[i]] for i in range(16)})
                self.ln_fm(z, bz, "ln1_g", "ln1_b", xa, bxa, x1b, bx1b, psS, bpsS, sqf, bsq, st, bst)
                K.dma(K.pool, self.x1T.rearrange("(c p) s -> p c s", p=128)[:, :, t0:t0 + TT], xa[:], reads=bxa, writes=[self.db["x1T"]], acc=True, owner=bxa[0])
                self.p1_win(slot, t, L)

    def p1_win(self, slot, t, L):
        nc, K = self.nc, self.K
        x1b, bx1b, wbuf, bwbuf, psA, bpsA, psS, bpsS, ev, bev = [L[k] for k in ["x1b", "bx1b", "wgu", "bwgu", "psA", "bpsA", "psS", "bpsS", "ev", "bev"]]
        z, bz, xb, bxb, hT, bhT, st, bst, sqf, bsq, bhTr = [L[k] for k in ["z", "bz", "xb", "bxb", "hT", "bhT", "st", "bst", "sqf", "bsq", "bhTr"]]
        wuq, bwuq, wukv, bwukv, vst, bvst, ropet, bropet = [L[k] for k in ["wuq", "bwuq", "wukv", "bwukv", "vst", "bvst", "ropet", "bropet"]]
        t0 = t * TT
        dbw = self.db["win"]
        K.dma(K.sp, wbuf[0][:], self.win_s[0], reads=[dbw], writes=[bwbuf[0]])
        K.dma(K.sp, ropet[64:96, 0, :], self.ropec[:, t0:t0 + TT], writes=[bropet])
        K.dma(K.sp, ropet[64:96, 1, :], self.ropes[:, t0:t0 + TT], writes=[bropet], acc=True)
        stt = {"pi": 0, "ei": 0}

        def nextps():
            i = stt["pi"] % 4
            stt["pi"] += 1
            return psA[i], bpsA[i]

        def proj(wt, bwt, c0, width):
            p, bp = nextps()
            for kc in range(8):
                K.op(K.pe, lambda: nc.tensor.matmul(p[0:width, :], lhsT=wt[:, kc, c0:c0 + width], rhs=x1b[:, kc, :], start=(kc == 0), stop=(kc == 7)),
                     reads=[bwt, bx1b[kc]], writes=[bp], inc=(kc == 7), acc=(kc > 0), lhs=[bwt])
            return p, bp

        def rope(pp, bpp, psw, bpsw, out_ap, bout):
            t1, t2 = z[64:96, 3, :], z[64:96, 4, :]
            K.op(K.dve, lambda: nc.vector.tensor_tensor(out=t1, in0=pp[64:96, :], in1=ropet[64:96, 0, :], op=ALU.mult), reads=[bpp, bropet], writes=[bz[3]])
            K.op(K.dve, lambda: nc.vector.tensor_tensor(out=t2, in0=psw[64:96, :], in1=ropet[64:96, 1, :], op=ALU.mult), reads=[bpsw, bropet], writes=[bz[4]])
            K.op(K.pool, lambda: nc.gpsimd.tensor_tensor(out=out_ap, in0=t1, in1=t2, op=ALU.add), reads=[bz[3], bz[4]], writes=bout)

        deferred = []
        for blk in range(11):
            if blk + 1 < 11:
                K.dma(K.sp, wbuf[(blk + 1) % 2][:], self.win_s[blk + 1], reads=[dbw], writes=[bwbuf[(blk + 1) % 2]])
            wt, bwt = wbuf[blk % 2], bwbuf[blk % 2]
            if blk == 0:
                for c in range(3):
                    p, bp = proj(wt, bwt, c * 128, 128)
                    K.op(K.act, lambda: nc.scalar.copy(out=z[:, c, :], in_=p[:]), reads=[bp], writes=[bz[c]])
                self.rms_fm(lambda c: z[:, c, :], bz, 3, 384.0, "q_norm_g", sqf, bsq, psS, bpsS, st, bst, lambda c: xb[:, c, :], bxb)

                def q_part_b():
                  for h in range(H):
                      pp, bpp = nextps()
                      psw, bpsw = nextps()
                      for v_, (pt_, bpt_) in enumerate([(pp, bpp), (psw, bpsw)]):
                          for kc in range(3):
                              K.op(K.pe, lambda: nc.tensor.matmul(pt_[0:96, :], lhsT=wuq[:, kc, (v_ * 8 + h) * 96:(v_ * 8 + h + 1) * 96], rhs=xb[:, kc, :], start=(kc == 0), stop=(kc == 2)),
                                   reads=[bwuq, bxb[kc]], writes=[bpt_], inc=(kc == 2), acc=(kc > 0))
                      K.op(K.act, lambda: nc.scalar.copy(out=hT[0:64, h, :], in_=pp[0:64, :]), reads=[bpp], writes=[bhT[h]])
                      rope(pp, bpp, psw, bpsw, hT[64:96, h, :], [bhTr[h]])
                  K.dma(K.pool, self.qT.rearrange("h d s -> d h s")[:, :, t0:t0 + TT], hT[0:96, 0:8, :], reads=bhT[0:8] + bhTr[0:8], writes=[self.db["qT"]], acc=True, owner=bhT[0])
                deferred.append(q_part_b)
            elif blk == 1:
                for c in range(2):
                    p, bp = proj(wt, bwt, c * 128, 128)
                    K.op(K.act, lambda: nc.scalar.copy(out=z[:, 5 + c, :], in_=p[:]), reads=[bp], writes=[bz[5 + c]])
                self.rms_fm(lambda c: z[:, 5 + c, :], bz[5:7], 2, 256.0, "kv_norm_g", sqf, bsq, psS, bpsS, st, bst, lambda c: xb[:, 3 + c, :], bxb[3:5])

                def kv_part_b():
                  for h in range(H):
                      p, bp = nextps()
                      for kc in range(2):
                          K.op(K.pe, lambda: nc.tensor.matmul(p[0:64, :], lhsT=wukv[:, kc, h * 64:(h + 1) * 64], rhs=xb[:, 3 + kc, :], start=(kc == 0), stop=(kc == 1)),
                               reads=[bwukv, bxb[3 + kc]], writes=[bp], inc=(kc == 1), acc=(kc > 0))
                      K.op(K.act, lambda: nc.scalar.copy(out=hT[0:64, 8 + h, :], in_=p[0:64, :]), reads=[bp], writes=[bhT[8 + h]])
                  for n in range(4):
                      p, bp = nextps()
                      for kc in range(2):
                          K.op(K.pe, lambda: nc.tensor.matmul(p[:], lhsT=xb[:, 3 + kc, n * 128:(n + 1) * 128], rhs=wukv[:, kc, 512:1024], start=(kc == 0), stop=(kc == 1)),
                               reads=[bwukv, bxb[3 + kc]], writes=[bp], inc=(kc == 1), acc=(kc > 0))
                      K.op(K.dve, lambda: nc.vector.tensor_copy(out=vst[:, n, :].rearrange("p (h d) -> p h d", d=65)[:, :, 0:64], in_=p[:].rearrange("p (h d) -> p h d", d=64)), reads=[bp], writes=[bvst], acc=(n > 0))
                  K.dma(K.pool, self.vtok[t0:t0 + TT].rearrange("(n p) h d -> p n (h d)", p=128), vst[:], reads=[bvst], writes=[self.db["vtok"]], acc=True, owner=bvst)
                deferred.append(kv_part_b)
            elif blk == 2:
                pp, bpp = proj(wt, bwt, 0, 96)
                psw, bpsw = proj(wt, bwt, 96, 96)
                rope(pp, bpp, psw, bpsw, z[64:96, 7, :], [bz[7]])
                for h in range(H):
                    eng = K.act if h % 2 == 0 else K.pool
                    if h % 2 == 0:
                        K.op(K.act, lambda: nc.scalar.copy(out=hT[64:96, 8 + h, :], in_=z[64:96, 7, :]), reads=[bz[7]], writes=[bhTr[8 + h]])
                    else:
                        K.op(K.pool, lambda: nc.gpsimd.tensor_copy(out=hT[64:96, 8 + h, :], in_=z[64:96, 7, :]), reads=[bz[7]], writes=[bhTr[8 + h]])
                deferred.append(lambda: K.dma(K.pool, self.kT.rearrange("h d s -> d h s")[:, :, t0:t0 + TT], hT[0:96, 8:16, :], reads=bhT[8:16] + bhTr[8:16], writes=[self.db["kT"]], acc=True, owner=bhT[8]))
            elif blk in (3, 4, 5, 6):
                widths = [128] * 4 if blk < 6 else [128, 64, 128]
                c0 = 0
                for i, wd_ in enumerate(widths):
                    p, bp = proj(wt, bwt, c0, wd_)
                    e, be = ev[:, stt["ei"] % 8, :], bev[stt["ei"] % 8]
                    stt["ei"] += 1
                    K.op(K.act, lambda: nc.scalar.copy(out=e[0:wd_, :], in_=p[0:wd_, :]), reads=[bp], writes=[be])
                    r0 = (blk - 3) * 512 + c0
                    K.dma(K.pool, self.hrT[r0:r0 + wd_, t0:t0 + TT], e[0:wd_, :], reads=[be], writes=[self.db["hrT"]], acc=True, owner=be)
                    c0 += wd_
            else:
                for i in range(4):
                    gc = (blk - 7) * 4 + i
                    p, bp = proj(wt, bwt, i * 128, 128)
                    e, be = ev[:, stt["ei"] % 8, :], bev[stt["ei"] % 8]
                    stt["ei"] += 1
                    K.op(K.act, lambda: nc.scalar.activation(out=e, in_=p[:], func=AF.Sigmoid, bias=self.col("b_gate", gc)),
                         reads=[bp, self.b_pc], writes=[be])
                    K.dma(K.pool, self.gates[gc * 128:(gc + 1) * 128, t0:t0 + TT], e, reads=[be], writes=[self.db["gates"]], acc=True, owner=be)


        for f in deferred:
            f()

    def p2(self, slot):
        nc, K = self.nc, self.K
        K.begin_scope("p2")
        S = self.S
        NK = S // 128
        scale = float(96 ** -0.5)
        with ExitStack() as es:
            sb = lambda n, sh, dt: self.sbt(es, "p2_%d_" % slot + n, sh, dt)
            KT = sb("KT", [96, 8, S], BF16); bKT = K.buf("KT")
            Vt = sb("Vt", [128, NK, 520], BF16); bVt = K.buf("Vt")
            Qt = [sb("Qt%d" % i, [96, 8, TT], BF16) for i in range(2)]; bQt = K.bufs(2, "Qt")
            PT = [sb("PT%d" % i, [128, TT], BF16) for i in range(4)]; bPT = K.bufs(4, "PT")
            Otok = sb("Otok", [128, 4, 512], BF16); bOtok = K.bufs(8, "Otok")
            OTs = sb("OTs", [128, 4, TT], BF16); bOTs = K.bufs(4, "OTs")
            rs = sb("rs", [128, 2, 4], F32); brs = K.bufs(2, "rs")
            psS = [self.pst(es, "p2_%d_pss%%d" % slot % i, [128, TT], F32) for i in range(3)]; bpsS = K.bufs(3, "psS")
            psO = [self.pst(es, "p2_%d_pso%%d" % slot % i, [128, TT], F32) for i in range(2)]; bpsO = K.bufs(2, "psO")
            psT = self.pst(es, "p2_%d_pst" % slot, [128, 2 * TT], BF16); bpsT = K.buf("psT")
            for h in range(H):
                K.dma(K.sp, KT[:, h, :], self.kT[h], reads=[self.db["kT"]], writes=[bKT], acc=(h > 0))
            K.dma(K.sp, Vt[:], self.vtok.rearrange("(n p) h d -> p n (h d)", p=128), reads=[self.db["vtok"]], writes=[bVt])
            qTv = self.qT.rearrange("h d s -> d h s")
            K.dma(K.sp, Qt[0][:], qTv[:, :, 0:TT], reads=[self.db["qT"]], writes=[bQt[0]])
            items = [(t, h, kt) for t in range(self.NT) for h in range(H) for kt in range(NK)]

            def emit_S(i):
                t, h, kt = items[i]
                if h == 0 and kt == 0 and t + 1 < self.NT:
                    K.dma(K.sp, Qt[(t + 1) % 2][:], qTv[:, :, (t + 1) * TT:(t + 2) * TT], reads=[self.db["qT"]], writes=[bQt[(t + 1) % 2]])
                pS, bpS = psS[i % 3], bpsS[i % 3]
                K.op(K.pe, lambda: nc.tensor.matmul(pS[:], lhsT=KT[:, h, kt * 128:(kt + 1) * 128], rhs=Qt[t % 2][:, h, :], start=True, stop=True),
                     reads=[bKT, bQt[t % 2]], writes=[bpS], lhs=[bKT])

            emit_S(0)
            emit_S(1)
            for i, (t, h, kt) in enumerate(items):
                t0 = t * TT
                if i + 2 < len(items):
                    emit_S(i + 2)
                pS, bpS = psS[i % 3], bpsS[i % 3]
                P_, bP = PT[i % 4], bPT[i % 4]
                pO, bpO = psO[h % 2], bpsO[h % 2]
                K.op(K.act, lambda: nc.scalar.activation(out=P_[:], in_=pS[:], func=AF.Exp, scale=scale), reads=[bpS], writes=[bP])
                for qs in range(4):
                    K.op(K.pe, lambda: nc.tensor.matmul(pO[:, qs * 65:(qs + 1) * 65], lhsT=P_[:, qs * 128:(qs + 1) * 128], rhs=Vt[:, kt, h * 65:(h + 1) * 65],
                                                         start=(kt == 0 and qs == 0), stop=(kt == NK - 1 and qs == 3)),
                         reads=[bP, bVt], writes=[bpO], inc=(qs == 3), acc=not (kt == 0 and qs == 0))
                if kt == NK - 1:
                    r_, br = rs[:, h % 2, :], brs[h % 2]
                    K.op(K.dve, lambda: nc.vector.reciprocal(out=r_, in_=pO[:, 0:260].rearrange("p (q d) -> p q d", d=65)[:, :, 64]), reads=[bpO], writes=[br])
                    for qs in range(4):
                        K.op(K.dve, lambda: nc.vector.tensor_scalar(out=Otok[:, qs, h * 64:(h + 1) * 64], in0=pO[:, qs * 65:qs * 65 + 64], scalar1=rs[:, h % 2, qs:qs + 1], scalar2=None, op0=ALU.mult),
                             reads=[bpO, br], writes=[bOtok[h]], acc=(qs > 0))
                    if h == H - 1:
                        for c in range(4):
                            for qs in range(4):
                                K.op(K.pe, lambda: nc.tensor.transpose(out=psT[:, qs * 128:(qs + 1) * 128], in_=Otok[:, qs, c * 128:(c + 1) * 128], identity=self.idb[:]),
                                     reads=[bOtok[2 * c], bOtok[2 * c + 1], self.b_id], writes=[bpsT], inc=(qs == 3), acc=(qs > 0))
                            K.op(K.dve, lambda: nc.vector.tensor_copy(out=OTs[:, c, :], in_=psT[:, 0:TT]), reads=[bpsT], writes=[bOTs[c]])
                        K.dma(K.pool, self.OT.rearrange("(c p) s -> p c s", p=128)[:, :, t0:t0 + TT], OTs[:], reads=bOTs, writes=[self.db["OT"]], acc=True, owner=bOTs[0])

    def p3a(self, slot):
        nc, K = self.nc, self.K
        K.begin_scope("p3a")
        S, NT = self.S, self.NT
        with ExitStack() as es:
            sb = lambda n, sh, dt: self.sbt(es, "p3a_%d_" % slot + n, sh, dt)
            hr = sb("hr", [128, 15, TT + 2], F32); bhr = K.bufs(15, "hr")
            sh = sb("sh", [128, 15, TT], F32); bsh = K.bufs(15, "sh")
            lw = sb("lw", [128, 8, TT], F32); blw = K.bufs(8, "lw")
            eta = sb("eta", [128, 4, TT], F32); beta = K.bufs(4, "eta")
            o4 = sb("o4", [128, 3, 4, TT], F32); bo4 = [K.bufs(4, "o4_%d" % i) for i in range(3)]
            gT = sb("gT", [128, 4, TT], F32); bgT = K.bufs(4, "gT")
            bon = sb("bon", [128, 4, TT], F32); bbon = K.bufs(4, "bon")
            tmp = sb("tmp", [128, 4, TT], F32); btmp = K.bufs(4, "tmp")
            tb16 = sb("tb16", [128, 3, TT], BF16); btb = K.bufs(3, "tb16")
            vrt = sb("vrt", [128, 4, 512], BF16); bvrt = K.bufs(4, "vrt")
            wup = sb("wup", [128, 512], BF16); aup = sb("aup", [64, 512], BF16); gup = sb("gup", [128, 512], BF16); bw = K.buf("rwkvw")
            cc = sb("cc", [128, 15 + 4], F32); bcc = K.buf("cc")
            K.dma(K.sp, wup[:], self.ws["wup"][:, 0, :], reads=[self.db["wup"]], writes=[bw])
            K.dma(K.sp, aup[:], self.ws["aup"][:, 0, :], reads=[self.db["aup"]], writes=[bw], acc=True)
            K.dma(K.sp, gup[:], self.ws["gup"][:, 0, :], reads=[self.db["gup"]], writes=[bw], acc=True)
            o, _ = COLS["mu_prev"]; o2, _ = COLS["mu_next"]; oka, _ = COLS["k_a"]
            K.op(K.dve, lambda: nc.vector.tensor_tensor(out=cc[:, 0:15], in0=self.pc[:, o:o + 15], in1=self.pc[:, o2:o2 + 15], op=ALU.add), reads=[self.b_pc], writes=[bcc])
            K.op(K.dve, lambda: nc.vector.tensor_scalar(out=cc[:, 0:15], in0=cc[:, 0:15], scalar1=-1.0, scalar2=1.0, op0=ALU.mult, op1=ALU.add), reads=[bcc], writes=[bcc])
            K.op(K.dve, lambda: nc.vector.tensor_scalar(out=cc[:, 15:19], in0=self.pc[:, oka:oka + 4], scalar1=-1.0, scalar2=1.0, op0=ALU.mult, op1=ALU.add), reads=[self.b_pc, bcc], writes=[bcc])
            ps = [self.pst(es, "p3a_%d_ps%%d" % slot % i, [128, TT], F32) for i in range(5)]; bps = K.bufs(5, "ps3a")
            pi = [0]

            def nps():
                i = pi[0] % 5
                pi[0] += 1
                return ps[i], bps[i]

            dbh = self.db["hrT"]
            for t in range(NT):
                t0 = t * TT
                lo, hi = max(t0 - 1, 0), min(t0 + TT + 1, S)
                a_, b_ = lo - (t0 - 1), hi - (t0 - 1)
                K.dma(K.sp, hr[:, 0:12, a_:b_], self.hrT[0:1536].rearrange("(c p) s -> p c s", p=128)[:, :, lo:hi], reads=[dbh], writes=bhr[0:12])
                K.dma(K.sp, hr[:, 12, a_:b_], self.hrT[1536:1664, lo:hi], reads=[dbh], writes=[bhr[12]])
                K.dma(K.sp, hr[0:64, 13, a_:b_], self.hrT[1664:1728, lo:hi], reads=[dbh], writes=[bhr[13]])
                K.dma(K.sp, hr[:, 14, a_:b_], self.hrT[1728:1856, lo:hi], reads=[dbh], writes=[bhr[14]])
                if t == 0:
                    K.op(K.pool, lambda: nc.gpsimd.memset(hr[:, :, 0:1], 0.0), writes=bhr, acc=True)
                if t == NT - 1:
                    K.op(K.pool, lambda: nc.gpsimd.memset(hr[:, :, TT + 1:TT + 2], 0.0), writes=bhr, acc=True)
                for j, (ro, wd_) in enumerate(RW_CH):
                    K.op(K.act, lambda: nc.scalar.activation(out=sh[0:wd_, j, :], in_=hr[0:wd_, j, 1:TT + 1], func=AF.Identity, scale=cc[0:wd_, j:j + 1]),
                         reads=[bhr[j], bcc], writes=[bsh[j]])
                    K.op(K.dve, lambda: nc.vector.scalar_tensor_tensor(out=sh[0:wd_, j, :], in0=hr[0:wd_, j, 0:TT], scalar=self.pc[0:wd_, o + j:o + j + 1], in1=sh[0:wd_, j, :], op0=ALU.mult, op1=ALU.add),
                         reads=[bhr[j], bsh[j], self.b_pc], writes=[bsh[j]])
                    K.op(K.dve, lambda: nc.vector.scalar_tensor_tensor(out=sh[0:wd_, j, :], in0=hr[0:wd_, j, 2:TT + 2], scalar=self.pc[0:wd_, o2 + j:o2 + j + 1], in1=sh[0:wd_, j, :], op0=ALU.mult, op1=ALU.add),
                         reads=[bhr[j], bsh[j], self.b_pc], writes=[bsh[j]])
                r_ = lambda c: sh[:, c, :]
                k_ = lambda c: sh[:, 4 + c, :]
                v_ = lambda c: sh[:, 8 + c, :]
                K.op(K.act, lambda: nc.scalar.activation(out=tb16[:, 0, :], in_=sh[:, 12, :], func=AF.Tanh), reads=[bsh[12]], writes=[btb[0]])
                K.op(K.pool, lambda: nc.gpsimd.tensor_copy(out=tb16[0:64, 1, :], in_=sh[0:64, 13, :]), reads=[bsh[13]], writes=[btb[1]])
                K.op(K.act, lambda: nc.scalar.activation(out=tb16[:, 2, :], in_=sh[:, 14, :], func=AF.Sigmoid), reads=[bsh[14]], writes=[btb[2]])
                for d in range(2):
                    for c in range(4):
                        p, bp = nps()
                        K.op(K.pe, lambda: nc.tensor.matmul(p[:], lhsT=wup[64 * d:64 * d + 64, c * 128:(c + 1) * 128], rhs=tb16[64 * d:64 * d + 64, 0, :], start=True, stop=True),
                             reads=[bw, btb[0]], writes=[bp])
                        K.op(K.act, lambda: nc.scalar.activation(out=lw[:, d * 4 + c, :], in_=p[:], func=AF.Sigmoid, bias=self.col("w0", d * 4 + c)), reads=[bp, self.b_pc], writes=[blw[d * 4 + c]])
                        K.op(K.pool, lambda: nc.gpsimd.tensor_scalar(out=lw[:, d * 4 + c, :], in0=lw[:, d * 4 + c, :], scalar1=-WCONST, scalar2=None, op0=ALU.mult), reads=[blw[d * 4 + c]], writes=[blw[d * 4 + c]])
                    K.dma(K.pool, self.rwlw[d].rearrange("(c p) s -> p c s", p=128)[:, :, t0:t0 + TT], lw[:, d * 4:d * 4 + 4, :], reads=blw[d * 4:d * 4 + 4], writes=[self.db["rwlw"]], acc=True, owner=blw[d * 4])
                for c in range(4):
                    p, bp = nps()
                    K.op(K.pe, lambda: nc.tensor.matmul(p[:], lhsT=aup[0:64, c * 128:(c + 1) * 128], rhs=tb16[0:64, 1, :], start=True, stop=True), reads=[bw, btb[1]], writes=[bp])
                    K.op(K.act, lambda: nc.scalar.activation(out=eta[:, c, :], in_=p[:], func=AF.Sigmoid, bias=self.col("a0", c)), reads=[bp, self.b_pc], writes=[beta[c]])
                    p, bp = nps()
                    K.op(K.pe, lambda: nc.tensor.matmul(p[:], lhsT=gup[:, c * 128:(c + 1) * 128], rhs=tb16[:, 2, :], start=True, stop=True), reads=[bw, btb[2]], writes=[bp])
                    K.op(K.act, lambda: nc.scalar.copy(out=gT[:, c, :], in_=p[:]), reads=[bp], writes=[bgT[c]])
                    kk, bkk = tmp[:, c % 2, :], btmp[c % 2]
                    sq, bsq_ = tmp[:, 2 + c % 2, :], btmp[2 + c % 2]
                    K.op(K.act, lambda: nc.scalar.activation(out=kk, in_=k_(c), func=AF.Identity, scale=self.col("k_k", c)), reads=[bsh[4 + c], self.b_pc], writes=[bkk])
                    K.op(K.act, lambda: nc.scalar.activation(out=sq, in_=kk, func=AF.Square), reads=[bkk], writes=[bsq_])
                    p, bp = nps()
                    K.op(K.pe, lambda: nc.tensor.matmul(p[:], lhsT=self.ones2, rhs=sq, start=True, stop=True), reads=[self.b_c2, bsq_], writes=[bp])
                    K.op(K.act, lambda: nc.scalar.activation(out=sq, in_=p[:], func=AF.Ln, bias=self.epsc[:, 3:4]), reads=[bp, self.b_ones], writes=[bsq_])
                    K.op(K.act, lambda: nc.scalar.activation(out=sq, in_=sq, func=AF.Exp, scale=-0.5), reads=[bsq_], writes=[bsq_])
                    av, bav = o4[:, 1, c, :], bo4[1][c]
                    bv, bbv = o4[:, 2, c, :], bo4[2][c]
                    km, bkm = o4[:, 0, c, :], bo4[0][c]
                    K.op(K.dve, lambda: nc.vector.scalar_tensor_tensor(out=av, in0=kk, scalar=-1.0, in1=sq, op0=ALU.mult, op1=ALU.mult), reads=[bkk, bsq_], writes=[bav])
                    K.op(K.dve, lambda: nc.vector.scalar_tensor_tensor(out=bv, in0=av, scalar=-1.0, in1=eta[:, c, :], op0=ALU.mult, op1=ALU.mult), reads=[bav, beta[c]], writes=[bbv])
                    K.op(K.dve, lambda: nc.vector.tensor_scalar(out=km, in0=eta[:, c, :], scalar1=self.col("k_a", c), scalar2=cc[:, 15 + c:16 + c], op0=ALU.mult, op1=ALU.add), reads=[beta[c], self.b_pc, bcc], writes=[bkm])
                    K.op(K.pool, lambda: nc.gpsimd.tensor_tensor(out=km, in0=km, in1=k_(c), op=ALU.mult), reads=[bkm, bsh[4 + c]], writes=[bkm])
                    K.op(K.dve, lambda: nc.vector.scalar_tensor_tensor(out=kk, in0=r_(c), scalar=self.col("r_k", c), in1=km, op0=ALU.mult, op1=ALU.mult), reads=[bsh[c], bkm, self.b_pc], writes=[bkk])
                    p, bp = nps()
                    K.op(K.pe, lambda: nc.tensor.matmul(p[:], lhsT=self.ones2, rhs=kk, start=True, stop=True), reads=[self.b_c2, bkk], writes=[bp])
                    K.op(K.dve, lambda: nc.vector.tensor_tensor(out=bon[:, c, :], in0=p[:], in1=v_(c), op=ALU.mult), reads=[bp, bsh[8 + c]], writes=[bbon[c]])
                for n in range(4):
                    p, bp = nps()
                    for c in range(4):
                        K.op(K.pe, lambda: nc.tensor.transpose(out=p[:, c * 128:(c + 1) * 128], in_=sh[:, 8 + c, n * 128:(n + 1) * 128], identity=self.idf[:]),
                             reads=[bsh[8 + c], self.b_id], writes=[bp], inc=(c == 3), acc=(c > 0))
                    K.op(K.act, lambda: nc.scalar.copy(out=vrt[:, n, :], in_=p[:]), reads=[bp], writes=[bvrt[n]])
                K.dma(K.pool, self.rwv[t0:t0 + TT].rearrange("(n p) c -> p n c", p=128), vrt[:], reads=bvrt, writes=[self.db["rwv"]], acc=True, owner=bvrt[0])
                rw4v = lambda i: self.rw4[i].rearrange("(c p) s -> p c s", p=128)[:, :, t0:t0 + TT]
                K.dma(K.pool, rw4v(0), sh[:, 0:4, :], reads=bsh[0:4], writes=[self.db["rw4"]], acc=True, owner=bsh[0])
                for i in range(3):
                    K.dma(K.pool, rw4v(1 + i), o4[:, i, :, :], reads=bo4[i], writes=[self.db["rw4"]], acc=True, owner=bo4[i][0])
                K.dma(K.pool, self.rwg.rearrange("(c p) s -> p c s", p=128)[:, :, t0:t0 + TT], gT[:], reads=bgT, writes=[self.db["rwg"]], acc=True, owner=bgT[0])
                K.dma(K.pool, self.rwbon.rearrange("(c p) s -> p c s", p=128)[:, :, t0:t0 + TT], bon[:], reads=bbon, writes=[self.db["rwbon"]], acc=True, owner=bbon[0])

    def p3b(self, slot):
        nc, K = self.nc, self.K
        K.begin_scope("p3b")
        S, NT = self.S, self.NT
        with ExitStack() as es:
            sb = lambda n, sh, dt: self.sbt(es, "p3b_%d_" % slot + n, sh, dt)
            R2 = range(2)
            arT = [sb("arT%d" % d, [128, 4, 8, 128], BF16) for d in R2]; barT = [K.bufs(4, "arT%d_" % d) for d in R2]
            bkT = [sb("bkT%d" % d, [128, 4, 8, 128], BF16) for d in R2]; bbkT = [K.bufs(4, "bkT%d_" % d) for d in R2]
            btk = [sb("btk%d" % d, [128, 8, 2, 512], BF16) for d in R2]; bbtk = [K.bufs(16, "btk%d_" % d) for d in R2]
            vt = [sb("vt%d" % d, [128, 8, 512], BF16) for d in R2]; bvt = K.bufs(2, "vt")
            et = [sb("et%d" % d, [128, 4, 8], F32) for d in R2]; bet = K.bufs(2, "et")
            S32 = [sb("S32_%d" % d, [128, 4, 64], F32) for d in R2]; bS32 = K.bufs(2, "S32")
            Sb = [sb("Sb%d" % d, [128, 4, 64], BF16) for d in R2]; bSb = K.bufs(2, "Sb")
            inb = [sb("inb%d" % i, [128, 5, TT], F32) for i in range(2)]; binb = K.bufs(2, "inb")
            gt = sb("gt", [128, 5, TT], F32); bgt = K.bufs(5, "gt")
            Xs = [[sb("Xs%d_%d" % (d, q), [128, 2, 4, 128], BF16) for q in R2] for d in R2]; bXs = [[K.bufs(2, "Xs%d_%d_" % (d, q)) for q in R2] for d in R2]
            As = [sb("As%d" % d, [128, 2, 4, 64], BF16) for d in R2]; bAs = [K.bufs(2, "As%d_" % d) for d in R2]
            Ns = [sb("Ns%d" % d, [128, 2, 4, 64], BF16) for d in R2]; bNs = [K.bufs(2, "Ns%d_" % d) for d in R2]
            Ms = [[sb("Ms%d_%d" % (d, q), [128, 4, 64], BF16) for q in R2] for d in R2]; bMs = [K.bufs(2, "Ms%d_" % d) for d in R2]
            Ws = [sb("Ws%d" % d, [128, 4, 64], BF16) for d in R2]; bWs = K.bufs(2, "Ws")
            Us = [sb("Us%d" % d, [128, 4, 64], BF16) for d in R2]; bUs = K.bufs(2, "Us")
            yt = [sb("yt%d" % d, [128, 2, 256], F32) for d in R2]; byt = [K.bufs(2, "yt%d_" % d) for d in R2]
            stmp = sb("stmp", [128, 2, 256], F32); bstmp = K.bufs(2, "stmp")
            ps = [self.pst(es, "p3b_%d_ps%%d" % slot % i, [128, TT], F32) for i in range(6)]; bps = K.bufs(6, "ps3b")
            pst = self.pst(es, "p3b_%d_pst" % slot, [128, 2 * TT], BF16); bpst = K.buf("ps3bt")
            pi = [0]

            def nps2():
                i = pi[0] % 3
                pi[0] += 1
                return [(ps[2 * i], bps[2 * i]), (ps[2 * i + 1], bps[2 * i + 1])]

            for d in R2:
                K.op(K.pool, lambda: nc.gpsimd.memset(S32[d][:], 0.0), writes=[bS32[d]])
                K.op(K.pool, lambda: nc.gpsimd.memset(Sb[d][:], 0.0), writes=[bSb[d]])
            ii = [0]
            LIM = int(os.environ.get("P3B_LIM", "9"))
            PL = lambda l: slice(64 * l, 64 * l + 64)

            def prep(d, tile):
                t0 = tile * TT
                for l in range(2):
                    K.dma(K.sp, vt[d][PL(l), :, :], self.rwv[t0:t0 + TT].rearrange("(n p) c -> p n c", p=64), reads=[self.db["rwv"]], writes=[bvt[d]], acc=(l > 0))
                for c in range(4):
                    ib, bib = inb[ii[0] % 2], binb[ii[0] % 2]
                    ii[0] += 1
                    K.dma(K.sp, ib[:, 0:4, :], self.rw4[:, c * 128:(c + 1) * 128, t0:t0 + TT].rearrange("i p s -> p i s"), reads=[self.db["rw4"]], writes=[bib])
                    K.dma(K.sp, ib[:, 4, :], self.rwlw[d, c * 128:(c + 1) * 128, t0:t0 + TT], reads=[self.db["rwlw"]], writes=[bib], acc=True)
                    r_, k_, a_, b_, lw_ = [ib[:, i, :] for i in range(5)]
                    L, G, Er, Ei, Ea = [gt[:, i, :] for i in range(5)]
                    v3 = lambda ap: ap.rearrange("p (n t) -> p n t", t=64)
                    K.op(K.dve, lambda: nc.vector.tensor_tensor_scan(out=L, data0=self.cmask, data1=lw_, initial=0.0, op0=ALU.mult, op1=ALU.add),
                         reads=[bib, self.b_c2], writes=[bgt[0]])
                    Ltot = v3(L)[:, :, 63]
                    K.op(K.act, lambda: nc.scalar.activation(out=et[d][:, c, :], in_=Ltot, func=AF.Exp), reads=[bgt[0]], writes=[bet[d]], acc=(c > 0))
                    if d == 0:
                        Gs, bG = L, bgt[0]
                    else:
                        K.op(K.dve, lambda: nc.vector.tensor_tensor(out=G, in0=lw_, in1=L, op=ALU.subtract), reads=[bib, bgt[0]], writes=[bgt[1]])
                        K.op(K.dve, lambda: nc.vector.tensor_tensor(out=v3(G), in0=v3(G), in1=Ltot.unsqueeze(2).broadcast_to([128, 8, 64]), op=ALU.add), reads=[bgt[1], bgt[0]], writes=[bgt[1]])
                        Gs, bG = G, bgt[1]
                    K.op(K.act, lambda: nc.scalar.activation(out=Er, in_=Gs, func=AF.Exp), reads=[bG], writes=[bgt[2]])
                    K.op(K.act, lambda: nc.scalar.activation(out=Ei, in_=Gs, func=AF.Exp, scale=-1.0), reads=[bG], writes=[bgt[3]])
                    K.op(K.dve, lambda: nc.vector.tensor_tensor(out=Ea, in0=Gs, in1=lw_, op=ALU.subtract), reads=[bG, bib], writes=[bgt[4]])
                    K.op(K.act, lambda: nc.scalar.activation(out=Ea, in_=Ea, func=AF.Exp), reads=[bgt[4]], writes=[bgt[4]])
                    K.op(K.dve, lambda: nc.vector.tensor_tensor(out=arT[d][:, c, :, 0:64], in0=v3(a_), in1=v3(Ea), op=ALU.mult), reads=[bib, bgt[4]], writes=[barT[d][c]])
                    K.op(K.pool, lambda: nc.gpsimd.tensor_tensor(out=arT[d][:, c, :, 64:128], in0=v3(r_), in1=v3(Er), op=ALU.mult), reads=[bib, bgt[2]], writes=[barT[d][c]], acc=True)
                    K.op(K.dve, lambda: nc.vector.tensor_tensor(out=bkT[d][:, c, :, 0:64], in0=v3(b_), in1=v3(Ei), op=ALU.mult), reads=[bib, bgt[3]], writes=[bbkT[d][c]])
                    K.op(K.pool, lambda: nc.gpsimd.tensor_tensor(out=bkT[d][:, c, :, 64:128], in0=v3(k_), in1=v3(Ei), op=ALU.mult), reads=[bib, bgt[3]], writes=[bbkT[d][c]], acc=True)
                for n in range(8):
                    for wh in range(2):
                        for hp in range(4):
                            for l in range(2):
                                K.op(K.pe, lambda: nc.tensor.transpose(out=pst[PL(l), hp * 128:(hp + 1) * 128], in_=bkT[d][:, hp, n, wh * 64:(wh + 1) * 64], identity=self.idb[:]),
                                     reads=[bbkT[d][hp], self.b_id], writes=[bpst], inc=(hp == 3 and l == 1), acc=not (hp == 0 and l == 0))
                        if wh == 0:
                            K.op(K.act, lambda: nc.scalar.copy(out=btk[d][:, n, wh, :], in_=pst[:, 0:512]), reads=[bpst], writes=[bbtk[d][n * 2 + wh]])
                        else:
                            K.op(K.dve, lambda: nc.vector.tensor_copy(out=btk[d][:, n, wh, :], in_=pst[:, 0:512]), reads=[bpst], writes=[bbtk[d][n * 2 + wh]])

            def grp_(n, mm, evac, width=64):
                pr = nps2()
                for l in range(2):
                    p, bp = pr[l]
                    for hh in range(4):
                        terms = mm(l, hh)
                        for ti, (l_, r_, lb, rb) in enumerate(terms):
                            K.op(K.pe, lambda: nc.tensor.matmul(p[PL(l), hh * width:(hh + 1) * width], lhsT=l_, rhs=r_, start=(ti == 0), stop=(ti == len(terms) - 1)),
                                 reads=lb + rb, writes=[bp], inc=(hh == 3 and ti == len(terms) - 1), acc=not (hh == 0 and ti == 0), lhs=lb)
                for l in range(2):
                    p, bp = pr[l]
                    evac(l, p[PL(l), 0:4 * width].rearrange("p (h t) -> p h t", t=width), bp)

            v64 = lambda ap: ap.rearrange("p (h t) -> p h t", t=64)

            def pre(d, n, q):
                XS, bX, MS, bM = Xs[d][q], bXs[d][q], Ms[d][q], bMs[d][q]
                mX = self.rmask(d)[:, 0:128].unsqueeze(1).broadcast_to([128, 4, 128])
                mA = self.rmask(d)[:, 128:192].unsqueeze(1).broadcast_to([128, 4, 64])
                idl4 = self.idl.unsqueeze(1).broadcast_to([128, 4, 64])
                AR, BK = arT[d], bkT[d]
                grp = lambda mm, evac, width=64: grp_(n, mm, evac, width)
                ar = lambda l, hh, c0: AR[PL(l), hh, n, c0:c0 + 64]
                bk = lambda l, hh, c0: BK[PL(l), hh, n, c0:c0 + 64]
                for wh in range(2):
                    def ev_X(l, pv, bp, wh=wh):
                        K.op(K.dve, lambda: nc.vector.tensor_tensor(out=XS[PL(l), wh, :, :], in0=pv, in1=mX[PL(l)], op=ALU.mult), reads=[bp, self.b_c2], writes=[bX[wh]], acc=(l > 0))
                    grp(lambda l, hh: [(bk(l, hh, wh * 64), AR[PL(l), hh, n, :], [bbkT[d][hh]], [barT[d][hh]])], ev_X, width=128)
                    yield

                def ev_A0(l, pv, bp):
                    K.op(K.act, lambda: nc.scalar.copy(out=As[d][PL(l), 0, :, :], in_=pv), reads=[bp], writes=[bAs[d][0]], acc=(l > 0))
                grp(lambda l, hh: [(ar(l, hh, 0), bk(l, hh, 0), [barT[d][hh]], [bbkT[d][hh]])], ev_A0)
                K.op(K.pool, lambda: nc.gpsimd.tensor_tensor(out=As[d][:, 0, :, :], in0=As[d][:, 0, :, :], in1=mA, op=ALU.mult), reads=[bAs[d][0], self.b_c2], writes=[bAs[d][0]])
                yield
                Ncur = lambda l, hh: XS[PL(l), 0, hh, 0:64]
                bNcur = [bX[0]]
                Acur = lambda l, hh: As[d][PL(l), 0, hh, :]
                bAcur = [bAs[d][0]]
                K.op(K.dve, lambda: nc.vector.tensor_tensor(out=MS[:], in0=XS[:, 0, :, 0:64], in1=idl4, op=ALU.add), reads=bNcur + [self.b_c2], writes=[bM])
                for lv in range(5):
                    o_ = (lv + 1) % 2
                    Nc, Ac, bNc, bAc = Ncur, Acur, bNcur, bAcur

                    def ev_A(l, pv, bp, o_=o_):
                        K.op(K.act, lambda: nc.scalar.copy(out=As[d][PL(l), o_, :, :], in_=pv), reads=[bp], writes=[bAs[d][o_]], acc=(l > 0))

                    def ev_N(l, pv, bp, o_=o_):
                        K.op(K.act, lambda: nc.scalar.copy(out=Ns[d][PL(l), o_, :, :], in_=pv), reads=[bp], writes=[bNs[d][o_]], acc=(l > 0))
                    grp(lambda l, hh: [(Nc(l, hh), Ac(l, hh), bNc, bAc)], ev_A)
                    yield
                    if lv < 4:
                        grp(lambda l, hh: [(Ac(l, hh), Nc(l, hh), bAc, bNc)], ev_N)
                        yield
                    Acur = (lambda oo: (lambda l, hh: As[d][PL(l), oo, hh, :]))(o_)
                    bAcur = [bAs[d][o_]]
                    Ncur = (lambda oo: (lambda l, hh: Ns[d][PL(l), oo, hh, :]))(o_)
                    bNcur = [bNs[d][o_]]
                    An, bAn = Acur, bAcur

                    def ev_M(l, pv, bp):
                        K.op(K.dve, lambda: nc.vector.tensor_tensor(out=MS[PL(l)], in0=MS[PL(l)], in1=pv, op=ALU.add), reads=[bM, bp], writes=[bM], acc=True)
                    grp(lambda l, hh: [(An(l, hh), MS[PL(l), hh, :], bAn, [bM])], ev_M)
                    yield

            def seq(d, tile, n, q, yi):
                XS, bX, MS, bM = Xs[d][q], bXs[d][q], Ms[d][q], bMs[d][q]
                AR = arT[d]
                grp = lambda mm, evac, width=64: grp_(n, mm, evac, width)
                ar = lambda l, hh, c0: AR[PL(l), hh, n, c0:c0 + 64]
                V = lambda l, hh: vt[d][PL(l), n, (2 * hh + l) * 64:(2 * hh + l + 1) * 64]
                St = lambda l, hh: Sb[d][PL(l), hh, :]

                def ev_W(l, pv, bp):
                    K.op(K.act, lambda: nc.scalar.copy(out=Ws[d][PL(l)], in_=pv), reads=[bp], writes=[bWs[d]], acc=(l > 0))
                grp(lambda l, hh: [(XS[PL(l), 1, hh, 0:64], V(l, hh), [bX[1]], [bvt[d]]),
                                   (ar(l, hh, 0), St(l, hh), [barT[d][hh]], [bSb[d]])], ev_W)
                yield

                def ev_U(l, pv, bp):
                    K.op(K.act, lambda: nc.scalar.copy(out=Us[d][PL(l)], in_=pv), reads=[bp], writes=[bUs[d]], acc=(l > 0))
                grp(lambda l, hh: [(MS[PL(l), hh, :], Ws[d][PL(l), hh, :], [bM], [bWs[d]])], ev_U)
                yield
                r0 = tile * TT + n * 64

                def ev_Y(l, pv, bp):
                    K.op(K.act, lambda: nc.scalar.copy(out=v64(yt[d][PL(l), yi % 2, :]), in_=pv), reads=[bp], writes=[byt[d][yi % 2]], acc=(l > 0))
                grp(lambda l, hh: [(ar(l, hh, 64), St(l, hh), [barT[d][hh]], [bSb[d]]),
                                   (XS[PL(l), 0, hh, 64:128], Us[d][PL(l), hh, :], [bX[0]], [bUs[d]]),
                                   (XS[PL(l), 1, hh, 64:128], V(l, hh), [bX[1]], [bvt[d]])], ev_Y)
                for l in range(2):
                    K.dma(K.pool, self.rwy[d, r0:r0 + 64, :].rearrange("t (hh two i) -> t hh two i", two=2, i=64)[:, :, l, :], v64(yt[d][PL(l), yi % 2, :]),
                          reads=[byt[d][yi % 2]], writes=[self.db["rwy%d" % d]], acc=True, owner=byt[d][yi % 2])
                yield

                def ev_S(l, pv, bp):
                    K.op(K.dve, lambda: nc.vector.tensor_tensor(out=v64(stmp[PL(l), d, :]), in0=pv, in1=S32[d][PL(l)], op=ALU.add), reads=[bp, bS32[d]], writes=[bstmp[d]], acc=(l > 0))
                grp(lambda l, hh: [(btk[d][PL(l), n, 0, (2 * hh + l) * 64:(2 * hh + l + 1) * 64], Us[d][PL(l), hh, :], [bbtk[d][n * 2]], [bUs[d]]),
                                   (btk[d][PL(l), n, 1, (2 * hh + l) * 64:(2 * hh + l + 1) * 64], V(l, hh), [bbtk[d][n * 2 + 1]], [bvt[d]])], ev_S)
                K.op(K.dve, lambda: nc.vector.tensor_tensor(out=S32[d][:], in0=v64(stmp[:, d, :]), in1=et[d][:, :, n:n + 1].broadcast_to([128, 4, 64]), op=ALU.mult),
                     reads=[bstmp[d], bet[d]], writes=[bS32[d]])
                K.op(K.act, lambda: nc.scalar.copy(out=Sb[d][:], in_=S32[d][:]), reads=[bS32[d]], writes=[bSb[d]])
                yield

            def run_rr(gens):
                gens = list(gens)
                while gens:
                    for g in list(gens):
                        try:
                            next(g)
                        except StopIteration:
                            gens.remove(g)

            yi = [0, 0]
            cn = lambda d, i: i if d == 0 else 7 - i
            for step in range(NT):
                tiles = [step, NT - 1 - step]
                for d in R2:
                    prep(d, tiles[d])
                run_rr([pre(d, cn(d, 0), 0) for d in R2])
                for i in range(8):
                    gens = [seq(d, tiles[d], cn(d, i), i % 2, yi[d]) for d in R2]
                    if i + 1 < 8:
                        gens = [g for pair in zip(gens, [pre(d, cn(d, i + 1), (i + 1) % 2) for d in R2]) for g in pair]
                    run_rr(gens)
                    for d in R2:
                        yi[d] += 1

    def p3c(self, slot):
        nc, K = self.nc, self.K
        K.begin_scope("p3c")
        with ExitStack() as es:
            sb = lambda n, sh, dt: self.sbt(es, "p3c_%d_" % slot + n, sh, dt)
            yf = sb("yf", [128, 4, 512], F32); byf = K.buf("yf")
            yb = sb("yb", [128, 4, 512], F32); byb = K.buf("yb")
            sq = sb("sq", [128, 4, 512], F32); bsq = K.buf("sq")
            stt = sb("stt", [128, 2, 32], F32); bstt = K.buf("stt")
            bon = sb("bon", [128, 4, TT], F32); bbon = K.buf("bon")
            gg = sb("gg", [128, 4, TT], F32); bgg = K.buf("gg")
            t1 = sb("t1", [128, 2, TT], F32); bt1 = K.bufs(2, "t1")
            ro = sb("ro", [128, 4, TT], BF16); bro = K.bufs(4, "ro")
            ps = [self.pst(es, "p3c_%d_ps%%d" % slot % i, [128, TT], F32) for i in range(2)]; bps = K.bufs(2, "ps3c")
            g3 = lambda ap: ap.rearrange("p n (h i) -> p (n h) i", i=64)
            for t in range(self.NT):
                t0 = t * TT
                K.dma(K.sp, yf[:], self.rwy[0, t0:t0 + TT].rearrange("(n p) c -> p n c", p=128), reads=[self.db["rwy0"]], writes=[byf])
                K.dma(K.sp, yb[:], self.rwy[1, t0:t0 + TT].rearrange("(n p) c -> p n c", p=128), reads=[self.db["rwy1"]], writes=[byb])
                K.dma(K.sp, bon[:], self.rwbon.rearrange("(c p) s -> p c s", p=128)[:, :, t0:t0 + TT], reads=[self.db["rwbon"]], writes=[bbon])
                K.dma(K.sp, gg[:], self.rwg.rearrange("(c p) s -> p c s", p=128)[:, :, t0:t0 + TT], reads=[self.db["rwg"]], writes=[bgg])
                K.op(K.dve, lambda: nc.vector.tensor_tensor(out=yf[:], in0=yf[:], in1=yb[:], op=ALU.add), reads=[byf, byb], writes=[byf])
                mean, var = stt[:, 0, :], stt[:, 1, :]
                K.op(K.dve, lambda: nc.vector.tensor_reduce(out=mean, in_=g3(yf[:]), axis=AX.X, op=ALU.add), reads=[byf], writes=[bstt])
                K.op(K.dve, lambda: nc.vector.tensor_scalar(out=mean, in0=mean, scalar1=1.0 / 64, scalar2=None, op0=ALU.mult), reads=[bstt], writes=[bstt])
                K.op(K.dve, lambda: nc.vector.tensor_tensor(out=g3(yf[:]), in0=g3(yf[:]), in1=mean.unsqueeze(2).broadcast_to([128, 32, 64]), op=ALU.subtract), reads=[byf, bstt], writes=[byf])
                K.op(K.act, lambda: nc.scalar.activation(out=sq[:], in_=yf[:], func=AF.Square), reads=[byf], writes=[bsq])
                K.op(K.dve, lambda: nc.vector.tensor_reduce(out=var, in_=g3(sq[:]), axis=AX.X, op=ALU.add), reads=[bsq], writes=[bstt], acc=True)
                K.op(K.act, lambda: nc.scalar.activation(out=var, in_=var, func=AF.Ln, scale=1.0 / 64, bias=self.epsc[:, 2:3]), reads=[bstt, self.b_ones], writes=[bstt], acc=True)
                K.op(K.act, lambda: nc.scalar.activation(out=var, in_=var, func=AF.Exp, scale=-0.5), reads=[bstt], writes=[bstt], acc=True)
                K.op(K.dve, lambda: nc.vector.tensor_tensor(out=g3(yf[:]), in0=g3(yf[:]), in1=var.unsqueeze(2).broadcast_to([128, 32, 64]), op=ALU.mult), reads=[byf, bstt], writes=[byf])
                for c in range(4):
                    p, bp = ps[c % 2], bps[c % 2]
                    for n in range(4):
                        K.op(K.pe, lambda: nc.tensor.transpose(out=p[:, n * 128:(n + 1) * 128], in_=yf[:, n, c * 128:(c + 1) * 128], identity=self.idf[:]),
                             reads=[byf, self.b_id], writes=[bp], inc=(n == 3), acc=(n > 0))
                    tt_, btt = t1[:, c % 2, :], bt1[c % 2]
                    K.op(K.act, lambda: nc.scalar.activation(out=tt_, in_=p[:], func=AF.Identity, scale=self.col("lnx_g", c), bias=self.col("lnx_b", c)), reads=[bp, self.b_pc], writes=[btt])
                    K.op(K.dve, lambda: nc.vector.tensor_tensor(out=tt_, in0=tt_, in1=bon[:, c, :], op=ALU.add), reads=[btt, bbon], writes=[btt])
                    K.op(K.pool, lambda: nc.gpsimd.tensor_tensor(out=ro[:, c, :], in0=tt_, in1=gg[:, c, :], op=ALU.mult), reads=[btt, bgg], writes=[bro[c]])
                K.dma(K.pool, self.rwo.rearrange("(c p) s -> p c s", p=128)[:, :, t0:t0 + TT], ro[:], reads=bro, writes=[self.db["rwo"]], acc=True, owner=bro[0])

    def p4m(self, slot):
        nc, K = self.nc, self.K
        K.begin_scope("p4m")
        with ExitStack() as es:
            sb = lambda n, sh, dt: self.sbt(es, "p4m_%d_" % slot + n, sh, dt)
            m = sb("m", [128, 2, D], F32); bm = K.buf("m")
            sq = sb("sq", [128, 2, D], F32); bsq = K.buf("sq")
            gb = sb("gb", [128, 2, D], F32); bgb = K.buf("gb")
            stt = sb("stt", [128, 2, 2], F32); bstt = K.buf("stt")
            mT = sb("mT", [128, 8, 256], BF16); bmT = K.bufs(8, "mT")
            wck = sb("wck", [128, 8, 1024], BF16); bwck = K.buf("wck")
            km = sb("km", [128, 4, 256], BF16); bkm = K.bufs(4, "km")
            vm = sb("vm", [128, 2, 516], BF16); bvm = K.buf("vm")
            ps = [self.pst(es, "p4m_%d_ps%%d" % slot % i, [128, TT], F32) for i in range(2)]; bps = K.bufs(2, "ps4m")
            K.dma(K.sp, m[:], self.mem[slot].rearrange("(n p) d -> p n d", p=128), writes=[bm])
            K.dma(K.sp, gb[:, 0, :], self.memgb[0:1, :].partition_broadcast(128), writes=[bgb])
            K.dma(K.sp, gb[:, 1, :], self.memgb[1:2, :].partition_broadcast(128), writes=[bgb], acc=True)
            K.dma(K.sp, wck[:], self.ws["wckv"], reads=[self.db["wckv"]], writes=[bwck])
            K.op(K.pool, lambda: nc.gpsimd.memset(vm[:], 1.0), writes=[bvm])
            mean, var = stt[:, 0, :], stt[:, 1, :]
            K.op(K.dve, lambda: nc.vector.tensor_reduce(out=mean, in_=m[:], axis=AX.X, op=ALU.add), reads=[bm], writes=[bstt])
            K.op(K.dve, lambda: nc.vector.tensor_scalar(out=mean, in0=mean, scalar1=1.0 / D, scalar2=None, op0=ALU.mult), reads=[bstt], writes=[bstt])
            K.op(K.dve, lambda: nc.vector.tensor_tensor(out=m[:], in0=m[:], in1=mean.unsqueeze(2).broadcast_to([128, 2, D]), op=ALU.subtract), reads=[bm, bstt], writes=[bm])
            K.op(K.act, lambda: nc.scalar.activation(out=sq[:], in_=m[:], func=AF.Square), reads=[bm], writes=[bsq])
            K.op(K.dve, lambda: nc.vector.tensor_reduce(out=var, in_=sq[:], axis=AX.X, op=ALU.add), reads=[bsq], writes=[bstt], acc=True)
            K.op(K.act, lambda: nc.scalar.activation(out=var, in_=var, func=AF.Ln, scale=1.0 / D, bias=self.epsc[:, 0:1]), reads=[bstt, self.b_ones], writes=[bstt], acc=True)
            K.op(K.act, lambda: nc.scalar.activation(out=var, in_=var, func=AF.Exp, scale=-0.5), reads=[bstt], writes=[bstt], acc=True)
            K.op(K.dve, lambda: nc.vector.tensor_tensor(out=m[:], in0=m[:], in1=var.unsqueeze(2).broadcast_to([128, 2, D]), op=ALU.mult), reads=[bm, bstt], writes=[bm])
            K.op(K.dve, lambda: nc.vector.tensor_tensor(out=m[:], in0=m[:], in1=gb[:, 0:1, :].broadcast_to([128, 2, D]), op=ALU.mult), reads=[bm, bgb], writes=[bm])
            K.op(K.dve, lambda: nc.vector.tensor_tensor(out=m[:], in0=m[:], in1=gb[:, 1:2, :].broadcast_to([128, 2, D]), op=ALU.add), reads=[bm, bgb], writes=[bm])
            for kc in range(8):
                p, bp = ps[kc % 2], bps[kc % 2]
                for n in range(2):
                    K.op(K.pe, lambda: nc.tensor.transpose(out=p[:, n * 128:(n + 1) * 128], in_=m[:, n, kc * 128:(kc + 1) * 128], identity=self.idf[:]),
                         reads=[bm, self.b_id], writes=[bp], inc=(n == 1), acc=(n > 0))
                K.op(K.act, lambda: nc.scalar.copy(out=mT[:, kc, :], in_=p[:, 0:256]), reads=[bp], writes=[bmT[kc]])
            for hc in range(4):
                p, bp = ps[hc % 2], bps[hc % 2]
                for kc in range(8):
                    K.op(K.pe, lambda: nc.tensor.matmul(p[:, 0:256], lhsT=wck[:, kc, hc * 128:(hc + 1) * 128], rhs=mT[:, kc, :], start=(kc == 0), stop=(kc == 7)),
                         reads=[bwck, bmT[kc]], writes=[bp], inc=(kc == 7), acc=(kc > 0))
                K.op(K.act, lambda: nc.scalar.copy(out=km[:, hc, :], in_=p[:, 0:256]), reads=[bp], writes=[bkm[hc]])
            for mt in range(2):
                p, bp = ps[mt % 2], bps[mt % 2]
                for kc in range(8):
                    K.op(K.pe, lambda: nc.tensor.matmul(p[:], lhsT=mT[:, kc, mt * 128:(mt + 1) * 128], rhs=wck[:, kc, 512:1024], start=(kc == 0), stop=(kc == 7)),
                         reads=[bwck, bmT[kc]], writes=[bp], inc=(kc == 7), acc=(kc > 0))
                K.op(K.dve, lambda: nc.vector.tensor_copy(out=vm[:, mt, :].rearrange("p (h d) -> p h d", d=129)[:, :, 0:128], in_=p[:].rearrange("p (h d) -> p h d", d=128)),
                     reads=[bp], writes=[bvm], acc=True)
            K.dma(K.pool, self.kmT, km[:], reads=bkm, writes=[self.db["kmT"]], owner=bkm[0])
            K.dma(K.pool, self.vms, vm[:], reads=[bvm], writes=[self.db["vms"]], owner=bvm)

    def p4(self, slot):
        nc, K = self.nc, self.K
        K.begin_scope("p4")
        scale_c = float(128 ** -0.5)
        with ExitStack() as es:
            sb = lambda n, sh, dt: self.sbt(es, "p4_%d_" % slot + n, sh, dt)
            xr = sb("xr", [128, 8, TT], F32); bxr = K.bufs(8, "xr")
            z = sb("z", [128, 8, TT], F32); bz = K.bufs(8, "z")
            xb = sb("xb", [128, 8, TT], BF16); bxb = K.bufs(8, "xb")
            st = sb("st", [128, TT], F32); bst = K.buf("st")
            st2 = sb("st2", [128, TT], F32); bst2 = K.buf("st2")
            ytok = sb("ytok", [128, 4, D], F32); bsq = K.bufs(8, "ytok_sq")
            sqf = lambda c: ytok[:, c // 2, (c % 2) * TT:(c % 2 + 1) * TT]
            hT = sb("hT", [128, NFC, TT], BF16); bhT = K.bufs(NFC, "hT")
            wgu = [sb("wgu%d" % i, [128, 8, 512], BF16) for i in range(2)]; bwgu = K.bufs(2, "wgu")
            wdn = [sb("wdn%d" % i, [128, NFC, 128], BF16) for i in range(2)]; bwdn = K.bufs(2, "wdn")
            OTt = sb("OTt", [128, 4, TT], BF16); bOT = K.buf("OTt")
            RWt = sb("RWt", [128, 4, TT], BF16); bRW = K.buf("RWt")
            gts = sb("gts", [128, 2, 2, TT], F32); bgts = K.bufs(2, "gts")
            tmpf = sb("tmpf", [128, 2, TT], F32); btmp = K.bufs(2, "tmpf")
            qc = sb("qc", [128, 4, TT], BF16); bqc = K.bufs(4, "qc")
            km = sb("km", [128, 4, 256], BF16); bkm = K.buf("km")
            vm = sb("vm", [128, 2, 516], BF16); bvm = K.buf("vm")
            PTc = [sb("PTc%d" % i, [128, TT], BF16) for i in range(2)]; bPTc = K.bufs(2, "PTc")
            oct_ = sb("oct", [128, 4, 512], BF16); boct = K.bufs(4, "oct")
            ocT = sb("ocT", [128, 4, TT], BF16); bocT = K.bufs(4, "ocT")
            rs = sb("rs", [128, 4], F32); brs = K.buf("rs")
            psA = [self.pst(es, "p4_%d_ps%%d" % slot % i, [128, TT], F32) for i in range(4)]; bpsA = K.bufs(4, "psA4")
            psS = self.pst(es, "p4_%d_pss" % slot, [128, TT], F32); bpsS = K.buf("psS4")
            psO = [self.pst(es, "p4_%d_pso%%d" % slot % i, [128, TT], F32) for i in range(2)]; bpsO = K.bufs(2, "psO4")
            psT = self.pst(es, "p4_%d_pst" % slot, [128, 2 * TT], BF16); bpsT = K.buf("psT4")
            K.dma(K.sp, km[:], self.kmT, reads=[self.db["kmT"]], writes=[bkm])
            K.dma(K.sp, vm[:], self.vms, reads=[self.db["vms"]], writes=[bvm])
            w4 = lambda wt: wt[:].rearrange("p a b -> p (a b)").rearrange("p (k c) -> p k c", k=4)
            pi = [0]

            def nps():
                i = pi[0] % 4
                pi[0] += 1
                return psA[i], bpsA[i]

            for t in range(self.NT):
                t0 = t * TT
                K.dma(K.sp, OTt[:], self.OT.rearrange("(c p) s -> p c s", p=128)[:, :, t0:t0 + TT], reads=[self.db["OT"]], writes=[bOT])
                K.dma(K.sp, RWt[:], self.rwo.rearrange("(c p) s -> p c s", p=128)[:, :, t0:t0 + TT], reads=[self.db["rwo"]], writes=[bRW])
                K.dma(K.sp, xr[:], self.x1T.rearrange("(c p) s -> p c s", p=128)[:, :, t0:t0 + TT], reads=[self.db["x1T"]], writes=bxr)
                K.dma(K.sp, wgu[0][:], self.ws["pmla"].rearrange("p k (a b) -> p (k a) b", a=2), reads=[self.db["pmla"]], writes=[bwgu[0]])
                K.dma(K.sp, wgu[1][:], self.ws["prwkv"].rearrange("p k (a b) -> p (k a) b", a=2), reads=[self.db["prwkv"]], writes=[bwgu[1]])
                WA, WB = w4(wgu[0]), w4(wgu[1])
                for c in range(8):
                    g_, bg = gts[:, c % 2], bgts[c % 2]
                    K.dma(K.sp, g_[:, 0, :], self.gates[c * 128:(c + 1) * 128, t0:t0 + TT], reads=[self.db["gates"]], writes=[bg])
                    K.dma(K.sp, g_[:, 1, :], self.gates[1024 + c * 128:1024 + (c + 1) * 128, t0:t0 + TT], reads=[self.db["gates"]], writes=[bg], acc=True)
                    pa, bpa = nps()
                    pb, bpb = nps()
                    for k in range(4):
                        K.op(K.pe, lambda: nc.tensor.matmul(pa[:], lhsT=WA[:, k, c * 128:(c + 1) * 128], rhs=OTt[:, k, :], start=(k == 0), stop=(k == 3)),
                             reads=[bwgu[0], bOT], writes=[bpa], inc=(k == 3), acc=(k > 0))
                    for k in range(4):
                        K.op(K.pe, lambda: nc.tensor.matmul(pb[:], lhsT=WB[:, k, c * 128:(c + 1) * 128], rhs=RWt[:, k, :], start=(k == 0), stop=(k == 3)),
                             reads=[bwgu[1], bRW], writes=[bpb], inc=(k == 3), acc=(k > 0))
                    K.op(K.dve, lambda: nc.vector.tensor_tensor(out=tmpf[:, 0, :], in0=pa[:], in1=g_[:, 0, :], op=ALU.mult), reads=[bpa, bg], writes=[btmp[0]])
                    K.op(K.dve, lambda: nc.vector.tensor_tensor(out=tmpf[:, 1, :], in0=pb[:], in1=g_[:, 1, :], op=ALU.mult), reads=[bpb, bg], writes=[btmp[1]])
                    K.op(K.pool, lambda: nc.gpsimd.tensor_tensor(out=xb[:, c, :], in0=tmpf[:, 0, :], in1=tmpf[:, 1, :], op=ALU.add), reads=btmp, writes=[bxb[c]])
                for half in range(2):
                    K.dma(K.sp, wgu[half][:], self.ws["wo"][:, :, half * 512:(half + 1) * 512], reads=[self.db["wo"]], writes=[bwgu[half]])
                for c in range(8):
                    wt, bwt = wgu[c // 4], bwgu[c // 4]
                    pz, bpz = nps()
                    for k in range(8):
                        K.op(K.pe, lambda: nc.tensor.matmul(pz[:], lhsT=wt[:, k, (c % 4) * 128:(c % 4 + 1) * 128], rhs=xb[:, k, :], start=(k == 0), stop=(k == 7)),
                             reads=[bwt, bxb[k]], writes=[bpz], inc=(k == 7), acc=(k > 0))
                    K.op(K.dve, lambda: nc.vector.scalar_tensor_tensor(out=z[:, c, :], in0=xr[:, c, :], scalar=ALPHA, in1=pz[:], op0=ALU.mult, op1=ALU.add),
                         reads=[bpz, bxr[c]], writes=[bz[c]])
                self.ln_fm(z, bz, "ln2_g", "ln2_b", xr, bxr, xb, bxb, psS, bpsS, sqf, bsq, st, bst)
                K.dma(K.sp, wgu[0][:], self.ws["wcq"], reads=[self.db["wcq"]], writes=[bwgu[0]])
                K.dma(K.sp, wgu[1][:], self.ws["wco"].rearrange("p k (a b) -> p (k a) b", a=2), reads=[self.db["wco"]], writes=[bwgu[1]])
                for hc in range(4):
                    p, bp = nps()
                    for k in range(8):
                        K.op(K.pe, lambda: nc.tensor.matmul(p[:], lhsT=wgu[0][:, k, hc * 128:(hc + 1) * 128], rhs=xb[:, k, :], start=(k == 0), stop=(k == 7)),
                             reads=[bwgu[0], bxb[k]], writes=[bp], inc=(k == 7), acc=(k > 0))
                    K.op(K.act, lambda: nc.scalar.copy(out=qc[:, hc, :], in_=p[:]), reads=[bp], writes=[bqc[hc]])
                si = 0
                for hc in range(4):
                    for mt in range(2):
                        pS_, bpS_ = nps()
                        P_, bP = PTc[si % 2], bPTc[si % 2]
                        si += 1
                        K.op(K.pe, lambda: nc.tensor.matmul(pS_[:], lhsT=km[:, hc, mt * 128:(mt + 1) * 128], rhs=qc[:, hc, :], start=True, stop=True), reads=[bkm, bqc[hc]], writes=[bpS_])
                        K.op(K.act, lambda: nc.scalar.activation(out=P_[:], in_=pS_[:], func=AF.Exp, scale=scale_c), reads=[bpS_], writes=[bP])
                        for qs in range(4):
                            po, bpo = psO[qs // 2], bpsO[qs // 2]
                            K.op(K.pe, lambda: nc.tensor.matmul(po[:, (qs % 2) * 129:(qs % 2 + 1) * 129], lhsT=P_[:, qs * 128:(qs + 1) * 128], rhs=vm[:, mt, hc * 129:(hc + 1) * 129],
                                                                 start=(mt == 0 and qs % 2 == 0), stop=(mt == 1 and qs % 2 == 1)),
                                 reads=[bP, bvm], writes=[bpo], inc=(qs % 2 == 1), acc=not (mt == 0 and qs % 2 == 0))
                    for qs in range(4):
                        po, bpo = psO[qs // 2], bpsO[qs // 2]
                        o0 = (qs % 2) * 129
                        K.op(K.dve, lambda: nc.vector.reciprocal(out=rs[:, qs:qs + 1], in_=po[:, o0 + 128:o0 + 129]), reads=[bpo], writes=[brs], acc=(qs > 0))
                        K.op(K.dve, lambda: nc.vector.tensor_scalar(out=oct_[:, qs, hc * 128:(hc + 1) * 128], in0=po[:, o0:o0 + 128], scalar1=rs[:, qs:qs + 1], scalar2=None, op0=ALU.mult),
                             reads=[bpo, brs], writes=[boct[hc]], acc=(qs > 0))
                for hc in range(4):
                    for qs in range(4):
                        K.op(K.pe, lambda: nc.tensor.transpose(out=psT[:, qs * 128:(qs + 1) * 128], in_=oct_[:, qs, hc * 128:(hc + 1) * 128], identity=self.idb[:]),
                             reads=[boct[hc], self.b_id], writes=[bpsT], inc=(qs == 3), acc=(qs > 0))
                    K.op(K.dve, lambda: nc.vector.tensor_copy(out=ocT[:, hc, :], in_=psT[:, 0:TT]), reads=[bpsT], writes=[bocT[hc]])
                WC = w4(wgu[1])
                for c in range(8):
                    pz, bpz = nps()
                    for k in range(4):
                        K.op(K.pe, lambda: nc.tensor.matmul(pz[:], lhsT=WC[:, k, c * 128:(c + 1) * 128], rhs=ocT[:, k, :], start=(k == 0), stop=(k == 3)),
                             reads=[bwgu[1], bocT[k]], writes=[bpz], inc=(k == 3), acc=(k > 0))
                    K.op(K.dve, lambda: nc.vector.scalar_tensor_tensor(out=z[:, c, :], in0=xr[:, c, :], scalar=ALPHA, in1=pz[:], op0=ALU.mult, op1=ALU.add),
                         reads=[bpz, bxr[c]], writes=[bz[c]])
                self.ln_fm(z, bz, "ln3_g", "ln3_b", xr, bxr, xb, bxb, psS, bpsS, sqf, bsq, st, bst)
                for c in range(8):
                    K.op(K.pool, lambda: nc.gpsimd.tensor_scalar(out=xr[:, c, :], in0=xr[:, c, :], scalar1=ALPHA, scalar2=None, op0=ALU.mult), reads=[bxr[c]], writes=[bxr[c]])
                self.ffn(1, xb, bxb, xr, bxr, hT, bhT, wgu, bwgu, wdn, bwdn, psA, bpsA, z, bz)
                self.ln_fm(z, bz, "ln4_g", "ln4_b", xr, bxr, None, None, psS, bpsS, sqf, bsq, st, bst)
                for n in range(4):
                    for hf in range(2):
                        p, bp = nps()
                        for cc in range(4):
                            c = hf * 4 + cc
                            K.op(K.pe, lambda: nc.tensor.transpose(out=p[:, cc * 128:(cc + 1) * 128], in_=xr[:, c, n * 128:(n + 1) * 128], identity=self.idf[:]),
                                 reads=[bxr[c], self.b_id], writes=[bp], inc=(cc == 3), acc=(cc > 0))
                        if hf == 0:
                            K.op(K.act, lambda: nc.scalar.copy(out=ytok[:, n, 0:512], in_=p[:]), reads=[bp], writes=[bsq[2 * n]])
                        else:
                            K.op(K.dve, lambda: nc.vector.tensor_copy(out=ytok[:, n, 512:1024], in_=p[:]), reads=[bp], writes=[bsq[2 * n + 1]])
                K.dma(K.pool, self.y[slot, t0:t0 + TT, :].rearrange("(n p) d -> p n d", p=128), ytok[:], reads=bsq, writes=[self.db["y"]], acc=True, owner=bsq[0])

def _prep_consts(S):
    inv = 1.0 / (10000.0 ** (np.arange(0, 32, 2, dtype=np.float32) / 32.0))
    ang = np.arange(S, dtype=np.float32)[:, None] * inv[None, :].astype(np.float32)
    c, s = np.cos(ang).astype(np.float32), np.sin(ang).astype(np.float32)
    ropec = np.concatenate([c, c], 1).T.copy()
    ropes = np.concatenate([-s, s], 1).T.copy()
    return ropec, ropes


def _cst2():
    c = np.zeros((128, 128 + 512 + 384 + 64), np.float32)
    c[0:64, 0:64] = 1.0
    c[64:128, 64:128] = 1.0
    cm = np.ones(512, np.float32)
    cm[::64] = 0.0
    c[:, 128:640] = cm[None, :]
    j = np.arange(64)[:, None]
    t = np.arange(64)[None, :]
    for d in range(2):
        strict = (j < t) if d == 0 else (j > t)
        incl = (j <= t) if d == 0 else (j >= t)
        base = 640 + d * 192
        for l0 in (0, 64):
            c[l0:l0 + 64, base:base + 64] = strict
            c[l0:l0 + 64, base + 64:base + 128] = incl
            c[l0:l0 + 64, base + 128:base + 192] = strict.T
    c[0:64, 1024:1088] = np.eye(64)
    c[64:128, 1024:1088] = np.eye(64)
    return c


def make_in_map(p, x, mem, S):
    ropec, ropes = _prep_consts(S)
    im = {"x": np.ascontiguousarray(x, np.float32), "mem": np.ascontiguousarray(mem, np.float32),
          "pcols": _pack_cols(p), "ident": np.eye(128, dtype=np.float32), "cst2": _cst2(), "ropec": ropec, "ropes": ropes,
          "memgb": np.stack([p["mem_g"][0], p["mem_b"][0]]).astype(np.float32)}
    im.update(_layout_weights(p))
    return im


_SEQ_MAP = None


def _slot_map():
    m = []
    seqs = [("p", i) for i in range(16)] + [("s", i) for i in range(4)]
    k = 0
    for c in range(NCORE):
        n = 3 if c < 4 else 2
        sl = seqs[k:k + n]
        k += n
        while len(sl) < NSLOT:
            sl = sl + [sl[-1]]
        m.append(sl)
    return m


def kernel(**inputs):
    S = 4096
    p = {k: np.asarray(v) for k, v in inputs.items() if k not in ("x_prompt", "x_sample", "mem_prompt", "mem_sample")}
    xs = {"p": np.asarray(inputs["x_prompt"], np.float32), "s": np.asarray(inputs["x_sample"], np.float32)}
    ms = {"p": np.asarray(inputs["mem_prompt"], np.float32), "s": np.asarray(inputs["mem_sample"], np.float32)}
    smap = _slot_map()
    B = Builder(S, NSLOT, debug=False)
    nc = B.build()
    shared = make_in_map(p, np.zeros((0,), np.float32), np.zeros((0,), np.float32), S)
    in_maps = []
    for c in range(NCORE):
        im = dict(shared)
        im["x"] = np.ascontiguousarray(np.stack([xs[g][i] for g, i in smap[c]]))
        im["mem"] = np.ascontiguousarray(np.stack([ms[g][i] for g, i in smap[c]]))
        in_maps.append(im)
    res = run_bass_kernel_spmd(nc, in_maps, core_ids=list(range(NCORE)))
    yp = np.zeros_like(xs["p"])
    ysm = np.zeros_like(xs["s"])
    done = set()
    for c in range(NCORE):
        y = res.results[c]["y"]
        for sl, (g, i) in enumerate(smap[c]):
            if (g, i) in done:
                continue
            done.add((g, i))
            (yp if g == "p" else ysm)[i] = y[sl]
    return (yp, ysm)
```
